# Optimizing a Trainium2 kernel written in Bass

```python
import math
import jax, jax.numpy as jnp
from jax import lax
import numpy as np

D_MODEL = 1024
BATCH = 8
SEQ = 2048
DEPTH = 4

D_BRANCH = D_MODEL
S5_GROUP = 16
S5_GROUPS = D_BRANCH // S5_GROUP
S5_STATE = 64
S5_DT_MIN = 1e-3
S5_DT_MAX = 1e-1
POOL_WINDOWS = (2, 4, 8, 16)
POOL_GROUP = D_BRANCH // len(POOL_WINDOWS)
CONV_EXPAND = 2
D_CONV = CONV_EXPAND * D_MODEL
CONV_WIDTH = 3
N_EVEN = (DEPTH + 1) // 2
N_ODD = DEPTH // 2
RMS_EPS = 1e-6

kernel_name = "hybrid_s5_pool_shortconv_trunk"


def rmsnorm(x, g):
    xf = x.astype(jnp.float32)
    return xf * lax.rsqrt(jnp.mean(xf * xf, axis=-1, keepdims=True) + RMS_EPS) * g


def _cmul(ar, ai, br, bi):
    return ar * br - ai * bi, ar * bi + ai * br


def _scan_combine(e1, e2):
    a1r, a1i, b1r, b1i = e1
    a2r, a2i, b2r, b2i = e2
    ar, ai = _cmul(a2r, a2i, a1r, a1i)
    br, bi = _cmul(a2r, a2i, b1r, b1i)
    return ar, ai, br + b2r, bi + b2i


def s5_mixer(u, a_re, a_im, log_dt, b_re, b_im, c_re, c_im, d_skip, w_glu, b_glu):
    f32 = jnp.float32
    bsz, seq, _ = u.shape
    u = u.astype(f32)
    ug = u.reshape(bsz, seq, S5_GROUPS, S5_GROUP)
    dt = jnp.exp(log_dt.astype(f32))[:, None]
    lam_re = a_re.astype(f32)
    lam_im = a_im.astype(f32)
    mag = jnp.exp(lam_re * dt)
    ang = lam_im * dt
    lb_re = mag * jnp.cos(ang)
    lb_im = mag * jnp.sin(ang)
    den = lam_re * lam_re + lam_im * lam_im
    f_re = ((lb_re - 1.0) * lam_re + lb_im * lam_im) / den
    f_im = (lb_im * lam_re - (lb_re - 1.0) * lam_im) / den
    bb_re, bb_im = _cmul(f_re[..., None], f_im[..., None],
                         b_re.astype(f32), b_im.astype(f32))
    bu_re = jnp.einsum('blgh,gph->blgp', ug, bb_re)
    bu_im = jnp.einsum('blgh,gph->blgp', ug, bb_im)
    a_r = jnp.broadcast_to(lb_re, (1, seq) + lb_re.shape)
    a_i = jnp.broadcast_to(lb_im, (1, seq) + lb_im.shape)
    _, _, s_re, s_im = lax.associative_scan(
        _scan_combine, (a_r, a_i, bu_re, bu_im), axis=1)
    y = (jnp.einsum('blgp,ghp->blgh', s_re, c_re.astype(f32))
         - jnp.einsum('blgp,ghp->blgh', s_im, c_im.astype(f32)))
    y = y.reshape(bsz, seq, D_BRANCH) + d_skip * u
    y = jax.nn.gelu(y)
    z = y @ w_glu + b_glu
    val, gate = jnp.split(z, 2, axis=-1)
    return val * jax.nn.sigmoid(gate)


def pool_mixer(u, w_pool, scale):
    bsz, seq, _ = u.shape
    uf = u.astype(jnp.float32)
    cs = jnp.cumsum(uf, axis=1)
    pos = jnp.arange(seq)
    pooled = []
    for gi, w in enumerate(POOL_WINDOWS):
        c = cs[..., gi * POOL_GROUP:(gi + 1) * POOL_GROUP]
        shifted = jnp.pad(c, ((0, 0), (w, 0), (0, 0)))[:, :seq]
        count = jnp.minimum(pos + 1, w).astype(jnp.float32)[:, None]
        pooled.append((c - shifted) / count)
    diff = jnp.concatenate(pooled, axis=-1) - uf
    diff = diff.reshape(bsz, seq, len(POOL_WINDOWS), POOL_GROUP)
    out = jnp.einsum('blgc,gcd->blgd', diff, w_pool).reshape(bsz, seq, D_BRANCH)
    return out * scale


def short_conv_mixer(h, w_in, conv_w, conv_b, w_out):
    z = h @ w_in
    x_in, b_gate, c_gate, g = jnp.split(z, 4, axis=-1)
    v = c_gate * x_in
    conv = lax.conv_general_dilated(
        v, conv_w.astype(v.dtype)[:, None, :], window_strides=(1,),
        padding=[(CONV_WIDTH - 1, 0)], dimension_numbers=('NWC', 'WIO', 'NWC'),
        feature_group_count=D_CONV) + conv_b
    y = b_gate * conv * jax.nn.silu(g)
    return y @ w_out


def setup_inputs(seed: int = 0) -> dict:
    key = jax.random.key(seed)
    ks = jax.random.split(key, 24)
    f32 = jnp.float32
    nrm = lambda k, shape, s: jax.random.normal(k, shape, f32) * s
    x = jax.random.normal(ks[0], (BATCH, SEQ, D_MODEL), f32)
    norm_g = 1.0 + nrm(ks[1], (DEPTH, D_MODEL), 0.02)
    final_g = 1.0 + nrm(ks[2], (D_MODEL,), 0.02)
    ev_w_in = nrm(ks[3], (N_EVEN, D_MODEL, 4 * D_BRANCH), D_MODEL ** -0.5)
    ev_w_out = nrm(ks[4], (N_EVEN, 2 * D_BRANCH, D_MODEL), (2 * D_BRANCH) ** -0.5)
    n_idx = jnp.arange(S5_STATE, dtype=f32)
    s5_a_re = -0.5 + nrm(ks[5], (N_EVEN, S5_GROUPS, S5_STATE), 0.01)
    s5_a_im = math.pi * n_idx + nrm(ks[6], (N_EVEN, S5_GROUPS, S5_STATE), 0.01)
    s5_log_dt = jax.random.uniform(ks[7], (N_EVEN, S5_GROUPS), f32,
                                   math.log(S5_DT_MIN), math.log(S5_DT_MAX))
    bs = (2.0 * S5_GROUP) ** -0.5
    cscale = (2.0 * S5_STATE) ** -0.5
    s5_b_re = nrm(ks[8], (N_EVEN, S5_GROUPS, S5_STATE, S5_GROUP), bs)
    s5_b_im = nrm(ks[9], (N_EVEN, S5_GROUPS, S5_STATE, S5_GROUP), bs)
    s5_c_re = nrm(ks[10], (N_EVEN, S5_GROUPS, S5_GROUP, S5_STATE), cscale)
    s5_c_im = nrm(ks[11], (N_EVEN, S5_GROUPS, S5_GROUP, S5_STATE), cscale)
    s5_d = nrm(ks[12], (N_EVEN, D_BRANCH), 1.0)
    s5_w_glu = nrm(ks[13], (N_EVEN, D_BRANCH, 2 * D_BRANCH), D_BRANCH ** -0.5)
    s5_b_glu = nrm(ks[14], (N_EVEN, 2 * D_BRANCH), 0.01)
    pool_w = nrm(ks[15], (N_EVEN, len(POOL_WINDOWS), POOL_GROUP, POOL_GROUP), POOL_GROUP ** -0.5)
    pool_scale = 1.0 + nrm(ks[16], (N_EVEN, D_BRANCH), 0.02)
    sc_w_in = nrm(ks[17], (N_ODD, D_MODEL, 4 * D_CONV), D_MODEL ** -0.5)
    sc_conv_w = nrm(ks[18], (N_ODD, CONV_WIDTH, D_CONV), CONV_WIDTH ** -0.5)
    sc_conv_b = nrm(ks[19], (N_ODD, D_CONV), 0.01)
    sc_w_out = nrm(ks[20], (N_ODD, D_CONV, D_MODEL), D_CONV ** -0.5)
    return {"x": x, "norm_g": norm_g, "final_g": final_g,
            "ev_w_in": ev_w_in, "ev_w_out": ev_w_out,
            "s5_a_re": s5_a_re, "s5_a_im": s5_a_im, "s5_log_dt": s5_log_dt,
            "s5_b_re": s5_b_re, "s5_b_im": s5_b_im, "s5_c_re": s5_c_re, "s5_c_im": s5_c_im,
            "s5_d": s5_d, "s5_w_glu": s5_w_glu, "s5_b_glu": s5_b_glu,
            "pool_w": pool_w, "pool_scale": pool_scale,
            "sc_w_in": sc_w_in, "sc_conv_w": sc_conv_w, "sc_conv_b": sc_conv_b,
            "sc_w_out": sc_w_out}


def reference(x, norm_g, final_g, ev_w_in, ev_w_out, s5_a_re, s5_a_im, s5_log_dt,
              s5_b_re, s5_b_im, s5_c_re, s5_c_im, s5_d, s5_w_glu, s5_b_glu,
              pool_w, pool_scale, sc_w_in, sc_conv_w, sc_conv_b, sc_w_out):
    out_dtype = x.dtype
    h_res = x.astype(jnp.float32)
    for layer in range(DEPTH):
        h = rmsnorm(h_res, norm_g[layer])
        if layer % 2 == 0:
            i = layer // 2
            z = h @ ev_w_in[i]
            u_a, g_a, u_b, g_b = jnp.split(z, 4, axis=-1)
            y_a = s5_mixer(u_a, s5_a_re[i], s5_a_im[i], s5_log_dt[i], s5_b_re[i], s5_b_im[i],
                           s5_c_re[i], s5_c_im[i], s5_d[i], s5_w_glu[i], s5_b_glu[i])
            y_b = pool_mixer(u_b, pool_w[i], pool_scale[i])
            y = jnp.concatenate([y_a * jax.nn.silu(g_a), y_b * jax.nn.silu(g_b)], axis=-1)
            h_res = h_res + y @ ev_w_out[i]
        else:
            i = layer // 2
            h_res = h_res + short_conv_mixer(h, sc_w_in[i], sc_conv_w[i], sc_conv_b[i], sc_w_out[i])
    return rmsnorm(h_res, final_g).astype(out_dtype)
```

```python
import contextlib
import numpy as np
import concourse.bass as bass
import concourse.mybir as mybir
from concourse.bass_utils import run_bass_kernel_spmd

F32 = mybir.dt.float32
BF16 = mybir.dt.bfloat16
AF = mybir.ActivationFunctionType
ALU = mybir.AluOpType

L_SEQ = 2048
T = 8
NCH = L_SEQ // T
POOL_W = (2, 4, 8, 16)
N_CORES = 8
STRICT_SAME_ENGINE = True

PV_NORM = 0
PV_FINAL = 32
PV_S5D = 40
PV_BGLU = 56
PV_PSCALE = 88
PV_CONVW = 104
PV_CONVB = 200
PV_N = 232
C_IDENT = 0
C_BD32 = 128
C_MASKE = 256
C_MASKN = 258
C_INVC = 260
C_N = 260 + 64


class Op:
    __slots__ = ('eng', 'fn', 'deps', 'sig', 'seq', 'dsem', 'dval', 'idx')

    def __init__(self, eng, fn):
        self.eng = eng; self.fn = fn; self.deps = []; self.sig = False
        self.seq = None; self.dsem = None; self.dval = None; self.idx = None


class Sched:
    ENGS = ('pe', 'act', 'dve', 'pool', 'sp')

    def __init__(self):
        self.ops = {e: [] for e in self.ENGS}
        self.lastw = {}
        self.readers = {}
        self.dcount = {}

    def _add_reader(self, key, o):
        d = self.readers.setdefault(key, {})
        if o.dsem is not None:
            d.setdefault('dma', []).append(o)
        else:
            d[o.eng] = o

    def op(self, eng, fn, reads=(), writes=(), dma=None):
        o = Op(eng, fn)
        if dma is not None:
            c = self.dcount.get(dma, 0) + 1
            self.dcount[dma] = c
            o.dsem = dma; o.dval = 16 * c
        deps = []
        for r in reads:
            w = self.lastw.get(r)
            if w is not None:
                deps.append((w, True))
        for k in writes:
            w = self.lastw.get(k)
            if w is not None:
                deps.append((w, False))
            rd = self.readers.get(k)
            if rd:
                for kk, v in rd.items():
                    if kk == 'dma':
                        deps.extend((x, False) for x in v)
                    else:
                        deps.append((v, False))
        best = {}
        dd = []
        for d, raw in deps:
            if d is o:
                continue
            if d.dsem is not None:
                if d not in dd:
                    dd.append(d)
            else:
                if eng == 'pe' and d.eng == 'pe' and o.dsem is None:
                    continue
                if (not STRICT_SAME_ENGINE) and d.eng == eng and o.dsem is None and not raw:
                    continue
                b = best.get(d.eng)
                if b is None or d.idx > b.idx:
                    best[d.eng] = d
        for d in best.values():
            d.sig = True
        o.deps = list(best.values()) + dd
        o.idx = len(self.ops[eng])
        self.ops[eng].append(o)
        for r in reads:
            self._add_reader(r, o)
        for k in writes:
            self.lastw[k] = o
            self.readers[k] = {}
        return o

    def emit(self, nc, stack):
        esem = {}
        for e in self.ENGS:
            esem[e] = stack.enter_context(nc.semaphore("es_" + e))
        dsem = {}
        for n, k in enumerate(self.dcount):
            dsem[k] = stack.enter_context(nc.semaphore("ds_%d" % n))
        for e in self.ENGS:
            c = 0
            for o in self.ops[e]:
                if o.dsem is None and o.sig:
                    c += 1; o.seq = c

        def run(e, engobj):
            waited = {}
            for o in self.ops[e]:
                for d in o.deps:
                    if d.dsem is not None:
                        key = ('d', d.dsem); sem = dsem[d.dsem]; val = d.dval
                    else:
                        key = ('e', d.eng); sem = esem[d.eng]; val = d.seq
                    if waited.get(key, 0) < val:
                        engobj.wait_ge(sem, val)
                        waited[key] = val
                inst = o.fn(engobj)
                if inst is None:
                    continue
                if o.dsem is not None:
                    inst.then_inc(dsem[o.dsem], 16)
                elif o.sig:
                    inst.then_inc(esem[e], 1)

        block = stack.enter_context(nc.Block())

        @block.tensor
        def _(t): run('pe', t)

        @block.scalar
        def _(t): run('act', t)

        @block.vector
        def _(t): run('dve', t)

        @block.gpsimd
        def _(t): run('pool', t)

        @block.sync
        def _(t): run('sp', t)


class Buf:
    def __init__(self, name, tens, off, dtype, n, ap=None):
        self.name = name; self.tens = tens; self.off = off; self.dtype = dtype; self.n = n
        self.esz = 4 if dtype == F32 else 2
        size = n * self.esz
        assert off % 4 == 0 and size % 4 == 0, (off, size)
        self.keys = [(name, b) for b in range(off // 1024, (off + size - 1) // 1024 + 1)]
        if ap is None:
            ap = tens[:, off // 4:(off + size) // 4]
            if dtype != F32:
                ap = ap.bitcast(dtype)
        self.ap = ap

    def __getitem__(self, s):
        a = 0 if s.start is None else s.start
        b = self.n if s.stop is None else s.stop
        assert s.step is None and 0 <= a < b <= self.n, (a, b, self.n)
        if (a * self.esz) % 4 == 0 and ((b - a) * self.esz) % 4 == 0:
            return Buf(self.name, self.tens, self.off + a * self.esz, self.dtype, b - a)
        o = Buf.__new__(Buf)
        o.name = self.name; o.tens = self.tens; o.off = self.off; o.dtype = self.dtype
        o.n = b - a; o.esz = self.esz
        lo = self.off + a * self.esz; hi = self.off + b * self.esz
        o.keys = [(self.name, k) for k in range(lo // 1024, (hi - 1) // 1024 + 1)]
        o.ap = self.ap[:, a:b]
        return o

    def v(self, ap):
        o = Buf.__new__(Buf)
        o.name = self.name; o.tens = self.tens; o.off = self.off; o.dtype = self.dtype
        o.n = self.n; o.esz = self.esz; o.keys = self.keys; o.ap = ap
        return o


class PSv:
    def __init__(self, ap, keys):
        self.ap = ap; self.keys = keys


def _k(*objs):
    out = []
    for o in objs:
        if o is None or isinstance(o, (int, float)):
            continue
        out.extend(o.keys)
    return out


def build_program(layers=(0, 1, 2, 3), do_final=True, n_setup=None):
    nc = bass.Bass("TRN2", target_bir_lowering=False)
    S = Sched()
    st = contextlib.ExitStack()

    def din(name, shape, dt=F32):
        return nc.dram_tensor(name, list(shape), dt, kind="ExternalInput").ap()

    x_d = din("x", [2048, 1024])
    need_i = sorted(set(l // 2 for l in layers))
    w_sc_in = {}; w_sc_out = {}; w_ev_in = {}; w_glu = {}; w_ev_out = {}; w_pool = {}
    for l in layers:
        i = l // 2
        if l % 2 == 1:
            w_sc_in[i] = din("w_sc_in%d" % i, [16, 4, 128, 1024])
            w_sc_out[i] = din("w_sc_out%d" % i, [8, 2, 128, 1024])
        else:
            w_ev_in[i] = din("w_ev_in%d" % i, [32, 128, 1024])
            w_glu[i] = din("w_glu%d" % i, [16, 128, 1024])
            w_ev_out[i] = din("w_ev_out%d" % i, [2, 8, 128, 1024])
            w_pool[i] = din("w_pool%d" % i, [2, 128, 1024])
    pvec_d = din("pvec", [128, PV_N])
    cst_d = din("cst", [128, C_N])
    s5q_d = din("s5q", [2, 128, 1120])
    s5h_d = din("s5h", [2, 128, 2560])
    out_d = nc.dram_tensor("out", [2048, 1024], F32, kind="ExternalOutput").ap()
    xb_s = nc.dram_tensor("xb_s", [2, 8, 128, T * 2 * 128], BF16, kind="Internal").ap()
    cs_s = nc.dram_tensor("cs_s", [2, 8, 128, T * 2 * 128], BF16, kind="Internal").ap()
    kb_s = nc.dram_tensor("kb_s", [2, 8, 128, T * 128], BF16, kind="Internal").ap()
    tab_s = nc.dram_tensor("tab_s", [2, 8, 128, 4 * 768], F32, kind="Internal").ap()

    with st:
        def sb(name, nwords):
            return st.enter_context(nc.sbuf_tensor(name, [128, nwords], F32))
        HT_t = sb("HT", 16384)
        R1_t = sb("R1", 8192)
        R23_t = sb("R23", 16384)
        WB_t = sb("WB", 4096)
        TMP_t = sb("TMP", 5184)
        MISC_t = sb("MISC", 1024)
        assert PV_N + C_N + 256 + 160 <= 1024
        ps_t = st.enter_context(nc.psum_tensor("ps", [128, 4096], F32))

        def HT(ft, c0=0, c1=2048):
            return Buf("HT", HT_t, (ft * 2048 + c0) * 4, F32, c1 - c0)

        def R1b(tile, c0=0, c1=2048):
            return Buf("R1", R1_t, (tile * 2048 + c0) * 2, BF16, c1 - c0)

        def R23b(tile, c0=0, c1=2048):
            return Buf("R23", R23_t, (tile * 2048 + c0) * 2, BF16, c1 - c0)

        def R1v(off, dtype, n): return Buf("R1", R1_t, off, dtype, n)
        def R23v(off, dtype, n): return Buf("R23", R23_t, off, dtype, n)
        def TMPv(off, dtype, n): return Buf("TMP", TMP_t, off, dtype, n)
        def MISCv(off, dtype, n): return Buf("MISC", MISC_t, off, dtype, n)

        def PS(bank, c0=0, c1=512, p0=0, p1=128):
            return PSv(ps_t[p0:p1, bank * 512 + c0: bank * 512 + c1], [('ps', bank)])

        def PSspan(b0, nb):
            return PSv(ps_t[:, b0 * 512:(b0 + nb) * 512], [('ps', b) for b in range(b0, b0 + nb)])

        pvec = MISCv(0, F32, PV_N)
        cst = MISCv(PV_N * 4, F32, C_N)
        dec = MISCv((PV_N + C_N) * 4, F32, 64)
        utp = MISCv((PV_N + C_N + 64) * 4, F32, 128)
        onesb = MISCv((PV_N + C_N + 192) * 4, BF16, 128)
        sm = MISCv((PV_N + C_N + 256) * 4, F32, 160)

        def pcol(c): return pvec[c:c + 1]
        ident = cst[C_IDENT:C_IDENT + 128]
        bd32 = cst[C_BD32:C_BD32 + 128]

        uniq = [0]

        def uchan():
            uniq[0] += 1
            return ('u', uniq[0])

        def dma(eng, out, in_ap_or_buf, chan, out_is_dram=False, rkeys=(), wkeys=()):
            if chan == 'par':
                chan = uchan()
            if out_is_dram:
                src = in_ap_or_buf
                S.op(eng, lambda e: e.dma_start(out=out, in_=src.ap), list(src.keys) + list(rkeys), list(wkeys), dma=chan)
            else:
                src = in_ap_or_buf
                S.op(eng, lambda e: e.dma_start(out=out.ap, in_=src), list(rkeys), list(out.keys) + list(wkeys), dma=chan)

        def act(out, in_, func, bias=None, scale=None):
            kw = {}
            rk = _k(in_, bias, scale)
            if bias is not None:
                kw['bias'] = bias if isinstance(bias, (int, float)) else bias.ap
            if scale is not None:
                kw['scale'] = scale if isinstance(scale, (int, float)) else scale.ap
            S.op('act', lambda e: e.activation(out=out.ap, in_=in_.ap, func=func, **kw), rk, out.keys)

        def tt(out, a, b, op, eng='dve'):
            S.op(eng, lambda e: e.tensor_tensor(out=out.ap, in0=a.ap, in1=b.ap, op=op), _k(a, b), out.keys)

        def stt(out, in0, scalar, in1, op0, op1, eng='dve'):
            sc = scalar if isinstance(scalar, (int, float)) else scalar.ap
            S.op(eng, lambda e: e.scalar_tensor_tensor(out=out.ap, in0=in0.ap, scalar=sc, in1=in1.ap, op0=op0, op1=op1),
                 _k(in0, scalar, in1), out.keys)

        def ts(out, in0, s1, s2, op0, op1=None, eng='dve'):
            a1 = s1 if isinstance(s1, (int, float)) else s1.ap
            a2 = None if s2 is None else (s2 if isinstance(s2, (int, float)) else s2.ap)
            kw = {} if op1 is None else {'op1': op1}
            S.op(eng, lambda e: e.tensor_scalar(out=out.ap, in0=in0.ap, scalar1=a1, scalar2=a2, op0=op0, **kw),
                 _k(in0, s1, s2), out.keys)

        def cp(out, in_, eng='dve'):
            S.op(eng, lambda e: e.tensor_copy(out=out.ap, in_=in_.ap), _k(in_), out.keys)

        def memset(out, val, eng='dve'):
            S.op(eng, lambda e: e.memset(out.ap, val), [], out.keys)

        def recip(out, in_):
            S.op('dve', lambda e: e.reciprocal(out=out.ap, in_=in_.ap), _k(in_), out.keys)

        def mm(out, lhsT, rhs, start, stop, tile_position=None):
            kw = {}
            if tile_position is not None:
                kw['tile_position'] = tile_position
            S.op('pe', lambda e: e.matmul(out.ap, lhsT=lhsT.ap, rhs=rhs.ap, start=start, stop=stop, **kw),
                 _k(lhsT, rhs), out.keys)

        def tr(out, in_):
            S.op('pe', lambda e: e.transpose(out.ap, in_.ap, ident.ap), _k(in_, ident), out.keys)

        NSLOT = 8
        wq = []
        wstate = {'issued': 0, 'next': 0}
        AHEAD = 6

        def wslot(n):
            return Buf("WB", WB_t, (n % NSLOT) * 2048, BF16, 1024)

        def w_issue_upto(n):
            while wstate['issued'] <= n and wstate['issued'] < len(wq):
                i = wstate['issued']
                dma('pool', wslot(i), wq[i], ('w', i % NSLOT))
                wstate['issued'] += 1

        def wget():
            n = wstate['next']
            wstate['next'] += 1
            w_issue_upto(n + AHEAD)
            return wslot(n)

        for l in layers:
            i = l // 2
            if l % 2 == 0:
                for j in range(8):
                    wq.append(w_ev_in[i][16 + j]); wq.append(w_ev_in[i][24 + j])
                for o in range(8):
                    wq.append(w_ev_out[i][1, o])
                for j in range(8):
                    wq.append(w_ev_in[i][j])
                for j in range(8):
                    wq.append(w_ev_in[i][8 + j])
                for j in range(8):
                    wq.append(w_glu[i][8 + j]); wq.append(w_glu[i][j])
                for o in range(8):
                    wq.append(w_ev_out[i][0, o])
            else:
                for j in range(16):
                    for w in range(4):
                        wq.append(w_sc_in[i][j, w])
                for o in range(8):
                    wq.append(w_sc_out[i][o, 0]); wq.append(w_sc_out[i][o, 1])

        dma('sp', pvec, pvec_d, 'par')
        dma('sp', cst, cst_d, 'par')
        memset(onesb, 1.0)
        w_issue_upto(AHEAD - 1)

        xs = [R1v(s_ * 16384, F32, 4096) for s_ in range(2)]
        for c in range(4):
            xsb = xs[c % 2]
            S.op('sp', (lambda xsb, c: lambda e: e.dma_start(
                out=xsb.ap.rearrange("p (t f) -> p t f", t=4),
                in_=x_d[c * 512:(c + 1) * 512, :].rearrange("(t p) f -> p t f", p=128)))(xsb, c),
                [], xsb.keys, dma=('x', c % 2))
            for ft in range(8):
                for tl in range(4):
                    tr(PS(ft, tl * 128, (tl + 1) * 128), xsb[tl * 1024 + ft * 128: tl * 1024 + (ft + 1) * 128])
                if ft % 2 == 0:
                    act(HT(ft, c * 512, (c + 1) * 512), PS(ft), AF.Copy)
                else:
                    cp(HT(ft, c * 512, (c + 1) * 512), PS(ft))

        def cmul2(o_re, o_im, a_re, a_im, b_re, b_im, t1, t2, t3, t4):
            tt(t1, a_re, b_re, ALU.mult); tt(t2, a_im, b_im, ALU.mult)
            tt(o_re, t1, t2, ALU.subtract)
            tt(t3, a_re, b_im, ALU.mult, 'pool'); tt(t4, a_im, b_re, ALU.mult, 'pool')
            tt(o_im, t3, t4, ALU.add, 'pool')

        def csq2(o_re, o_im, a_re, a_im, t1):
            tt(o_re, a_re, a_re, ALU.mult); tt(t1, a_im, a_im, ALU.mult); tt(o_re, o_re, t1, ALU.subtract)
            tt(o_im, a_re, a_im, ALU.mult, 'pool'); ts(o_im, o_im, 2.0, 0.0, ALU.mult, ALU.add, eng='pool')

        def discretise(lam_re, lam_im, logdt, n, alloc):
            dtv = alloc(n); act(dtv, logdt, AF.Exp)
            prd = alloc(n); tt(prd, lam_re, dtv, ALU.mult)
            mag = alloc(n); act(mag, prd, AF.Exp)
            ang = alloc(n); tt(ang, lam_im, dtv, ALU.mult)
            c = alloc(n); s_ = alloc(n); c2 = alloc(n); s2 = alloc(n); t1 = alloc(n); hp = alloc(n)
            ts(hp, ang, -1.0 / 8.0, float(np.pi / 2), ALU.mult, ALU.add)
            act(c, hp, AF.Sin)
            act(s_, ang, AF.Sin, scale=1.0 / 8.0)
            for _ in range(3):
                csq2(c2, s2, c, s_, t1)
                c, c2 = c2, c; s_, s2 = s2, s_
            lb_re = alloc(n); lb_im = alloc(n)
            tt(lb_re, mag, c, ALU.mult); tt(lb_im, mag, s_, ALU.mult, 'pool')
            return dict(mag=mag, c1=c, s1=s_, lb_re=lb_re, lb_im=lb_im, dtv=dtv, prd=prd)

        evac_rr = [0]

        def evac(dst, src):
            evac_rr[0] ^= 1
            if evac_rr[0]:
                act(dst, src, AF.Copy)
            else:
                cp(dst, src)

        def s5_setup(i):
            cur = [0]

            def alloc23(n, dtype=F32):
                b_ = R23v(cur[0], dtype, n); cur[0] += n * (4 if dtype == F32 else 2)
                assert cur[0] <= 65536, cur[0]
                return b_
            cs0 = alloc23(2048)
            hin = alloc23(2560)
            dma('sp', hin, s5h_d[i], 'par')
            lamre, lamim, ldt, cre, cim = [hin[k * 512:(k + 1) * 512] for k in range(5)]
            d = discretise(lamre, lamim, ldt, 512, alloc23)
            zb = [[alloc23(512), alloc23(512)], [alloc23(512), alloc23(512)]]
            t3 = alloc23(512); t4 = alloc23(512)
            assert cur[0] == 55296, cur[0]
            t1 = TMPv(16384, F32, 512); t2 = TMPv(18432, F32, 512)
            cste = [[TMPv(0, F32, 1024), TMPv(4096, F32, 1024)], [TMPv(8192, F32, 1024), TMPv(12288, F32, 1024)]]
            csst = R1v(0, BF16, 8 * T * 2 * 128)
            allocq = alloc23
            qin = allocq(1120)
            dma('sp', qin, s5q_d[i], 'par')
            qlre = qin[0:32]; qlim = qin[32:64]; qldt = qin[64:96]
            bqre = qin[96:608]; bqim = qin[608:1120]
            dq = discretise(qlre, qlim, qldt, 32, allocq)
            den = allocq(32); q1 = allocq(32); q2 = allocq(32); fre = allocq(32); fim = allocq(32)
            lm1 = allocq(32)
            tt(q1, qlre, qlre, ALU.mult); tt(q2, qlim, qlim, ALU.mult); tt(den, q1, q2, ALU.add)
            recip(den, den)
            ts(lm1, dq['lb_re'], -1.0, None, ALU.add)
            tt(q1, lm1, qlre, ALU.mult); tt(q2, dq['lb_im'], qlim, ALU.mult); tt(q1, q1, q2, ALU.add)
            tt(fre, q1, den, ALU.mult)
            tt(q1, dq['lb_im'], qlre, ALU.mult); tt(q2, lm1, qlim, ALU.mult); tt(q1, q1, q2, ALU.subtract)
            tt(fim, q1, den, ALU.mult)
            act(dec[i * 32:(i + 1) * 32], dq['prd'], AF.Exp, scale=float(T))
            uc = allocq(32); us = allocq(32); uc2 = allocq(32); us2 = allocq(32)
            cp(uc, dq['c1']); cp(us, dq['s1'])
            nsq = {8: 3, 16: 4}[T]
            for _ in range(nsq):
                csq2(uc2, us2, uc, us, q1)
                uc, uc2 = uc2, uc; us, us2 = us2, us
            cp(utp[i * 64:i * 64 + 32], uc); cp(utp[i * 64 + 32:i * 64 + 64], us)
            z = [cre, cim]
            for k in range(-1, T):
                if k >= 0:
                    zn = zb[k % 2]
                    cmul2(zn[0], zn[1], z[0], z[1], d['lb_re'], d['lb_im'], t1, t2, t3, t4)
                    z = zn
                ce = cste[(k + 1) % 2]
                for ri in range(2):
                    for e_ in range(2):
                        outv = ce[ri].v(ce[ri].ap.rearrange("p (u e q) -> p u e q", u=8, e=2)[:, :, e_, :])
                        inv = z[ri].v(z[ri].ap.rearrange("p (u q) -> p u q", u=8))
                        mc = (C_MASKE if ri == 0 else C_MASKN) + e_
                        act(outv, inv, AF.Identity, scale=cst[mc:mc + 1])
                for ri in range(2):
                    for hf in range(2):
                        bank = ((k + 1) % 2) * 4 + ri * 2 + hf
                        for u4 in range(4):
                            ut = hf * 4 + u4
                            tr(PS(bank, u4 * 128, (u4 + 1) * 128), ce[ri][ut * 128:(ut + 1) * 128])
                        psv = PSv(PS(bank).ap.rearrange("p (a b) -> p a b", a=4), PS(bank).keys)
                        if k == -1:
                            dst = cs0.v(cs0.ap.rearrange("p (u f) -> p u f", u=8)[:, hf * 4:(hf + 1) * 4, ri * 128:(ri + 1) * 128])
                        else:
                            o_ = (k * 2 + ri) * 128
                            dst = csst.v(csst.ap.rearrange("p (u f) -> p u f", u=8)[:, hf * 4:(hf + 1) * 4, o_:o_ + 128])
                        evac(dst, psv)
            S.op('sp', lambda e: e.dma_start(out=cs_s[i].rearrange("u p f -> p u f"),
                                             in_=csst.ap.rearrange("p (u f) -> p u f", u=8)),
                 csst.keys, [('cs_s', i)], dma=uchan())

            cur[0] = 8192

            def bc16(b32):
                return b32.v(b32.ap.rearrange("p (g o) -> p g o", o=1).to_broadcast([128, 32, 16]))

            def v3(b512):
                return b512.v(b512.ap.rearrange("p (g h) -> p g h", h=16))
            bb = [[allocq(512), allocq(512)], [allocq(512), allocq(512)]]
            bste = [[allocq(1024), allocq(1024)], [allocq(1024), allocq(1024)]]
            kst = allocq(8 * T * 128, BF16)
            assert cur[0] <= 55296, cur[0]
            b1 = TMPv(0, F32, 512); b2 = TMPv(2048, F32, 512); b3 = TMPv(4096, F32, 512); b4 = TMPv(6144, F32, 512)
            ktmp = TMPv(8192, F32, 512)
            cmul2(v3(bb[0][0]), v3(bb[0][1]), bc16(fre), bc16(fim), v3(bqre), v3(bqim), v3(b1), v3(b2), v3(b3), v3(b4))
            xbst = R1v(0, BF16, 8 * T * 2 * 128)
            memset(bste[0][0], 0.0); memset(bste[0][1], 0.0, eng='pool')
            memset(bste[1][0], 0.0); memset(bste[1][1], 0.0, eng='pool')
            bd4 = bd32.v(bd32.ap.rearrange("p (o c) -> p o c", o=1).to_broadcast([128, 4, 128]))
            curb = bb[0]
            for m in range(T):
                if m >= 1:
                    nb = bb[m % 2]
                    cmul2(v3(nb[0]), v3(nb[1]), bc16(dq['lb_re']), bc16(dq['lb_im']), v3(curb[0]), v3(curb[1]),
                          v3(b1), v3(b2), v3(b3), v3(b4))
                    curb = nb
                be = bste[m % 2]
                for ri in range(2):
                    for e_ in range(2):
                        w0 = be[ri].off // 4
                        dstap = be[ri].tens[e_ * 64:(e_ + 1) * 64, w0:w0 + 1024] \
                            .rearrange("p (g e h) -> p g e h", e=2, h=16)[:, :, e_, :]
                        w1 = curb[ri].off // 4
                        srcap = curb[ri].tens[e_ * 64:(e_ + 1) * 64, w1:w1 + 512] \
                            .rearrange("p (g h) -> p g h", h=16)
                        S.op('act', (lambda da, sa: lambda e: e.activation(out=da, in_=sa, func=AF.Copy))(dstap, srcap),
                             curb[ri].keys, be[ri].keys)
                k = T - 1 - m
                for ri in range(2):
                    for hf in range(2):
                        bank = ri * 2 + hf
                        for u4 in range(4):
                            ut = hf * 4 + u4
                            tr(PS(bank, u4 * 128, (u4 + 1) * 128), be[ri][ut * 128:(ut + 1) * 128])
                        psv = PSv(PS(bank).ap.rearrange("p (a b) -> p a b", a=4), PS(bank).keys)
                        o_ = (k * 2 + ri) * 128
                        dst = xbst.v(xbst.ap.rearrange("p (u f) -> p u f", u=8)[:, hf * 4:(hf + 1) * 4, o_:o_ + 128])
                        evac(dst, psv)
                for hf in range(2):
                    kb = 4 + (m % 2) * 2 + hf
                    for u4 in range(4):
                        ut = hf * 4 + u4
                        for ri in range(2):
                            mm(PS(kb, u4 * 128, (u4 + 1) * 128), be[ri][ut * 128:(ut + 1) * 128],
                               cs0[(ut * 2 + ri) * 128:(ut * 2 + ri + 1) * 128], ri == 0, ri == 1)
                    psv = PSv(PS(kb).ap.rearrange("p (a b) -> p a b", a=4), PS(kb).keys)
                    if m == 0:
                        kt4 = ktmp.v(ktmp.ap.rearrange("p (a b) -> p a b", a=4))
                        tt(kt4, psv, bd4, ALU.mult)
                        for u4 in range(4):
                            ut = hf * 4 + u4
                            ko = (ut * T + m) * 128
                            stt(kst[ko:ko + 128], ident, pcol(PV_S5D + i * 8 + ut), ktmp[u4 * 128:(u4 + 1) * 128],
                                ALU.mult, ALU.add)
                    else:
                        dst = kst.v(kst.ap.rearrange("p (u f) -> p u f", u=8)[:, hf * 4:(hf + 1) * 4, m * 128:(m + 1) * 128])
                        tt(dst, psv, bd4, ALU.mult)
            S.op('sp', lambda e: e.dma_start(out=xb_s[i].rearrange("u p f -> p u f"),
                                             in_=xbst.ap.rearrange("p (u f) -> p u f", u=8)),
                 xbst.keys, [('xb_s', i)], dma=uchan())
            S.op('sp', lambda e: e.dma_start(out=kb_s[i].rearrange("u p f -> p u f"),
                                             in_=kst.ap.rearrange("p (u f) -> p u f", u=8)),
                 kst.keys, [('kb_s', i)], dma=uchan())

            tabc = R1v(0, F32, 32 * NCH)
            tabs = R23v(0, F32, 32 * NCH)
            twd = [R23v(32768, F32, 2048), R23v(40960, F32, 2048)]
            twp = [R23v(49152, F32, 2048), R23v(57344, F32, 2048)]
            pn = [sm[0:32], sm[32:64]]; pn2 = [sm[64:96], sm[96:128]]; sq1 = sm[128:160]

            def t3(b_, g0, g1, n0, n1):
                return b_.v(b_.ap.rearrange("p (g c) -> p g c", c=NCH)[:, g0:g1, n0:n1])

            def bcn(b32, g0, g1, n):
                return b32.v(b32.ap.rearrange("p (g o) -> p g o", o=1)[:, g0:g1, :].to_broadcast([128, g1 - g0, n]))
            memset(t3(tabc, 0, 32, 0, 1), 1.0); memset(t3(tabs, 0, 32, 0, 1), 0.0)
            cp(pn[0], utp[i * 64:i * 64 + 32]); cp(pn[1], utp[i * 64 + 32:i * 64 + 64])
            n = 1
            while n < NCH:
                gstep = min(32, 2048 // n)
                for g0 in range(0, 32, gstep):
                    g1 = g0 + gstep
                    ng = g1 - g0

                    def wv(b_):
                        bb_ = b_[0:ng * n]
                        return bb_.v(bb_.ap.rearrange("p (g c) -> p g c", c=n))
                    tt(wv(twd[0]), t3(tabc, g0, g1, 0, n), bcn(pn[0], g0, g1, n), ALU.mult)
                    tt(wv(twd[1]), t3(tabs, g0, g1, 0, n), bcn(pn[1], g0, g1, n), ALU.mult)
                    tt(wv(twp[1]), t3(tabs, g0, g1, 0, n), bcn(pn[0], g0, g1, n), ALU.mult)
                    tt(wv(twp[0]), t3(tabc, g0, g1, 0, n), bcn(pn[1], g0, g1, n), ALU.mult, 'pool')
                    tt(t3(tabc, g0, g1, n, 2 * n), wv(twd[0]), wv(twd[1]), ALU.subtract)
                    tt(t3(tabs, g0, g1, n, 2 * n), wv(twp[0]), wv(twp[1]), ALU.add, 'pool')
                n *= 2
                if n < NCH:
                    csq2(pn2[0], pn2[1], pn[0], pn[1], sq1)
                    pn, pn2 = pn2, pn
            for (src, offs) in ((tabc, (0, 512)), (tabs, (256,))):
                for o_ in offs:
                    for ut in range(8):
                        S.op('sp', (lambda src, o_, ut: lambda e: e.dma_start(
                            out=tab_s[i, ut].rearrange("p (r w) -> p r w", w=768)[:, :, o_:o_ + NCH],
                            in_=src.ap.rearrange("p (g c) -> p g c", c=NCH)[:, ut * 4:(ut + 1) * 4, :]))(src, o_, ut),
                            src.keys, [('tab_s', i, o_, ut)], dma=uchan())

        setup_layers = [l // 2 for l in layers if l % 2 == 0]
        for i in setup_layers:
            s5_setup(i)

        def rmsnorm_to(gcol0, dst_fn, dst_is_fp32_inplace=False):
            sq = [TMPv(0, BF16, 2048), TMPv(4096, BF16, 2048)]
            for ft in range(8):
                sqb = sq[ft % 2]
                act(sqb, HT(ft), AF.Square)
                for c in range(4):
                    mm(PS(c), onesb, sqb[c * 512:(c + 1) * 512], ft == 0, ft == 7)
            rs = TMPv(8192, F32, 2048)
            act(rs, PSspan(0, 4), AF.Sqrt, bias=1e-6, scale=1.0 / 1024.0)
            recip(PSspan(4, 4), rs)
            for ft in range(8):
                stt(dst_fn(ft), HT(ft), pcol(gcol0 + ft), PSspan(4, 4), ALU.mult, ALU.mult)

        def proj(wt, rhs_fn, nk, g):
            for kt in range(nk):
                wbuf = wt[kt // 8]
                lhs = wbuf[(kt % 8) * 128:(kt % 8 + 1) * 128]
                for c in range(4):
                    mm(PS(4 * g + c), lhs, rhs_fn(kt, c), kt == 0, kt == nk - 1)

        grp = [0]

        def nextg():
            g = grp[0]; grp[0] ^= 1
            return g

        def hn(kt, c): return R1b(kt, c * 512, (c + 1) * 512)

        def residual_add(o, g):
            tt(HT(o), PSspan(4 * g, 4), HT(o), ALU.add)

        def odd_layer(l):
            i = l // 2
            rmsnorm_to(PV_NORM + l * 8, lambda ft: R1b(ft))
            V = TMPv(0, F32, 2064)
            ACC = TMPv(8256, F32, 2048)
            SG = TMPv(8256 + 8192, BF16, 2048)
            memset(V[0:16], 0.0)
            Vd = V[16:2064]
            for j in range(16):
                cw = [pcol(PV_CONVW + i * 48 + k * 16 + j) for k in range(3)]
                cb = pcol(PV_CONVB + i * 16 + j)
                g = nextg(); proj([wget()], hn, 8, g)
                act(Vd, PSspan(4 * g, 4), AF.Copy)
                g = nextg(); proj([wget()], hn, 8, g)
                tt(Vd, PSspan(4 * g, 4), Vd, ALU.mult)
                act(ACC, Vd, AF.Identity, bias=cb, scale=cw[2])
                stt(ACC, V[15:2063], cw[1], ACC, ALU.mult, ALU.add)
                stt(ACC, V[14:2062], cw[0], ACC, ALU.mult, ALU.add)
                g = nextg(); proj([wget()], hn, 8, g)
                act(SG, PSspan(4 * g, 4), AF.Silu)
                g = nextg(); proj([wget()], hn, 8, g)
                tt(ACC, PSspan(4 * g, 4), ACC, ALU.mult)
                tt(R23b(j), ACC, SG, ALU.mult)
            for o in range(8):
                g = nextg()
                proj([wget(), wget()], lambda kt, c: R23b(kt, c * 512, (c + 1) * 512), 16, g)
                residual_add(o, g)

        def even_layer(l):
            i = l // 2
            rmsnorm_to(PV_NORM + l * 8, lambda ft: R1b(ft))
            R3o = 32768
            UB = R23v(R3o + 0, F32, 2064)
            WA = R23v(R3o + 9216, F32, 2064)
            WBf = R23v(R3o + 18432, F32, 2064)
            DFB = [TMPv(0, BF16, 2048), TMPv(4096, BF16, 2048)]
            SGB = [TMPv(8192, BF16, 2048), TMPv(12288, BF16, 2048)]
            T16 = TMPv(16384, F32, 16)
            memset(UB[0:16], 0.0); memset(WA[0:16], 0.0); memset(WBf[0:16], 0.0)
            PW = [R23v(R3o + 27648, BF16, 1024), R23v(R3o + 27648 + 2048, BF16, 1024)]
            for hh in range(2):
                dma('pool', PW[hh], w_pool[i][hh], ('pw', hh))
            for j in range(8):
                gi = j // 2; w = POOL_W[gi]
                g = nextg(); proj([wget()], hn, 8, g)
                act(UB[16:2064], PSspan(4 * g, 4), AF.Copy)
                bufs = [UB, WA, WBf]
                src = UB; sh = 1; nadd = {2: 1, 4: 2, 8: 3, 16: 4}[w]
                dsts = [WA, WBf]
                for a in range(nadd):
                    dst = dsts[a % 2]
                    tt(dst[16:2064], src[16:2064], src[16 - sh:2064 - sh], ALU.add, 'pool')
                    src = dst; sh *= 2
                stt(DFB[j % 2], src[16:2064], 1.0 / w, UB[16:2064], ALU.mult, ALU.subtract)
                tt(T16, src[16:32], cst[C_INVC + gi * 16: C_INVC + gi * 16 + 16], ALU.mult)
                tt(DFB[j % 2][0:16], T16, UB[16:32], ALU.subtract)
                g = nextg(); proj([wget()], hn, 8, g)
                act(SGB[j % 2], PSspan(4 * g, 4), AF.Silu)
                if j % 2 == 1:
                    for dt_ in range(2):
                        g = nextg()
                        pwb = PW[gi // 2]
                        base = ((gi % 2) * 2 + dt_) * 256
                        for kt in range(2):
                            for c in range(4):
                                mm(PS(4 * g + c), pwb[base + kt * 128: base + (kt + 1) * 128],
                                   DFB[kt][c * 512:(c + 1) * 512], kt == 0, kt == 1)
                        o_t = gi * 2 + dt_
                        stt(R23b(o_t), PSspan(4 * g, 4), pcol(PV_PSCALE + i * 8 + o_t), SGB[dt_], ALU.mult, ALU.mult)
            for o in range(8):
                g = nextg()
                proj([wget()], lambda kt, c: R23b(kt, c * 512, (c + 1) * 512), 8, g)
                residual_add(o, g)
            for j in range(8):
                g = nextg(); proj([wget()], hn, 8, g)
                ubj = R23b(j)
                psn = PSspan(4 * g, 4)
                act(ubj.v(ubj.ap.rearrange("p (k c) -> p c k", k=T)),
                    PSv(psn.ap.rearrange("p (c k) -> p c k", k=T), psn.keys), AF.Copy)
            for j in range(8):
                g = nextg(); proj([wget()], hn, 8, g)
                act(R23b(8 + j), PSspan(4 * g, 4), AF.Silu)
            tset = [dict(P1=TMPv(0, F32, 512), P2=TMPv(2048, F32, 512), BT=TMPv(4096, F32, 512),
                         Q1=TMPv(6144, F32, 512), Q2=TMPv(8192, F32, 512)),
                    dict(P1=R1v(22528, F32, 512), P2=R1v(24576, F32, 512), BT=R1v(26624, F32, 512),
                         Q1=R1v(28672, F32, 512), Q2=R1v(30720, F32, 512))]
            SBW = 258
            SBs = [TMPv(12288, BF16, 4 * 2 * SBW), TMPv(12288 + 4 * 2 * SBW * 2, BF16, 4 * 2 * SBW)]
            memset(SBs[0], 0.0); memset(SBs[1], 0.0)
            XBc = R1v(0, BF16, T * 2 * 128)
            CSc = R1v(4096, BF16, T * 2 * 128)
            KBc = R1v(8192, BF16, T * 128)
            TAB = R1v(10240, F32, 4 * 768)
            pcount = 0

            def c_load_x(ut):
                dma('sp', XBc, xb_s[i, ut], ('c5', 0), rkeys=[('xb_s', i)])
                for pr in range(4):
                    dma('sp', TAB[pr * 768:(pr + 1) * 768], tab_s[i, ut][:, pr * 768:(pr + 1) * 768], ('c5', 3, pr),
                        rkeys=[('tab_s', i, 0, ut), ('tab_s', i, 256, ut), ('tab_s', i, 512, ut)])

            def c_load_y(ut):
                dma('sp', CSc, cs_s[i, ut], ('c5', 1), rkeys=[('cs_s', i)])
                dma('sp', KBc, kb_s[i, ut], ('c5', 2), rkeys=[('kb_s', i)])

            def c_xmm(ut):
                ub = R23b(ut)
                for ri in range(2):
                    for k in range(T):
                        for pr in range(4):
                            bank = 4 + pr
                            tp = (96, 0) if pr == 3 else None
                            o_ = (k * 2 + ri) * 128
                            lhs = XBc.v(XBc.ap[32 * pr:32 * pr + 32, o_:o_ + 128])
                            rhs = ub.v(ub.ap[32 * pr:32 * pr + 32, k * NCH:(k + 1) * NCH])
                            mm(PS(bank, ri * NCH, (ri + 1) * NCH), lhs, rhs, k == 0, k == T - 1, tile_position=tp)

            c_load_x(0); c_load_y(0); c_xmm(0)
            for ut in range(8):
                SB = SBs[ut % 2]
                ub = R23b(ut)
                for pr in range(4):
                    gp = ut * 4 + pr
                    bank = 4 + pr
                    ts_ = tset[pcount % 2]; pcount += 1
                    P1 = ts_['P1']; P2 = ts_['P2']; BT = ts_['BT']; Q1 = ts_['Q1']; Q2 = ts_['Q2']
                    tb = TAB[pr * 768:(pr + 1) * 768]
                    tt(P1, PS(bank), tb[0:512], ALU.mult)
                    tt(P2, PS(bank), tb[256:768], ALU.mult)
                    tt(BT[0:256], P1[0:256], P1[256:512], ALU.add, 'pool')
                    tt(BT[256:512], P2[256:512], P2[0:256], ALU.subtract, 'pool')
                    dcol = dec[i * 32 + gp: i * 32 + gp + 1]
                    dbc = dcol.v(dcol.ap.to_broadcast([128, NCH]))
                    for ri in range(2):
                        S.op('dve', (lambda o, d0, d1: lambda e: e.tensor_tensor_scan(
                            out=o.ap, data0=d0.ap, data1=d1.ap, initial=0.0, op0=ALU.mult, op1=ALU.add))(
                            PS(bank, ri * NCH, (ri + 1) * NCH), dbc, BT[ri * 256:(ri + 1) * 256]),
                            _k(dbc, BT[ri * 256:(ri + 1) * 256]), PS(bank).keys)
                    tt(Q1, PS(bank), tb[0:512], ALU.mult)
                    tt(Q2, PS(bank), tb[256:768], ALU.mult)
                    sre = SB[(pr * 2 + 0) * SBW + 1:(pr * 2 + 0) * SBW + 1 + NCH]
                    sim = SB[(pr * 2 + 1) * SBW + 1:(pr * 2 + 1) * SBW + 1 + NCH]
                    tt(sre, Q1[0:256], Q1[256:512], ALU.subtract, 'pool')
                    tt(sim, Q2[0:256], Q2[256:512], ALU.add, 'pool')
                if ut + 1 < 8:
                    c_load_x(ut + 1)
                    c_xmm(ut + 1)
                for k in range(T):
                    bank = k // 2
                    c0 = (k % 2) * NCH
                    for kp in range(k + 1):
                        lhs = KBc[(k - kp) * 128:(k - kp + 1) * 128]
                        rhs = ub[kp * NCH:(kp + 1) * NCH]
                        mm(PS(bank, c0, c0 + NCH), lhs, rhs, kp == 0, False)
                    for pr in range(4):
                        for ri in range(2):
                            o_ = (k * 2 + ri) * 128 + 32 * pr
                            lhs = CSc[o_:o_ + 32]
                            rhs = SB[(pr * 2 + ri) * SBW:(pr * 2 + ri) * SBW + NCH]
                            tp = (0, 96) if pr == 3 else None
                            mm(PS(bank, c0, c0 + NCH, 32 * pr, 32 * pr + 32), lhs, rhs, False,
                               (ri == 1), tile_position=tp)
                psy = PSspan(0, 4)
                psy_perm = PSv(psy.ap.rearrange("p (k c) -> p c k", k=T), psy.keys)
                outv = ub.v(ub.ap.rearrange("p (c k) -> p c k", k=T))
                act(outv, psy_perm, AF.Gelu_apprx_tanh)
                if ut + 1 < 8:
                    c_load_y(ut + 1)
            SGf = TMPv(0, F32, 2048); TT = TMPv(8192, F32, 2048)
            for j in range(8):
                gg = nextg(); proj([wget()], lambda kt, c: R23b(kt, c * 512, (c + 1) * 512), 8, gg)
                act(SGf, PSspan(4 * gg, 4), AF.Sigmoid, bias=pcol(PV_BGLU + i * 16 + 8 + j))
                gv = nextg(); proj([wget()], lambda kt, c: R23b(kt, c * 512, (c + 1) * 512), 8, gv)
                stt(TT, PSspan(4 * gv, 4), pcol(PV_BGLU + i * 16 + j), SGf, ALU.add, ALU.mult)
                tt(R1b(j), TT, R23b(8 + j), ALU.mult)
            for o in range(8):
                g = nextg()
                proj([wget()], lambda kt, c: R1b(kt, c * 512, (c + 1) * 512), 8, g)
                residual_add(o, g)

        for l in layers:
            if l % 2 == 0:
                even_layer(l)
            else:
                odd_layer(l)

        if do_final:
            rmsnorm_to(PV_FINAL, lambda ft: HT(ft))
        ost = [R1v(0, F32, 1024), R1v(4096, F32, 1024)]
        for tt_ in range(16):
            b0 = (tt_ % 4) * 2
            for ft in range(8):
                tr(PS(b0 + ft // 4, (ft % 4) * 128, (ft % 4 + 1) * 128), HT(ft, tt_ * 128, (tt_ + 1) * 128))
            ob = ost[tt_ % 2]
            if tt_ % 2 == 0:
                act(ob, PSspan(b0, 2), AF.Copy)
            else:
                cp(ob, PSspan(b0, 2))
            dma('sp', out_d[tt_ * 128:(tt_ + 1) * 128, :], ob, ('o', tt_ % 2), out_is_dram=True, wkeys=[('out', tt_)])
        S.op('sp', lambda e: None, [('out', t_) for t_ in range(16)], [])
        assert wstate['next'] == len(wq), (wstate['next'], len(wq))
        S.emit(nc, st)
    return nc


def _tile_w(w, ncol_tiles=None):
    K, N = w.shape
    kb = K // 1024
    a = w.reshape(kb, 8, 128, N // 128, 128)
    a = a.transpose(3, 0, 2, 1, 4)
    return np.ascontiguousarray(a).reshape(N // 128, kb, 128, 1024)


def prep_inputs(inp):
    f = lambda a: np.asarray(a, dtype=np.float32)
    sc_w_in = f(inp["sc_w_in"]); sc_w_out = f(inp["sc_w_out"])
    ev_w_in = f(inp["ev_w_in"]); ev_w_out = f(inp["ev_w_out"])
    glu = f(inp["s5_w_glu"]); pw = f(inp["pool_w"])
    w_sc_in = np.zeros((2, 16, 4, 128, 1024), np.float32)
    w_sc_out = np.zeros((2, 8, 2, 128, 1024), np.float32)
    w_ev_in = np.zeros((2, 32, 128, 1024), np.float32)
    w_glu = np.zeros((2, 16, 128, 1024), np.float32)
    w_ev_out = np.zeros((2, 2, 8, 128, 1024), np.float32)
    w_pool = np.zeros((2, 2, 128, 1024), np.float32)
    order = (0, 2, 3, 1)
    for i in range(2):
        t = _tile_w(sc_w_in[i])[:, 0]
        for j in range(16):
            for w_, blk in enumerate(order):
                w_sc_in[i][j, w_] = t[blk * 16 + j]
        w_sc_out[i] = _tile_w(sc_w_out[i])
        w_ev_in[i] = _tile_w(ev_w_in[i])[:, 0]
        w_glu[i] = _tile_w(glu[i])[:, 0]
        t = _tile_w(ev_w_out[i])
        w_ev_out[i] = t.transpose(1, 0, 2, 3)
        a = pw[i].reshape(4, 2, 128, 2, 128)
        a = a.transpose(0, 3, 2, 1, 4).reshape(8, 128, 256)
        a = a.reshape(2, 4, 128, 256).transpose(0, 2, 1, 3).reshape(2, 128, 1024)
        w_pool[i] = a
    pvec = np.zeros((128, PV_N), np.float32)
    colT = lambda v: np.asarray(v, np.float32).reshape(-1, 128).T
    for l in range(4):
        pvec[:, PV_NORM + l * 8: PV_NORM + l * 8 + 8] = colT(inp["norm_g"][l])
    pvec[:, PV_FINAL:PV_FINAL + 8] = colT(inp["final_g"])
    for i in range(2):
        pvec[:, PV_S5D + i * 8: PV_S5D + i * 8 + 8] = colT(inp["s5_d"][i])
        pvec[:, PV_BGLU + i * 16: PV_BGLU + i * 16 + 16] = colT(inp["s5_b_glu"][i])
        pvec[:, PV_PSCALE + i * 8: PV_PSCALE + i * 8 + 8] = colT(inp["pool_scale"][i])
        for k in range(3):
            pvec[:, PV_CONVW + i * 48 + k * 16: PV_CONVW + i * 48 + k * 16 + 16] = colT(inp["sc_conv_w"][i][k])
        pvec[:, PV_CONVB + i * 16: PV_CONVB + i * 16 + 16] = colT(inp["sc_conv_b"][i])
    cst = np.zeros((128, C_N), np.float32)
    cst[:, C_IDENT:C_IDENT + 128] = np.eye(128, dtype=np.float32)
    r = np.arange(128)
    cst[:, C_BD32:C_BD32 + 128] = (r[:, None] // 32 == r[None, :] // 32)
    cst[:, C_MASKE + 0] = ((r // 16) % 2 == 0)
    cst[:, C_MASKE + 1] = ((r // 16) % 2 == 1)
    cst[:, C_MASKN + 0] = -cst[:, C_MASKE + 0]
    cst[:, C_MASKN + 1] = -cst[:, C_MASKE + 1]
    for wi, w_ in enumerate(POOL_W):
        cnt = np.minimum(np.arange(16) + 1, w_)
        cst[:, C_INVC + wi * 16: C_INVC + wi * 16 + 16] = np.float32(1.0) / cnt.astype(np.float32)
    s5q = np.zeros((2, 128, 1120), np.float32)
    s5h = np.zeros((2, 128, 2560), np.float32)
    for i in range(2):
        are = f(inp["s5_a_re"][i]); aim = f(inp["s5_a_im"][i]); ldt = f(inp["s5_log_dt"][i])
        bre = f(inp["s5_b_re"][i]); bim = f(inp["s5_b_im"][i])
        cre = f(inp["s5_c_re"][i]); cim = f(inp["s5_c_im"][i])
        qa = lambda a: a.reshape(32, 2, 64).transpose(1, 2, 0).reshape(128, 32)
        s5q[i, :, 0:32] = qa(are); s5q[i, :, 32:64] = qa(aim)
        s5q[i, :, 64:96] = qa(np.broadcast_to(ldt[:, None], (64, 64)))
        qb = lambda b: b.reshape(32, 2, 64, 16).transpose(1, 2, 0, 3).reshape(128, 512)
        s5q[i, :, 96:608] = qb(bre); s5q[i, :, 608:1120] = qb(bim)
        ha = lambda a: np.broadcast_to(a.reshape(8, 8, 1, 64), (8, 8, 16, 64)).transpose(1, 2, 0, 3).reshape(128, 512)
        s5h[i, :, 0:512] = ha(are); s5h[i, :, 512:1024] = ha(aim)
        s5h[i, :, 1024:1536] = ha(np.broadcast_to(ldt[:, None], (64, 64)))
        hc = lambda c: c.reshape(8, 8, 16, 64).transpose(1, 2, 0, 3).reshape(128, 512)
        s5h[i, :, 1536:2048] = hc(cre); s5h[i, :, 2048:2560] = hc(cim)
    shared = dict(pvec=pvec, cst=cst, s5q=s5q, s5h=s5h)
    for i in range(2):
        shared["w_sc_in%d" % i] = w_sc_in[i]; shared["w_sc_out%d" % i] = w_sc_out[i]
        shared["w_ev_in%d" % i] = w_ev_in[i]; shared["w_glu%d" % i] = w_glu[i]
        shared["w_ev_out%d" % i] = w_ev_out[i]; shared["w_pool%d" % i] = w_pool[i]
    return shared


def needed_keys(layers):
    ks = ["pvec", "cst", "s5q", "s5h"]
    for l in layers:
        i = l // 2
        if l % 2 == 1:
            ks += ["w_sc_in%d" % i, "w_sc_out%d" % i]
        else:
            ks += ["w_ev_in%d" % i, "w_glu%d" % i, "w_ev_out%d" % i, "w_pool%d" % i]
    return ks


_PROG = {}


def kernel(**inputs):
    x = np.asarray(inputs["x"], dtype=np.float32)
    shared = prep_inputs(inputs)
    key = "full"
    if key not in _PROG:
        _PROG[key] = build_program()
    nc = _PROG[key]
    in_maps = []
    for b in range(N_CORES):
        d = dict(shared)
        d["x"] = np.ascontiguousarray(x[b])
        in_maps.append(d)
    res = run_bass_kernel_spmd(nc, in_maps, core_ids=list(range(N_CORES)))
    out = np.stack([np.asarray(r["out"], dtype=np.float32) for r in res.results], axis=0)
    return out
```

```python
import contextlib
import numpy as np
import concourse.bass as bass
import concourse.mybir as mybir
from concourse.bass_utils import run_bass_kernel_spmd

F32 = mybir.dt.float32
BF16 = mybir.dt.bfloat16
AF = mybir.ActivationFunctionType
ALU = mybir.AluOpType

L_SEQ = 2048
T = 8
NCH = L_SEQ // T
POOL_W = (2, 4, 8, 16)
N_CORES = 8
STRICT_SAME_ENGINE = True

PV_NORM = 0
PV_FINAL = 32
PV_S5D = 40
PV_BGLU = 56
PV_PSCALE = 88
PV_CONVW = 104
PV_CONVB = 200
PV_N = 232
C_IDENT = 0
C_BD32 = 128
C_MASKE = 256
C_MASKN = 258
C_INVC = 260
C_N = 260 + 64


class Op:
    __slots__ = ('eng', 'fn', 'deps', 'sig', 'seq', 'dsem', 'dval', 'idx')

    def __init__(self, eng, fn):
        self.eng = eng; self.fn = fn; self.deps = []; self.sig = False
        self.seq = None; self.dsem = None; self.dval = None; self.idx = None


class Sched:
    ENGS = ('pe', 'act', 'dve', 'pool', 'sp')

    def __init__(self):
        self.ops = {e: [] for e in self.ENGS}
        self.lastw = {}
        self.readers = {}
        self.dcount = {}

    def _add_reader(self, key, o):
        d = self.readers.setdefault(key, {})
        if o.dsem is not None:
            d.setdefault('dma', []).append(o)
        else:
            d[o.eng] = o

    def op(self, eng, fn, reads=(), writes=(), dma=None):
        o = Op(eng, fn)
        if dma is not None:
            c = self.dcount.get(dma, 0) + 1
            self.dcount[dma] = c
            o.dsem = dma; o.dval = 16 * c
        deps = []
        for r in reads:
            w = self.lastw.get(r)
            if w is not None:
                deps.append((w, True))
        for k in writes:
            w = self.lastw.get(k)
            if w is not None:
                deps.append((w, False))
            rd = self.readers.get(k)
            if rd:
                for kk, v in rd.items():
                    if kk == 'dma':
                        deps.extend((x, False) for x in v)
                    else:
                        deps.append((v, False))
        best = {}
        dd = []
        for d, raw in deps:
            if d is o:
                continue
            if d.dsem is not None:
                if d not in dd:
                    dd.append(d)
            else:
                if eng == 'pe' and d.eng == 'pe' and o.dsem is None:
                    continue
                if (not STRICT_SAME_ENGINE) and d.eng == eng and o.dsem is None and not raw:
                    continue
                b = best.get(d.eng)
                if b is None or d.idx > b.idx:
                    best[d.eng] = d
        for d in best.values():
            d.sig = True
        o.deps = list(best.values()) + dd
        o.idx = len(self.ops[eng])
        self.ops[eng].append(o)
        for r in reads:
            self._add_reader(r, o)
        for k in writes:
            self.lastw[k] = o
            self.readers[k] = {}
        return o

    def emit(self, nc, stack):
        esem = {}
        for e in self.ENGS:
            esem[e] = stack.enter_context(nc.semaphore("es_" + e))
        dsem = {}
        for n, k in enumerate(self.dcount):
            dsem[k] = stack.enter_context(nc.semaphore("ds_%d" % n))
        for e in self.ENGS:
            c = 0
            for o in self.ops[e]:
                if o.dsem is None and o.sig:
                    c += 1; o.seq = c

        def run(e, engobj):
            waited = {}
            for o in self.ops[e]:
                for d in o.deps:
                    if d.dsem is not None:
                        key = ('d', d.dsem); sem = dsem[d.dsem]; val = d.dval
                    else:
                        key = ('e', d.eng); sem = esem[d.eng]; val = d.seq
                    if waited.get(key, 0) < val:
                        engobj.wait_ge(sem, val)
                        waited[key] = val
                inst = o.fn(engobj)
                if inst is None:
                    continue
                if o.dsem is not None:
                    inst.then_inc(dsem[o.dsem], 16)
                elif o.sig:
                    inst.then_inc(esem[e], 1)

        block = stack.enter_context(nc.Block())

        @block.tensor
        def _(t): run('pe', t)

        @block.scalar
        def _(t): run('act', t)

        @block.vector
        def _(t): run('dve', t)

        @block.gpsimd
        def _(t): run('pool', t)

        @block.sync
        def _(t): run('sp', t)


class Buf:
    def __init__(self, name, tens, off, dtype, n, ap=None):
        self.name = name; self.tens = tens; self.off = off; self.dtype = dtype; self.n = n
        self.esz = 4 if dtype == F32 else 2
        size = n * self.esz
        assert off % 4 == 0 and size % 4 == 0, (off, size)
        self.keys = [(name, b) for b in range(off // 1024, (off + size - 1) // 1024 + 1)]
        if ap is None:
            ap = tens[:, off // 4:(off + size) // 4]
            if dtype != F32:
                ap = ap.bitcast(dtype)
        self.ap = ap

    def __getitem__(self, s):
        a = 0 if s.start is None else s.start
        b = self.n if s.stop is None else s.stop
        assert s.step is None and 0 <= a < b <= self.n, (a, b, self.n)
        if (a * self.esz) % 4 == 0 and ((b - a) * self.esz) % 4 == 0:
            return Buf(self.name, self.tens, self.off + a * self.esz, self.dtype, b - a)
        o = Buf.__new__(Buf)
        o.name = self.name; o.tens = self.tens; o.off = self.off; o.dtype = self.dtype
        o.n = b - a; o.esz = self.esz
        lo = self.off + a * self.esz; hi = self.off + b * self.esz
        o.keys = [(self.name, k) for k in range(lo // 1024, (hi - 1) // 1024 + 1)]
        o.ap = self.ap[:, a:b]
        return o

    def v(self, ap):
        o = Buf.__new__(Buf)
        o.name = self.name; o.tens = self.tens; o.off = self.off; o.dtype = self.dtype
        o.n = self.n; o.esz = self.esz; o.keys = self.keys; o.ap = ap
        return o


class PSv:
    def __init__(self, ap, keys):
        self.ap = ap; self.keys = keys


def _k(*objs):
    out = []
    for o in objs:
        if o is None or isinstance(o, (int, float)):
            continue
        out.extend(o.keys)
    return out


def build_program(layers=(0, 1, 2, 3), do_final=True, n_setup=None):
    nc = bass.Bass("TRN2", target_bir_lowering=False)
    S = Sched()
    st = contextlib.ExitStack()

    def din(name, shape, dt=F32):
        return nc.dram_tensor(name, list(shape), dt, kind="ExternalInput").ap()

    x_d = din("x", [2048, 1024])
    need_i = sorted(set(l // 2 for l in layers))
    w_sc_in = {}; w_sc_out = {}; w_ev_in = {}; w_glu = {}; w_ev_out = {}; w_pool = {}
    for l in layers:
        i = l // 2
        if l % 2 == 1:
            w_sc_in[i] = din("w_sc_in%d" % i, [16, 4, 128, 1024])
            w_sc_out[i] = din("w_sc_out%d" % i, [8, 2, 128, 1024])
        else:
            w_ev_in[i] = din("w_ev_in%d" % i, [32, 128, 1024])
            w_glu[i] = din("w_glu%d" % i, [16, 128, 1024])
            w_ev_out[i] = din("w_ev_out%d" % i, [2, 8, 128, 1024])
            w_pool[i] = din("w_pool%d" % i, [2, 128, 1024])
    pvec_d = din("pvec", [128, PV_N])
    cst_d = din("cst", [128, C_N])
    s5q_d = din("s5q", [2, 128, 1120])
    s5h_d = din("s5h", [2, 128, 2560])
    out_d = nc.dram_tensor("out", [2048, 1024], F32, kind="ExternalOutput").ap()
    xb_s = nc.dram_tensor("xb_s", [2, 8, 128, T * 2 * 128], BF16, kind="Internal").ap()
    cs_s = nc.dram_tensor("cs_s", [2, 8, 128, T * 2 * 128], BF16, kind="Internal").ap()
    kb_s = nc.dram_tensor("kb_s", [2, 8, 128, T * 128], BF16, kind="Internal").ap()
    tab_s = nc.dram_tensor("tab_s", [2, 8, 128, 4 * 768], F32, kind="Internal").ap()

    with st:
        def sb(name, nwords):
            return st.enter_context(nc.sbuf_tensor(name, [128, nwords], F32))
        HT_t = sb("HT", 16384)
        R1_t = sb("R1", 8192)
        R23_t = sb("R23", 16384)
        WB_t = sb("WB", 4096)
        TMP_t = sb("TMP", 5184)
        MISC_t = sb("MISC", 1024)
        assert PV_N + C_N + 256 + 160 <= 1024
        ps_t = st.enter_context(nc.psum_tensor("ps", [128, 4096], F32))

        def HT(ft, c0=0, c1=2048):
            return Buf("HT", HT_t, (ft * 2048 + c0) * 4, F32, c1 - c0)

        def R1b(tile, c0=0, c1=2048):
            return Buf("R1", R1_t, (tile * 2048 + c0) * 2, BF16, c1 - c0)

        def R23b(tile, c0=0, c1=2048):
            return Buf("R23", R23_t, (tile * 2048 + c0) * 2, BF16, c1 - c0)

        def R1v(off, dtype, n): return Buf("R1", R1_t, off, dtype, n)
        def R23v(off, dtype, n): return Buf("R23", R23_t, off, dtype, n)
        def TMPv(off, dtype, n): return Buf("TMP", TMP_t, off, dtype, n)
        def MISCv(off, dtype, n): return Buf("MISC", MISC_t, off, dtype, n)

        def PS(bank, c0=0, c1=512, p0=0, p1=128):
            return PSv(ps_t[p0:p1, bank * 512 + c0: bank * 512 + c1], [('ps', bank)])

        def PSspan(b0, nb):
            return PSv(ps_t[:, b0 * 512:(b0 + nb) * 512], [('ps', b) for b in range(b0, b0 + nb)])

        pvec = MISCv(0, F32, PV_N)
        cst = MISCv(PV_N * 4, F32, C_N)
        dec = MISCv((PV_N + C_N) * 4, F32, 64)
        utp = MISCv((PV_N + C_N + 64) * 4, F32, 128)
        onesb = MISCv((PV_N + C_N + 192) * 4, BF16, 128)
        sm = MISCv((PV_N + C_N + 256) * 4, F32, 160)

        def pcol(c): return pvec[c:c + 1]
        ident = cst[C_IDENT:C_IDENT + 128]
        bd32 = cst[C_BD32:C_BD32 + 128]

        uniq = [0]

        def uchan():
            uniq[0] += 1
            return ('u', uniq[0])

        def dma(eng, out, in_ap_or_buf, chan, out_is_dram=False, rkeys=(), wkeys=()):
            if chan == 'par':
                chan = uchan()
            if out_is_dram:
                src = in_ap_or_buf
                S.op(eng, lambda e: e.dma_start(out=out, in_=src.ap), list(src.keys) + list(rkeys), list(wkeys), dma=chan)
            else:
                src = in_ap_or_buf
                S.op(eng, lambda e: e.dma_start(out=out.ap, in_=src), list(rkeys), list(out.keys) + list(wkeys), dma=chan)

        def act(out, in_, func, bias=None, scale=None):
            kw = {}
            rk = _k(in_, bias, scale)
            if bias is not None:
                kw['bias'] = bias if isinstance(bias, (int, float)) else bias.ap
            if scale is not None:
                kw['scale'] = scale if isinstance(scale, (int, float)) else scale.ap
            S.op('act', lambda e: e.activation(out=out.ap, in_=in_.ap, func=func, **kw), rk, out.keys)

        def tt(out, a, b, op, eng='dve'):
            S.op(eng, lambda e: e.tensor_tensor(out=out.ap, in0=a.ap, in1=b.ap, op=op), _k(a, b), out.keys)

        def stt(out, in0, scalar, in1, op0, op1, eng='dve'):
            sc = scalar if isinstance(scalar, (int, float)) else scalar.ap
            S.op(eng, lambda e: e.scalar_tensor_tensor(out=out.ap, in0=in0.ap, scalar=sc, in1=in1.ap, op0=op0, op1=op1),
                 _k(in0, scalar, in1), out.keys)

        def ts(out, in0, s1, s2, op0, op1=None, eng='dve'):
            a1 = s1 if isinstance(s1, (int, float)) else s1.ap
            a2 = None if s2 is None else (s2 if isinstance(s2, (int, float)) else s2.ap)
            kw = {} if op1 is None else {'op1': op1}
            S.op(eng, lambda e: e.tensor_scalar(out=out.ap, in0=in0.ap, scalar1=a1, scalar2=a2, op0=op0, **kw),
                 _k(in0, s1, s2), out.keys)

        def cp(out, in_, eng='dve'):
            S.op(eng, lambda e: e.tensor_copy(out=out.ap, in_=in_.ap), _k(in_), out.keys)

        def memset(out, val, eng='dve'):
            S.op(eng, lambda e: e.memset(out.ap, val), [], out.keys)

        def recip(out, in_):
            S.op('dve', lambda e: e.reciprocal(out=out.ap, in_=in_.ap), _k(in_), out.keys)

        def mm(out, lhsT, rhs, start, stop, tile_position=None):
            kw = {}
            if tile_position is not None:
                kw['tile_position'] = tile_position
            S.op('pe', lambda e: e.matmul(out.ap, lhsT=lhsT.ap, rhs=rhs.ap, start=start, stop=stop, **kw),
                 _k(lhsT, rhs), out.keys)

        def tr(out, in_):
            S.op('pe', lambda e: e.transpose(out.ap, in_.ap, ident.ap), _k(in_, ident), out.keys)

        NSLOT = 8
        wq = []
        wstate = {'issued': 0, 'next': 0}
        AHEAD = 6

        def wslot(n):
            return Buf("WB", WB_t, (n % NSLOT) * 2048, BF16, 1024)

        def w_issue_upto(n):
            while wstate['issued'] <= n and wstate['issued'] < len(wq):
                i = wstate['issued']
                dma('pool', wslot(i), wq[i], ('w', i % NSLOT))
                wstate['issued'] += 1

        def wget():
            n = wstate['next']
            wstate['next'] += 1
            w_issue_upto(n + AHEAD)
            return wslot(n)

        for l in layers:
            i = l // 2
            if l % 2 == 0:
                for j in range(8):
                    wq.append(w_ev_in[i][16 + j]); wq.append(w_ev_in[i][24 + j])
                for o in range(8):
                    wq.append(w_ev_out[i][1, o])
                for j in range(8):
                    wq.append(w_ev_in[i][j])
                for j in range(8):
                    wq.append(w_ev_in[i][8 + j])
                for j in range(8):
                    wq.append(w_glu[i][8 + j]); wq.append(w_glu[i][j])
                for o in range(8):
                    wq.append(w_ev_out[i][0, o])
            else:
                for j in range(16):
                    for w in range(4):
                        wq.append(w_sc_in[i][j, w])
                for o in range(8):
                    wq.append(w_sc_out[i][o, 0]); wq.append(w_sc_out[i][o, 1])

        dma('sp', pvec, pvec_d, 'par')
        dma('sp', cst, cst_d, 'par')
        memset(onesb, 1.0)
        w_issue_upto(AHEAD - 1)

        xs = [R1v(s_ * 16384, F32, 4096) for s_ in range(2)]
        for c in range(4):
            xsb = xs[c % 2]
            S.op('sp', (lambda xsb, c: lambda e: e.dma_start(
                out=xsb.ap.rearrange("p (t f) -> p t f", t=4),
                in_=x_d[c * 512:(c + 1) * 512, :].rearrange("(t p) f -> p t f", p=128)))(xsb, c),
                [], xsb.keys, dma=('x', c % 2))
            for ft in range(8):
                for tl in range(4):
                    tr(PS(ft, tl * 128, (tl + 1) * 128), xsb[tl * 1024 + ft * 128: tl * 1024 + (ft + 1) * 128])
                if ft % 2 == 0:
                    act(HT(ft, c * 512, (c + 1) * 512), PS(ft), AF.Copy)
                else:
                    cp(HT(ft, c * 512, (c + 1) * 512), PS(ft))

        def cmul2(o_re, o_im, a_re, a_im, b_re, b_im, t1, t2, t3, t4):
            tt(t1, a_re, b_re, ALU.mult); tt(t2, a_im, b_im, ALU.mult)
            tt(o_re, t1, t2, ALU.subtract)
            tt(t3, a_re, b_im, ALU.mult, 'pool'); tt(t4, a_im, b_re, ALU.mult, 'pool')
            tt(o_im, t3, t4, ALU.add, 'pool')

        def csq2(o_re, o_im, a_re, a_im, t1):
            tt(o_re, a_re, a_re, ALU.mult); tt(t1, a_im, a_im, ALU.mult); tt(o_re, o_re, t1, ALU.subtract)
            tt(o_im, a_re, a_im, ALU.mult, 'pool'); ts(o_im, o_im, 2.0, 0.0, ALU.mult, ALU.add, eng='pool')

        def discretise(lam_re, lam_im, logdt, n, alloc):
            dtv = alloc(n); act(dtv, logdt, AF.Exp)
            prd = alloc(n); tt(prd, lam_re, dtv, ALU.mult)
            mag = alloc(n); act(mag, prd, AF.Exp)
            ang = alloc(n); tt(ang, lam_im, dtv, ALU.mult)
            c = alloc(n); s_ = alloc(n); c2 = alloc(n); s2 = alloc(n); t1 = alloc(n); hp = alloc(n)
            ts(hp, ang, -1.0 / 8.0, float(np.pi / 2), ALU.mult, ALU.add)
            act(c, hp, AF.Sin)
            act(s_, ang, AF.Sin, scale=1.0 / 8.0)
            for _ in range(3):
                csq2(c2, s2, c, s_, t1)
                c, c2 = c2, c; s_, s2 = s2, s_
            lb_re = alloc(n); lb_im = alloc(n)
            tt(lb_re, mag, c, ALU.mult); tt(lb_im, mag, s_, ALU.mult, 'pool')
            return dict(mag=mag, c1=c, s1=s_, lb_re=lb_re, lb_im=lb_im, dtv=dtv, prd=prd)

        evac_rr = [0]

        def evac(dst, src):
            evac_rr[0] ^= 1
            if evac_rr[0]:
                act(dst, src, AF.Copy)
            else:
                cp(dst, src)

        def s5_setup(i):
            cur = [0]

            def alloc23(n, dtype=F32):
                b_ = R23v(cur[0], dtype, n); cur[0] += n * (4 if dtype == F32 else 2)
                assert cur[0] <= 65536, cur[0]
                return b_
            cs0 = alloc23(2048)
            hin = alloc23(2560)
            dma('sp', hin, s5h_d[i], 'par')
            lamre, lamim, ldt, cre, cim = [hin[k * 512:(k + 1) * 512] for k in range(5)]
            d = discretise(lamre, lamim, ldt, 512, alloc23)
            zb = [[alloc23(512), alloc23(512)], [alloc23(512), alloc23(512)]]
            t3 = alloc23(512); t4 = alloc23(512)
            assert cur[0] == 55296, cur[0]
            t1 = TMPv(16384, F32, 512); t2 = TMPv(18432, F32, 512)
            cste = [[TMPv(0, F32, 1024), TMPv(4096, F32, 1024)], [TMPv(8192, F32, 1024), TMPv(12288, F32, 1024)]]
            csst = R1v(0, BF16, 8 * T * 2 * 128)
            allocq = alloc23
            qin = allocq(1120)
            dma('sp', qin, s5q_d[i], 'par')
            qlre = qin[0:32]; qlim = qin[32:64]; qldt = qin[64:96]
            bqre = qin[96:608]; bqim = qin[608:1120]
            dq = discretise(qlre, qlim, qldt, 32, allocq)
            den = allocq(32); q1 = allocq(32); q2 = allocq(32); fre = allocq(32); fim = allocq(32)
            lm1 = allocq(32)
            tt(q1, qlre, qlre, ALU.mult); tt(q2, qlim, qlim, ALU.mult); tt(den, q1, q2, ALU.add)
            recip(den, den)
            ts(lm1, dq['lb_re'], -1.0, None, ALU.add)
            tt(q1, lm1, qlre, ALU.mult); tt(q2, dq['lb_im'], qlim, ALU.mult); tt(q1, q1, q2, ALU.add)
            tt(fre, q1, den, ALU.mult)
            tt(q1, dq['lb_im'], qlre, ALU.mult); tt(q2, lm1, qlim, ALU.mult); tt(q1, q1, q2, ALU.subtract)
            tt(fim, q1, den, ALU.mult)
            act(dec[i * 32:(i + 1) * 32], dq['prd'], AF.Exp, scale=float(T))
            uc = allocq(32); us = allocq(32); uc2 = allocq(32); us2 = allocq(32)
            cp(uc, dq['c1']); cp(us, dq['s1'])
            nsq = {8: 3, 16: 4}[T]
            for _ in range(nsq):
                csq2(uc2, us2, uc, us, q1)
                uc, uc2 = uc2, uc; us, us2 = us2, us
            cp(utp[i * 64:i * 64 + 32], uc); cp(utp[i * 64 + 32:i * 64 + 64], us)
            z = [cre, cim]
            for k in range(-1, T):
                if k >= 0:
                    zn = zb[k % 2]
                    cmul2(zn[0], zn[1], z[0], z[1], d['lb_re'], d['lb_im'], t1, t2, t3, t4)
                    z = zn
                ce = cste[(k + 1) % 2]
                for ri in range(2):
                    for e_ in range(2):
                        outv = ce[ri].v(ce[ri].ap.rearrange("p (u e q) -> p u e q", u=8, e=2)[:, :, e_, :])
                        inv = z[ri].v(z[ri].ap.rearrange("p (u q) -> p u q", u=8))
                        mc = (C_MASKE if ri == 0 else C_MASKN) + e_
                        act(outv, inv, AF.Identity, scale=cst[mc:mc + 1])
                for ri in range(2):
                    for hf in range(2):
                        bank = ((k + 1) % 2) * 4 + ri * 2 + hf
                        for u4 in range(4):
                            ut = hf * 4 + u4
                            tr(PS(bank, u4 * 128, (u4 + 1) * 128), ce[ri][ut * 128:(ut + 1) * 128])
                        psv = PSv(PS(bank).ap.rearrange("p (a b) -> p a b", a=4), PS(bank).keys)
                        if k == -1:
                            dst = cs0.v(cs0.ap.rearrange("p (u f) -> p u f", u=8)[:, hf * 4:(hf + 1) * 4, ri * 128:(ri + 1) * 128])
                        else:
                            o_ = (k * 2 + ri) * 128
                            dst = csst.v(csst.ap.rearrange("p (u f) -> p u f", u=8)[:, hf * 4:(hf + 1) * 4, o_:o_ + 128])
                        evac(dst, psv)
            S.op('sp', lambda e: e.dma_start(out=cs_s[i].rearrange("u p f -> p u f"),
                                             in_=csst.ap.rearrange("p (u f) -> p u f", u=8)),
                 csst.keys, [('cs_s', i)], dma=uchan())

            cur[0] = 8192

            def bc16(b32):
                return b32.v(b32.ap.rearrange("p (g o) -> p g o", o=1).to_broadcast([128, 32, 16]))

            def v3(b512):
                return b512.v(b512.ap.rearrange("p (g h) -> p g h", h=16))
            bb = [[allocq(512), allocq(512)], [allocq(512), allocq(512)]]
            bste = [[allocq(1024), allocq(1024)], [allocq(1024), allocq(1024)]]
            kst = allocq(8 * T * 128, BF16)
            assert cur[0] <= 55296, cur[0]
            b1 = TMPv(0, F32, 512); b2 = TMPv(2048, F32, 512); b3 = TMPv(4096, F32, 512); b4 = TMPv(6144, F32, 512)
            ktmp = TMPv(8192, F32, 512)
            cmul2(v3(bb[0][0]), v3(bb[0][1]), bc16(fre), bc16(fim), v3(bqre), v3(bqim), v3(b1), v3(b2), v3(b3), v3(b4))
            xbst = R1v(0, BF16, 8 * T * 2 * 128)
            memset(bste[0][0], 0.0); memset(bste[0][1], 0.0, eng='pool')
            memset(bste[1][0], 0.0); memset(bste[1][1], 0.0, eng='pool')
            bd4 = bd32.v(bd32.ap.rearrange("p (o c) -> p o c", o=1).to_broadcast([128, 4, 128]))
            curb = bb[0]
            for m in range(T):
                if m >= 1:
                    nb = bb[m % 2]
                    cmul2(v3(nb[0]), v3(nb[1]), bc16(dq['lb_re']), bc16(dq['lb_im']), v3(curb[0]), v3(curb[1]),
                          v3(b1), v3(b2), v3(b3), v3(b4))
                    curb = nb
                be = bste[m % 2]
                for ri in range(2):
                    for e_ in range(2):
                        w0 = be[ri].off // 4
                        dstap = be[ri].tens[e_ * 64:(e_ + 1) * 64, w0:w0 + 1024] \
                            .rearrange("p (g e h) -> p g e h", e=2, h=16)[:, :, e_, :]
                        w1 = curb[ri].off // 4
                        srcap = curb[ri].tens[e_ * 64:(e_ + 1) * 64, w1:w1 + 512] \
                            .rearrange("p (g h) -> p g h", h=16)
                        S.op('act', (lambda da, sa: lambda e: e.activation(out=da, in_=sa, func=AF.Copy))(dstap, srcap),
                             curb[ri].keys, be[ri].keys)
                k = T - 1 - m
                for ri in range(2):
                    for hf in range(2):
                        bank = ri * 2 + hf
                        for u4 in range(4):
                            ut = hf * 4 + u4
                            tr(PS(bank, u4 * 128, (u4 + 1) * 128), be[ri][ut * 128:(ut + 1) * 128])
                        psv = PSv(PS(bank).ap.rearrange("p (a b) -> p a b", a=4), PS(bank).keys)
                        o_ = (k * 2 + ri) * 128
                        dst = xbst.v(xbst.ap.rearrange("p (u f) -> p u f", u=8)[:, hf * 4:(hf + 1) * 4, o_:o_ + 128])
                        evac(dst, psv)
                for hf in range(2):
                    kb = 4 + (m % 2) * 2 + hf
                    for u4 in range(4):
                        ut = hf * 4 + u4
                        for ri in range(2):
                            mm(PS(kb, u4 * 128, (u4 + 1) * 128), be[ri][ut * 128:(ut + 1) * 128],
                               cs0[(ut * 2 + ri) * 128:(ut * 2 + ri + 1) * 128], ri == 0, ri == 1)
                    psv = PSv(PS(kb).ap.rearrange("p (a b) -> p a b", a=4), PS(kb).keys)
                    if m == 0:
                        kt4 = ktmp.v(ktmp.ap.rearrange("p (a b) -> p a b", a=4))
                        tt(kt4, psv, bd4, ALU.mult)
                        for u4 in range(4):
                            ut = hf * 4 + u4
                            ko = (ut * T + m) * 128
                            stt(kst[ko:ko + 128], ident, pcol(PV_S5D + i * 8 + ut), ktmp[u4 * 128:(u4 + 1) * 128],
                                ALU.mult, ALU.add)
                    else:
                        dst = kst.v(kst.ap.rearrange("p (u f) -> p u f", u=8)[:, hf * 4:(hf + 1) * 4, m * 128:(m + 1) * 128])
                        tt(dst, psv, bd4, ALU.mult)
            S.op('sp', lambda e: e.dma_start(out=xb_s[i].rearrange("u p f -> p u f"),
                                             in_=xbst.ap.rearrange("p (u f) -> p u f", u=8)),
                 xbst.keys, [('xb_s', i)], dma=uchan())
            S.op('sp', lambda e: e.dma_start(out=kb_s[i].rearrange("u p f -> p u f"),
                                             in_=kst.ap.rearrange("p (u f) -> p u f", u=8)),
                 kst.keys, [('kb_s', i)], dma=uchan())

            tabc = R1v(0, F32, 32 * NCH)
            tabs = R23v(0, F32, 32 * NCH)
            twd = [R23v(32768, F32, 2048), R23v(40960, F32, 2048)]
            twp = [R23v(49152, F32, 2048), R23v(57344, F32, 2048)]
            pn = [sm[0:32], sm[32:64]]; pn2 = [sm[64:96], sm[96:128]]; sq1 = sm[128:160]

            def t3(b_, g0, g1, n0, n1):
                return b_.v(b_.ap.rearrange("p (g c) -> p g c", c=NCH)[:, g0:g1, n0:n1])

            def bcn(b32, g0, g1, n):
                return b32.v(b32.ap.rearrange("p (g o) -> p g o", o=1)[:, g0:g1, :].to_broadcast([128, g1 - g0, n]))
            Bc = TMPv(0, F32, 512); Bs = TMPv(2048, F32, 512); Ac = TMPv(4096, F32, 512); As = TMPv(6144, F32, 512)

            def tiny(k_):
                return TMPv(8192 + 128 * k_, F32, 32)
            powB = [[tiny(0), tiny(1)], [tiny(2), tiny(3)], [tiny(4), tiny(5)], [tiny(6), tiny(7)]]
            powA = [[tiny(8), tiny(9)], [tiny(10), tiny(11)], [tiny(12), tiny(13)], [tiny(14), tiny(15)]]
            sq1 = tiny(16)
            cp(powB[0][0], utp[i * 64:i * 64 + 32]); cp(powB[0][1], utp[i * 64 + 32:i * 64 + 64])
            for j_ in range(3):
                csq2(powB[j_ + 1][0], powB[j_ + 1][1], powB[j_][0], powB[j_][1], sq1)
            csq2(powA[0][0], powA[0][1], powB[3][0], powB[3][1], sq1)
            for j_ in range(3):
                csq2(powA[j_ + 1][0], powA[j_ + 1][1], powA[j_][0], powA[j_][1], sq1)

            def small_table(tc, ts_, pows):
                def s3(b_, n0, n1):
                    return b_.v(b_.ap.rearrange("p (g c) -> p g c", c=16)[:, :, n0:n1])
                memset(s3(tc, 0, 1), 1.0); memset(s3(ts_, 0, 1), 0.0)
                n_ = 1; lv = 0
                while n_ < 16:
                    pr_, pi_ = pows[lv]
                    w1 = twd[0][0:32 * n_]; w2 = twd[1][0:32 * n_]
                    w1v = w1.v(w1.ap.rearrange("p (g c) -> p g c", c=n_)); w2v = w2.v(w2.ap.rearrange("p (g c) -> p g c", c=n_))
                    tt(w1v, s3(tc, 0, n_), bcn(pr_, 0, 32, n_), ALU.mult); tt(w2v, s3(ts_, 0, n_), bcn(pi_, 0, 32, n_), ALU.mult)
                    tt(s3(tc, n_, 2 * n_), w1v, w2v, ALU.subtract)
                    tt(w1v, s3(tc, 0, n_), bcn(pi_, 0, 32, n_), ALU.mult); tt(w2v, s3(ts_, 0, n_), bcn(pr_, 0, 32, n_), ALU.mult)
                    tt(s3(ts_, n_, 2 * n_), w1v, w2v, ALU.add)
                    n_ *= 2; lv += 1
            small_table(Bc, Bs, powB)
            small_table(Ac, As, powA)
            act(PS(0), Ac, AF.Copy); act(PS(1), As, AF.Copy)
            for g0 in range(0, 32, 8):
                g1 = g0 + 8

                def abro(psb):
                    return PSv(psb.ap.rearrange("p (g a o) -> p g a o", a=16, o=1)[:, g0:g1, :, :].to_broadcast([128, 8, 16, 16]), psb.keys)

                def bbro(bt_):
                    return bt_.v(bt_.ap.rearrange("p (g o b) -> p g o b", o=1, b=16)[:, g0:g1, :, :].to_broadcast([128, 8, 16, 16]))

                def w4(b_):
                    return b_.v(b_.ap.rearrange("p (g a b) -> p g a b", a=16, b=16))

                def o4(tb_):
                    return tb_.v(tb_.ap.rearrange("p (g a b) -> p g a b", a=16, b=16)[:, g0:g1, :, :])
                tt(w4(twd[0]), abro(PS(0)), bbro(Bc), ALU.mult)
                tt(w4(twd[1]), abro(PS(1)), bbro(Bs), ALU.mult)
                tt(o4(tabc), w4(twd[0]), w4(twd[1]), ALU.subtract, 'pool')
                tt(w4(twp[0]), abro(PS(0)), bbro(Bs), ALU.mult)
                tt(w4(twp[1]), abro(PS(1)), bbro(Bc), ALU.mult)
                tt(o4(tabs), w4(twp[0]), w4(twp[1]), ALU.add, 'pool')
            for (src, offs) in ((tabc, (0, 512)), (tabs, (256,))):
                for o_ in offs:
                    for ut in range(8):
                        S.op('sp', (lambda src, o_, ut: lambda e: e.dma_start(
                            out=tab_s[i, ut].rearrange("p (r w) -> p r w", w=768)[:, :, o_:o_ + NCH],
                            in_=src.ap.rearrange("p (g c) -> p g c", c=NCH)[:, ut * 4:(ut + 1) * 4, :]))(src, o_, ut),
                            src.keys, [('tab_s', i, o_, ut)], dma=uchan())

        setup_layers = [l // 2 for l in layers if l % 2 == 0]
        for i in setup_layers:
            s5_setup(i)

        SQSUM = TMPv(0, F32, 2048); SQT = TMPv(8192, F32, 2048); SQB = TMPv(16384, BF16, 2048)

        def tail_stats(o):
            if o == 0:
                act(SQSUM, HT(0), AF.Square)
            else:
                act(SQT, HT(o), AF.Square)
                tt(SQB if o == 7 else SQSUM, SQSUM, SQT, ALU.add)

        def rmsnorm_to(gcol0, dst_fn, pre=False):
            if pre:
                for c in range(4):
                    mm(PS(c), onesb, SQB[c * 512:(c + 1) * 512], True, True)
            else:
                sq = [TMPv(0, BF16, 2048), TMPv(4096, BF16, 2048)]
                for ft in range(8):
                    sqb = sq[ft % 2]
                    act(sqb, HT(ft), AF.Square)
                    for c in range(4):
                        mm(PS(c), onesb, sqb[c * 512:(c + 1) * 512], ft == 0, ft == 7)
            rs = TMPv(8192, F32, 2048)
            act(rs, PSspan(0, 4), AF.Ln, bias=1e-6, scale=1.0 / 1024.0)
            act(PSspan(4, 4), rs, AF.Exp, scale=-0.5)
            for ft in range(8):
                stt(dst_fn(ft), HT(ft), pcol(gcol0 + ft), PSspan(4, 4), ALU.mult, ALU.mult)

        def proj(wt, rhs_fn, nk, g):
            for kt in range(nk):
                wbuf = wt[kt // 8]
                lhs = wbuf[(kt % 8) * 128:(kt % 8 + 1) * 128]
                for c in range(4):
                    mm(PS(4 * g + c), lhs, rhs_fn(kt, c), kt == 0, kt == nk - 1)

        grp = [0]

        def nextg():
            g = grp[0]; grp[0] ^= 1
            return g

        def hn(kt, c): return R1b(kt, c * 512, (c + 1) * 512)

        def residual_add(o, g):
            tt(HT(o), PSspan(4 * g, 4), HT(o), ALU.add)

        def odd_layer(l, pre):
            i = l // 2
            rmsnorm_to(PV_NORM + l * 8, lambda ft: R1b(ft), pre=pre)
            V = TMPv(0, F32, 2064)
            ACC = TMPv(8256, F32, 2048)
            SG = TMPv(8256 + 8192, BF16, 2048)
            memset(V[0:16], 0.0)
            Vd = V[16:2064]
            for j in range(16):
                cw = [pcol(PV_CONVW + i * 48 + k * 16 + j) for k in range(3)]
                cb = pcol(PV_CONVB + i * 16 + j)
                g = nextg(); proj([wget()], hn, 8, g)
                act(Vd, PSspan(4 * g, 4), AF.Copy)
                g = nextg(); proj([wget()], hn, 8, g)
                tt(Vd, PSspan(4 * g, 4), Vd, ALU.mult)
                act(ACC, Vd, AF.Identity, bias=cb, scale=cw[2])
                stt(ACC, V[15:2063], cw[1], ACC, ALU.mult, ALU.add)
                stt(ACC, V[14:2062], cw[0], ACC, ALU.mult, ALU.add)
                g = nextg(); proj([wget()], hn, 8, g)
                act(SG, PSspan(4 * g, 4), AF.Silu)
                g = nextg(); proj([wget()], hn, 8, g)
                tt(ACC, PSspan(4 * g, 4), ACC, ALU.mult)
                tt(R23b(j), ACC, SG, ALU.mult)
            for o in range(8):
                g = nextg()
                proj([wget(), wget()], lambda kt, c: R23b(kt, c * 512, (c + 1) * 512), 16, g)
                residual_add(o, g)
                tail_stats(o)

        def even_layer(l, pre):
            i = l // 2
            rmsnorm_to(PV_NORM + l * 8, lambda ft: R1b(ft), pre=pre)
            R3o = 32768
            UB = R23v(R3o + 0, F32, 2064)
            WA = R23v(R3o + 9216, F32, 2064)
            WBf = R23v(R3o + 18432, F32, 2064)
            DFB = [TMPv(0, BF16, 2048), TMPv(4096, BF16, 2048)]
            SGB = [TMPv(8192, BF16, 2048), TMPv(12288, BF16, 2048)]
            T16 = TMPv(16384, F32, 16)
            memset(UB[0:16], 0.0); memset(WA[0:16], 0.0); memset(WBf[0:16], 0.0)
            PW = [R23v(R3o + 27648, BF16, 1024), R23v(R3o + 27648 + 2048, BF16, 1024)]
            for hh in range(2):
                dma('pool', PW[hh], w_pool[i][hh], ('pw', hh))
            for j in range(8):
                gi = j // 2; w = POOL_W[gi]
                g = nextg(); proj([wget()], hn, 8, g)
                act(UB[16:2064], PSspan(4 * g, 4), AF.Copy)
                bufs = [UB, WA, WBf]
                src = UB; sh = 1; nadd = {2: 1, 4: 2, 8: 3, 16: 4}[w]
                dsts = [WA, WBf]
                for a in range(nadd):
                    dst = dsts[a % 2]
                    tt(dst[16:2064], src[16:2064], src[16 - sh:2064 - sh], ALU.add, 'dve' if a in (0, 3) else 'pool')
                    src = dst; sh *= 2
                stt(DFB[j % 2], src[16:2064], 1.0 / w, UB[16:2064], ALU.mult, ALU.subtract)
                tt(T16, src[16:32], cst[C_INVC + gi * 16: C_INVC + gi * 16 + 16], ALU.mult)
                tt(DFB[j % 2][0:16], T16, UB[16:32], ALU.subtract)
                g = nextg(); proj([wget()], hn, 8, g)
                act(SGB[j % 2], PSspan(4 * g, 4), AF.Silu)
                if j % 2 == 1:
                    for dt_ in range(2):
                        g = nextg()
                        pwb = PW[gi // 2]
                        base = ((gi % 2) * 2 + dt_) * 256
                        for kt in range(2):
                            for c in range(4):
                                mm(PS(4 * g + c), pwb[base + kt * 128: base + (kt + 1) * 128],
                                   DFB[kt][c * 512:(c + 1) * 512], kt == 0, kt == 1)
                        o_t = gi * 2 + dt_
                        stt(R23b(o_t), PSspan(4 * g, 4), pcol(PV_PSCALE + i * 8 + o_t), SGB[dt_], ALU.mult, ALU.mult)
            for o in range(8):
                g = nextg()
                proj([wget()], lambda kt, c: R23b(kt, c * 512, (c + 1) * 512), 8, g)
                residual_add(o, g)
            for j in range(8):
                g = nextg(); proj([wget()], hn, 8, g)
                ubj = R23b(j)
                psn = PSspan(4 * g, 4)
                act(ubj.v(ubj.ap.rearrange("p (k c) -> p c k", k=T)),
                    PSv(psn.ap.rearrange("p (c k) -> p c k", k=T), psn.keys), AF.Copy)
            for j in range(8):
                g = nextg(); proj([wget()], hn, 8, g)
                act(R23b(8 + j), PSspan(4 * g, 4), AF.Silu)
            tset = [dict(P1=TMPv(0, F32, 512), P2=TMPv(2048, F32, 512), BT=TMPv(4096, F32, 512),
                         Q1=TMPv(6144, F32, 512), Q2=TMPv(8192, F32, 512)),
                    dict(P1=R1v(22528, F32, 512), P2=R1v(24576, F32, 512), BT=R1v(26624, F32, 512),
                         Q1=R1v(28672, F32, 512), Q2=R1v(30720, F32, 512))]
            SBW = 258
            SBs = [TMPv(12288, BF16, 4 * 2 * SBW), TMPv(12288 + 4 * 2 * SBW * 2, BF16, 4 * 2 * SBW)]
            memset(SBs[0], 0.0); memset(SBs[1], 0.0)
            XBc = R1v(0, BF16, T * 2 * 128)
            CSc = R1v(4096, BF16, T * 2 * 128)
            KBc = R1v(8192, BF16, T * 128)
            TAB = R1v(10240, F32, 4 * 768)
            pcount = 0

            def c_load_x(ut):
                dma('sp', XBc, xb_s[i, ut], ('c5', 0), rkeys=[('xb_s', i)])
                for pr in range(4):
                    dma('sp', TAB[pr * 768:(pr + 1) * 768], tab_s[i, ut][:, pr * 768:(pr + 1) * 768], ('c5', 3, pr),
                        rkeys=[('tab_s', i, 0, ut), ('tab_s', i, 256, ut), ('tab_s', i, 512, ut)])

            def c_load_y(ut):
                dma('sp', CSc, cs_s[i, ut], ('c5', 1), rkeys=[('cs_s', i)])
                dma('sp', KBc, kb_s[i, ut], ('c5', 2), rkeys=[('kb_s', i)])

            def c_xmm(ut):
                ub = R23b(ut)
                for ri in range(2):
                    for k in range(T):
                        for pr in range(4):
                            bank = 4 + pr
                            tp = (96, 0) if pr == 3 else None
                            o_ = (k * 2 + ri) * 128
                            lhs = XBc.v(XBc.ap[32 * pr:32 * pr + 32, o_:o_ + 128])
                            rhs = ub.v(ub.ap[32 * pr:32 * pr + 32, k * NCH:(k + 1) * NCH])
                            mm(PS(bank, ri * NCH, (ri + 1) * NCH), lhs, rhs, k == 0, k == T - 1, tile_position=tp)

            c_load_x(0); c_load_y(0); c_xmm(0)
            for ut in range(8):
                SB = SBs[ut % 2]
                ub = R23b(ut)
                for pr in range(4):
                    gp = ut * 4 + pr
                    bank = 4 + pr
                    ts_ = tset[pcount % 2]; pcount += 1
                    P1 = ts_['P1']; P2 = ts_['P2']; BT = ts_['BT']; Q1 = ts_['Q1']; Q2 = ts_['Q2']
                    tb = TAB[pr * 768:(pr + 1) * 768]
                    tt(P1, PS(bank), tb[0:512], ALU.mult)
                    tt(P2, PS(bank), tb[256:768], ALU.mult)
                    tt(BT[0:256], P1[0:256], P1[256:512], ALU.add, 'pool')
                    tt(BT[256:512], P2[256:512], P2[0:256], ALU.subtract, 'pool')
                    dcol = dec[i * 32 + gp: i * 32 + gp + 1]
                    dbc = dcol.v(dcol.ap.to_broadcast([128, NCH]))
                    for ri in range(2):
                        S.op('dve', (lambda o, d0, d1: lambda e: e.tensor_tensor_scan(
                            out=o.ap, data0=d0.ap, data1=d1.ap, initial=0.0, op0=ALU.mult, op1=ALU.add))(
                            PS(bank, ri * NCH, (ri + 1) * NCH), dbc, BT[ri * 256:(ri + 1) * 256]),
                            _k(dbc, BT[ri * 256:(ri + 1) * 256]), PS(bank).keys)
                    tt(Q1, PS(bank), tb[0:512], ALU.mult)
                    tt(Q2, PS(bank), tb[256:768], ALU.mult)
                    sre = SB[(pr * 2 + 0) * SBW + 1:(pr * 2 + 0) * SBW + 1 + NCH]
                    sim = SB[(pr * 2 + 1) * SBW + 1:(pr * 2 + 1) * SBW + 1 + NCH]
                    tt(sre, Q1[0:256], Q1[256:512], ALU.subtract, 'pool')
                    tt(sim, Q2[0:256], Q2[256:512], ALU.add, 'pool')
                if ut + 1 < 8:
                    c_load_x(ut + 1)
                    c_xmm(ut + 1)
                for k in range(T):
                    bank = k // 2
                    c0 = (k % 2) * NCH
                    for kp in range(k + 1):
                        lhs = KBc[(k - kp) * 128:(k - kp + 1) * 128]
                        rhs = ub[kp * NCH:(kp + 1) * NCH]
                        mm(PS(bank, c0, c0 + NCH), lhs, rhs, kp == 0, False)
                    for pr in range(4):
                        for ri in range(2):
                            o_ = (k * 2 + ri) * 128 + 32 * pr
                            lhs = CSc[o_:o_ + 32]
                            rhs = SB[(pr * 2 + ri) * SBW:(pr * 2 + ri) * SBW + NCH]
                            tp = (0, 96) if pr == 3 else None
                            mm(PS(bank, c0, c0 + NCH, 32 * pr, 32 * pr + 32), lhs, rhs, False,
                               (ri == 1), tile_position=tp)
                psy = PSspan(0, 4)
                psy_perm = PSv(psy.ap.rearrange("p (k c) -> p c k", k=T), psy.keys)
                outv = ub.v(ub.ap.rearrange("p (c k) -> p c k", k=T))
                act(outv, psy_perm, AF.Gelu_apprx_tanh)
                if ut + 1 < 8:
                    c_load_y(ut + 1)
            SGf = TMPv(0, F32, 2048); TT = TMPv(8192, F32, 2048)
            for j in range(8):
                gg = nextg(); proj([wget()], lambda kt, c: R23b(kt, c * 512, (c + 1) * 512), 8, gg)
                act(SGf, PSspan(4 * gg, 4), AF.Sigmoid, bias=pcol(PV_BGLU + i * 16 + 8 + j))
                gv = nextg(); proj([wget()], lambda kt, c: R23b(kt, c * 512, (c + 1) * 512), 8, gv)
                stt(TT, PSspan(4 * gv, 4), pcol(PV_BGLU + i * 16 + j), SGf, ALU.add, ALU.mult)
                tt(R1b(j), TT, R23b(8 + j), ALU.mult)
            for o in range(8):
                g = nextg()
                proj([wget()], lambda kt, c: R1b(kt, c * 512, (c + 1) * 512), 8, g)
                residual_add(o, g)
                tail_stats(o)

        for n_, l in enumerate(layers):
            if l % 2 == 0:
                even_layer(l, pre=(n_ > 0))
            else:
                odd_layer(l, pre=(n_ > 0))

        if do_final:
            rmsnorm_to(PV_FINAL, lambda ft: HT(ft), pre=(len(layers) > 0))
        ost = [R1v(0, F32, 1024), R1v(4096, F32, 1024)]
        for tt_ in range(16):
            b0 = (tt_ % 4) * 2
            for ft in range(8):
                tr(PS(b0 + ft // 4, (ft % 4) * 128, (ft % 4 + 1) * 128), HT(ft, tt_ * 128, (tt_ + 1) * 128))
            ob = ost[tt_ % 2]
            if tt_ % 2 == 0:
                act(ob, PSspan(b0, 2), AF.Copy)
            else:
                cp(ob, PSspan(b0, 2))
            dma('sp', out_d[tt_ * 128:(tt_ + 1) * 128, :], ob, ('o', tt_ % 2), out_is_dram=True, wkeys=[('out', tt_)])
        S.op('sp', lambda e: None, [('out', t_) for t_ in range(16)], [])
        assert wstate['next'] == len(wq), (wstate['next'], len(wq))
        S.emit(nc, st)
    return nc


def _tile_w(w, ncol_tiles=None):
    K, N = w.shape
    kb = K // 1024
    a = w.reshape(kb, 8, 128, N // 128, 128)
    a = a.transpose(3, 0, 2, 1, 4)
    return np.ascontiguousarray(a).reshape(N // 128, kb, 128, 1024)


def prep_inputs(inp):
    f = lambda a: np.asarray(a, dtype=np.float32)
    sc_w_in = f(inp["sc_w_in"]); sc_w_out = f(inp["sc_w_out"])
    ev_w_in = f(inp["ev_w_in"]); ev_w_out = f(inp["ev_w_out"])
    glu = f(inp["s5_w_glu"]); pw = f(inp["pool_w"])
    w_sc_in = np.zeros((2, 16, 4, 128, 1024), np.float32)
    w_sc_out = np.zeros((2, 8, 2, 128, 1024), np.float32)
    w_ev_in = np.zeros((2, 32, 128, 1024), np.float32)
    w_glu = np.zeros((2, 16, 128, 1024), np.float32)
    w_ev_out = np.zeros((2, 2, 8, 128, 1024), np.float32)
    w_pool = np.zeros((2, 2, 128, 1024), np.float32)
    order = (0, 2, 3, 1)
    for i in range(2):
        t = _tile_w(sc_w_in[i])[:, 0]
        for j in range(16):
            for w_, blk in enumerate(order):
                w_sc_in[i][j, w_] = t[blk * 16 + j]
        w_sc_out[i] = _tile_w(sc_w_out[i])
        w_ev_in[i] = _tile_w(ev_w_in[i])[:, 0]
        w_glu[i] = _tile_w(glu[i])[:, 0]
        t = _tile_w(ev_w_out[i])
        w_ev_out[i] = t.transpose(1, 0, 2, 3)
        a = pw[i].reshape(4, 2, 128, 2, 128)
        a = a.transpose(0, 3, 2, 1, 4).reshape(8, 128, 256)
        a = a.reshape(2, 4, 128, 256).transpose(0, 2, 1, 3).reshape(2, 128, 1024)
        w_pool[i] = a
    pvec = np.zeros((128, PV_N), np.float32)
    colT = lambda v: np.asarray(v, np.float32).reshape(-1, 128).T
    for l in range(4):
        pvec[:, PV_NORM + l * 8: PV_NORM + l * 8 + 8] = colT(inp["norm_g"][l])
    pvec[:, PV_FINAL:PV_FINAL + 8] = colT(inp["final_g"])
    for i in range(2):
        pvec[:, PV_S5D + i * 8: PV_S5D + i * 8 + 8] = colT(inp["s5_d"][i])
        pvec[:, PV_BGLU + i * 16: PV_BGLU + i * 16 + 16] = colT(inp["s5_b_glu"][i])
        pvec[:, PV_PSCALE + i * 8: PV_PSCALE + i * 8 + 8] = colT(inp["pool_scale"][i])
        for k in range(3):
            pvec[:, PV_CONVW + i * 48 + k * 16: PV_CONVW + i * 48 + k * 16 + 16] = colT(inp["sc_conv_w"][i][k])
        pvec[:, PV_CONVB + i * 16: PV_CONVB + i * 16 + 16] = colT(inp["sc_conv_b"][i])
    cst = np.zeros((128, C_N), np.float32)
    cst[:, C_IDENT:C_IDENT + 128] = np.eye(128, dtype=np.float32)
    r = np.arange(128)
    cst[:, C_BD32:C_BD32 + 128] = (r[:, None] // 32 == r[None, :] // 32)
    cst[:, C_MASKE + 0] = ((r // 16) % 2 == 0)
    cst[:, C_MASKE + 1] = ((r // 16) % 2 == 1)
    cst[:, C_MASKN + 0] = -cst[:, C_MASKE + 0]
    cst[:, C_MASKN + 1] = -cst[:, C_MASKE + 1]
    for wi, w_ in enumerate(POOL_W):
        cnt = np.minimum(np.arange(16) + 1, w_)
        cst[:, C_INVC + wi * 16: C_INVC + wi * 16 + 16] = np.float32(1.0) / cnt.astype(np.float32)
    s5q = np.zeros((2, 128, 1120), np.float32)
    s5h = np.zeros((2, 128, 2560), np.float32)
    for i in range(2):
        are = f(inp["s5_a_re"][i]); aim = f(inp["s5_a_im"][i]); ldt = f(inp["s5_log_dt"][i])
        bre = f(inp["s5_b_re"][i]); bim = f(inp["s5_b_im"][i])
        cre = f(inp["s5_c_re"][i]); cim = f(inp["s5_c_im"][i])
        qa = lambda a: a.reshape(32, 2, 64).transpose(1, 2, 0).reshape(128, 32)
        s5q[i, :, 0:32] = qa(are); s5q[i, :, 32:64] = qa(aim)
        s5q[i, :, 64:96] = qa(np.broadcast_to(ldt[:, None], (64, 64)))
        qb = lambda b: b.reshape(32, 2, 64, 16).transpose(1, 2, 0, 3).reshape(128, 512)
        s5q[i, :, 96:608] = qb(bre); s5q[i, :, 608:1120] = qb(bim)
        ha = lambda a: np.broadcast_to(a.reshape(8, 8, 1, 64), (8, 8, 16, 64)).transpose(1, 2, 0, 3).reshape(128, 512)
        s5h[i, :, 0:512] = ha(are); s5h[i, :, 512:1024] = ha(aim)
        s5h[i, :, 1024:1536] = ha(np.broadcast_to(ldt[:, None], (64, 64)))
        hc = lambda c: c.reshape(8, 8, 16, 64).transpose(1, 2, 0, 3).reshape(128, 512)
        s5h[i, :, 1536:2048] = hc(cre); s5h[i, :, 2048:2560] = hc(cim)
    shared = dict(pvec=pvec, cst=cst, s5q=s5q, s5h=s5h)
    for i in range(2):
        shared["w_sc_in%d" % i] = w_sc_in[i]; shared["w_sc_out%d" % i] = w_sc_out[i]
        shared["w_ev_in%d" % i] = w_ev_in[i]; shared["w_glu%d" % i] = w_glu[i]
        shared["w_ev_out%d" % i] = w_ev_out[i]; shared["w_pool%d" % i] = w_pool[i]
    return shared


def needed_keys(layers):
    ks = ["pvec", "cst", "s5q", "s5h"]
    for l in layers:
        i = l // 2
        if l % 2 == 1:
            ks += ["w_sc_in%d" % i, "w_sc_out%d" % i]
        else:
            ks += ["w_ev_in%d" % i, "w_glu%d" % i, "w_ev_out%d" % i, "w_pool%d" % i]
    return ks


_PROG = {}


def kernel(**inputs):
    x = np.asarray(inputs["x"], dtype=np.float32)
    shared = prep_inputs(inputs)
    key = "full"
    if key not in _PROG:
        _PROG[key] = build_program()
    nc = _PROG[key]
    in_maps = []
    for b in range(N_CORES):
        d = dict(shared)
        d["x"] = np.ascontiguousarray(x[b])
        in_maps.append(d)
    res = run_bass_kernel_spmd(nc, in_maps, core_ids=list(range(N_CORES)))
    out = np.stack([np.asarray(r["out"], dtype=np.float32) for r in res.results], axis=0)
    return out
```

```python
import contextlib
import numpy as np
import concourse.bass as bass
import concourse.mybir as mybir
from concourse.bass_utils import run_bass_kernel_spmd

F32 = mybir.dt.float32
BF16 = mybir.dt.bfloat16
AF = mybir.ActivationFunctionType
ALU = mybir.AluOpType

L_SEQ = 2048
T = 8
NCH = L_SEQ // T
POOL_W = (2, 4, 8, 16)
N_CORES = 8
STRICT_SAME_ENGINE = True

PV_NORM = 0
PV_FINAL = 32
PV_S5D = 40
PV_BGLU = 56
PV_PSCALE = 88
PV_CONVW = 104
PV_CONVB = 200
PV_N = 232
C_IDENT = 0
C_BD32 = 128
C_MASKE = 256
C_MASKN = 258
C_INVC = 260
C_N = 260 + 64


class Op:
    __slots__ = ('eng', 'fn', 'deps', 'sig', 'seq', 'dsem', 'dval', 'idx')

    def __init__(self, eng, fn):
        self.eng = eng; self.fn = fn; self.deps = []; self.sig = False
        self.seq = None; self.dsem = None; self.dval = None; self.idx = None


class Sched:
    ENGS = ('pe', 'act', 'dve', 'pool', 'sp')

    def __init__(self):
        self.ops = {e: [] for e in self.ENGS}
        self.lastw = {}
        self.readers = {}
        self.dcount = {}

    def _add_reader(self, key, o):
        d = self.readers.setdefault(key, {})
        if o.dsem is not None:
            d.setdefault('dma', []).append(o)
        else:
            d[o.eng] = o

    def op(self, eng, fn, reads=(), writes=(), dma=None):
        o = Op(eng, fn)
        if dma is not None:
            c = self.dcount.get(dma, 0) + 1
            self.dcount[dma] = c
            o.dsem = dma; o.dval = 16 * c
        deps = []
        for r in reads:
            w = self.lastw.get(r)
            if w is not None:
                deps.append((w, True))
        for k in writes:
            w = self.lastw.get(k)
            if w is not None:
                deps.append((w, False))
            rd = self.readers.get(k)
            if rd:
                for kk, v in rd.items():
                    if kk == 'dma':
                        deps.extend((x, False) for x in v)
                    else:
                        deps.append((v, False))
        best = {}
        dd = []
        for d, raw in deps:
            if d is o:
                continue
            if d.dsem is not None:
                if d not in dd:
                    dd.append(d)
            else:
                if eng == 'pe' and d.eng == 'pe' and o.dsem is None:
                    continue
                if (not STRICT_SAME_ENGINE) and d.eng == eng and o.dsem is None and not raw:
                    continue
                b = best.get(d.eng)
                if b is None or d.idx > b.idx:
                    best[d.eng] = d
        for d in best.values():
            d.sig = True
        o.deps = list(best.values()) + dd
        o.idx = len(self.ops[eng])
        self.ops[eng].append(o)
        for r in reads:
            self._add_reader(r, o)
        for k in writes:
            self.lastw[k] = o
            self.readers[k] = {}
        return o

    def emit(self, nc, stack):
        esem = {}
        for e in self.ENGS:
            esem[e] = stack.enter_context(nc.semaphore("es_" + e))
        dsem = {}
        for n, k in enumerate(self.dcount):
            dsem[k] = stack.enter_context(nc.semaphore("ds_%d" % n))
        for e in self.ENGS:
            c = 0
            for o in self.ops[e]:
                if o.dsem is None and o.sig:
                    c += 1; o.seq = c

        def run(e, engobj):
            waited = {}
            for o in self.ops[e]:
                for d in o.deps:
                    if d.dsem is not None:
                        key = ('d', d.dsem); sem = dsem[d.dsem]; val = d.dval
                    else:
                        key = ('e', d.eng); sem = esem[d.eng]; val = d.seq
                    if waited.get(key, 0) < val:
                        engobj.wait_ge(sem, val)
                        waited[key] = val
                inst = o.fn(engobj)
                if inst is None:
                    continue
                if o.dsem is not None:
                    inst.then_inc(dsem[o.dsem], 16)
                elif o.sig:
                    inst.then_inc(esem[e], 1)

        block = stack.enter_context(nc.Block())

        @block.tensor
        def _(t): run('pe', t)

        @block.scalar
        def _(t): run('act', t)

        @block.vector
        def _(t): run('dve', t)

        @block.gpsimd
        def _(t): run('pool', t)

        @block.sync
        def _(t): run('sp', t)


class Buf:
    def __init__(self, name, tens, off, dtype, n, ap=None):
        self.name = name; self.tens = tens; self.off = off; self.dtype = dtype; self.n = n
        self.esz = 4 if dtype == F32 else 2
        size = n * self.esz
        assert off % 4 == 0 and size % 4 == 0, (off, size)
        self.keys = [(name, b) for b in range(off // 1024, (off + size - 1) // 1024 + 1)]
        if ap is None:
            ap = tens[:, off // 4:(off + size) // 4]
            if dtype != F32:
                ap = ap.bitcast(dtype)
        self.ap = ap

    def __getitem__(self, s):
        a = 0 if s.start is None else s.start
        b = self.n if s.stop is None else s.stop
        assert s.step is None and 0 <= a < b <= self.n, (a, b, self.n)
        if (a * self.esz) % 4 == 0 and ((b - a) * self.esz) % 4 == 0:
            return Buf(self.name, self.tens, self.off + a * self.esz, self.dtype, b - a)
        o = Buf.__new__(Buf)
        o.name = self.name; o.tens = self.tens; o.off = self.off; o.dtype = self.dtype
        o.n = b - a; o.esz = self.esz
        lo = self.off + a * self.esz; hi = self.off + b * self.esz
        o.keys = [(self.name, k) for k in range(lo // 1024, (hi - 1) // 1024 + 1)]
        o.ap = self.ap[:, a:b]
        return o

    def v(self, ap):
        o = Buf.__new__(Buf)
        o.name = self.name; o.tens = self.tens; o.off = self.off; o.dtype = self.dtype
        o.n = self.n; o.esz = self.esz; o.keys = self.keys; o.ap = ap
        return o


class PSv:
    def __init__(self, ap, keys):
        self.ap = ap; self.keys = keys


def _k(*objs):
    out = []
    for o in objs:
        if o is None or isinstance(o, (int, float)):
            continue
        out.extend(o.keys)
    return out


def build_program(layers=(0, 1, 2, 3), do_final=True, n_setup=None):
    nc = bass.Bass("TRN2", target_bir_lowering=False)
    S = Sched()
    st = contextlib.ExitStack()

    def din(name, shape, dt=F32):
        return nc.dram_tensor(name, list(shape), dt, kind="ExternalInput").ap()

    x_d = din("x", [2048, 1024])
    need_i = sorted(set(l // 2 for l in layers))
    w_sc_in = {}; w_sc_out = {}; w_ev_in = {}; w_glu = {}; w_ev_out = {}; w_pool = {}
    for l in layers:
        i = l // 2
        if l % 2 == 1:
            w_sc_in[i] = din("w_sc_in%d" % i, [16, 4, 128, 1024])
            w_sc_out[i] = din("w_sc_out%d" % i, [8, 2, 128, 1024])
        else:
            w_ev_in[i] = din("w_ev_in%d" % i, [32, 128, 1024])
            w_glu[i] = din("w_glu%d" % i, [16, 128, 1024])
            w_ev_out[i] = din("w_ev_out%d" % i, [2, 8, 128, 1024])
            w_pool[i] = din("w_pool%d" % i, [2, 128, 1024])
    pvec_d = din("pvec", [128, PV_N])
    cst_d = din("cst", [128, C_N])
    s5q_d = din("s5q", [2, 128, 1120])
    s5h_d = din("s5h", [2, 128, 2560])
    out_d = nc.dram_tensor("out", [2048, 1024], F32, kind="ExternalOutput").ap()
    xb_s = nc.dram_tensor("xb_s", [2, 8, 128, T * 2 * 128], BF16, kind="Internal").ap()
    cs_s = nc.dram_tensor("cs_s", [2, 8, 128, T * 2 * 128], BF16, kind="Internal").ap()
    kb_s = nc.dram_tensor("kb_s", [2, 8, 128, T * 128], BF16, kind="Internal").ap()
    tab_s = nc.dram_tensor("tab_s", [2, 8, 128, 4 * 768], F32, kind="Internal").ap()

    with st:
        def sb(name, nwords):
            return st.enter_context(nc.sbuf_tensor(name, [128, nwords], F32))
        HT_t = sb("HT", 16384)
        R1_t = sb("R1", 8192)
        R23_t = sb("R23", 16384)
        WB_t = sb("WB", 4096)
        TMP_t = sb("TMP", 5184)
        MISC_t = sb("MISC", 1024)
        assert PV_N + C_N + 256 + 160 <= 1024
        ps_t = st.enter_context(nc.psum_tensor("ps", [128, 4096], F32))

        def HT(ft, c0=0, c1=2048):
            return Buf("HT", HT_t, (ft * 2048 + c0) * 4, F32, c1 - c0)

        def R1b(tile, c0=0, c1=2048):
            return Buf("R1", R1_t, (tile * 2048 + c0) * 2, BF16, c1 - c0)

        def R23b(tile, c0=0, c1=2048):
            return Buf("R23", R23_t, (tile * 2048 + c0) * 2, BF16, c1 - c0)

        def R1v(off, dtype, n): return Buf("R1", R1_t, off, dtype, n)
        def R23v(off, dtype, n): return Buf("R23", R23_t, off, dtype, n)
        def TMPv(off, dtype, n): return Buf("TMP", TMP_t, off, dtype, n)
        def MISCv(off, dtype, n): return Buf("MISC", MISC_t, off, dtype, n)

        def PS(bank, c0=0, c1=512, p0=0, p1=128):
            return PSv(ps_t[p0:p1, bank * 512 + c0: bank * 512 + c1], [('ps', bank)])

        def PSspan(b0, nb):
            return PSv(ps_t[:, b0 * 512:(b0 + nb) * 512], [('ps', b) for b in range(b0, b0 + nb)])

        pvec = MISCv(0, F32, PV_N)
        cst = MISCv(PV_N * 4, F32, C_N)
        dec = MISCv((PV_N + C_N) * 4, F32, 64)
        utp = MISCv((PV_N + C_N + 64) * 4, F32, 128)
        onesb = MISCv((PV_N + C_N + 192) * 4, BF16, 128)
        sm = MISCv((PV_N + C_N + 256) * 4, F32, 160)

        def pcol(c): return pvec[c:c + 1]
        ident = cst[C_IDENT:C_IDENT + 128]
        bd32 = cst[C_BD32:C_BD32 + 128]

        uniq = [0]

        def uchan():
            uniq[0] += 1
            return ('u', uniq[0])

        def dma(eng, out, in_ap_or_buf, chan, out_is_dram=False, rkeys=(), wkeys=()):
            if chan == 'par':
                chan = uchan()
            if out_is_dram:
                src = in_ap_or_buf
                S.op(eng, lambda e: e.dma_start(out=out, in_=src.ap), list(src.keys) + list(rkeys), list(wkeys), dma=chan)
            else:
                src = in_ap_or_buf
                S.op(eng, lambda e: e.dma_start(out=out.ap, in_=src), list(rkeys), list(out.keys) + list(wkeys), dma=chan)

        def act(out, in_, func, bias=None, scale=None):
            kw = {}
            rk = _k(in_, bias, scale)
            if bias is not None:
                kw['bias'] = bias if isinstance(bias, (int, float)) else bias.ap
            if scale is not None:
                kw['scale'] = scale if isinstance(scale, (int, float)) else scale.ap
            S.op('act', lambda e: e.activation(out=out.ap, in_=in_.ap, func=func, **kw), rk, out.keys)

        def tt(out, a, b, op, eng='dve'):
            S.op(eng, lambda e: e.tensor_tensor(out=out.ap, in0=a.ap, in1=b.ap, op=op), _k(a, b), out.keys)

        def stt(out, in0, scalar, in1, op0, op1, eng='dve'):
            sc = scalar if isinstance(scalar, (int, float)) else scalar.ap
            S.op(eng, lambda e: e.scalar_tensor_tensor(out=out.ap, in0=in0.ap, scalar=sc, in1=in1.ap, op0=op0, op1=op1),
                 _k(in0, scalar, in1), out.keys)

        def ts(out, in0, s1, s2, op0, op1=None, eng='dve'):
            a1 = s1 if isinstance(s1, (int, float)) else s1.ap
            a2 = None if s2 is None else (s2 if isinstance(s2, (int, float)) else s2.ap)
            kw = {} if op1 is None else {'op1': op1}
            S.op(eng, lambda e: e.tensor_scalar(out=out.ap, in0=in0.ap, scalar1=a1, scalar2=a2, op0=op0, **kw),
                 _k(in0, s1, s2), out.keys)

        def cp(out, in_, eng='dve'):
            S.op(eng, lambda e: e.tensor_copy(out=out.ap, in_=in_.ap), _k(in_), out.keys)

        def memset(out, val, eng='dve'):
            S.op(eng, lambda e: e.memset(out.ap, val), [], out.keys)

        def recip(out, in_):
            S.op('dve', lambda e: e.reciprocal(out=out.ap, in_=in_.ap), _k(in_), out.keys)

        def mm(out, lhsT, rhs, start, stop, tile_position=None):
            kw = {}
            if tile_position is not None:
                kw['tile_position'] = tile_position
            S.op('pe', lambda e: e.matmul(out.ap, lhsT=lhsT.ap, rhs=rhs.ap, start=start, stop=stop, **kw),
                 _k(lhsT, rhs), out.keys)

        def tr(out, in_):
            S.op('pe', lambda e: e.transpose(out.ap, in_.ap, ident.ap), _k(in_, ident), out.keys)

        NSLOT = 8
        wq = []
        wstate = {'issued': 0, 'next': 0}
        AHEAD = 6

        def wslot(n):
            return Buf("WB", WB_t, (n % NSLOT) * 2048, BF16, 1024)

        def w_issue_upto(n):
            while wstate['issued'] <= n and wstate['issued'] < len(wq):
                i = wstate['issued']
                dma('pool', wslot(i), wq[i], ('w', i % NSLOT))
                wstate['issued'] += 1

        def wget():
            n = wstate['next']
            wstate['next'] += 1
            w_issue_upto(n + AHEAD)
            return wslot(n)

        for l in layers:
            i = l // 2
            if l % 2 == 0:
                for j in range(8):
                    wq.append(w_ev_in[i][16 + j]); wq.append(w_ev_in[i][24 + j])
                for o in range(8):
                    wq.append(w_ev_out[i][1, o])
                for j in range(8):
                    wq.append(w_ev_in[i][j])
                for j in range(8):
                    wq.append(w_ev_in[i][8 + j])
                for j in range(8):
                    wq.append(w_glu[i][8 + j]); wq.append(w_glu[i][j])
                for o in range(8):
                    wq.append(w_ev_out[i][0, o])
            else:
                for j in range(16):
                    for w in range(4):
                        wq.append(w_sc_in[i][j, w])
                for o in range(8):
                    wq.append(w_sc_out[i][o, 0]); wq.append(w_sc_out[i][o, 1])

        dma('sp', pvec, pvec_d, 'par')
        dma('sp', cst, cst_d, 'par')
        memset(onesb, 1.0)
        w_issue_upto(AHEAD - 1)

        xs = [R1v(s_ * 16384, F32, 4096) for s_ in range(2)]
        for c in range(4):
            xsb = xs[c % 2]
            S.op('sp', (lambda xsb, c: lambda e: e.dma_start(
                out=xsb.ap.rearrange("p (t f) -> p t f", t=4),
                in_=x_d[c * 512:(c + 1) * 512, :].rearrange("(t p) f -> p t f", p=128)))(xsb, c),
                [], xsb.keys, dma=('x', c % 2))
            for ft in range(8):
                for tl in range(4):
                    tr(PS(ft, tl * 128, (tl + 1) * 128), xsb[tl * 1024 + ft * 128: tl * 1024 + (ft + 1) * 128])
                if ft % 2 == 0:
                    act(HT(ft, c * 512, (c + 1) * 512), PS(ft), AF.Copy)
                else:
                    cp(HT(ft, c * 512, (c + 1) * 512), PS(ft))

        def cmul2(o_re, o_im, a_re, a_im, b_re, b_im, t1, t2, t3, t4):
            tt(t1, a_re, b_re, ALU.mult); tt(t2, a_im, b_im, ALU.mult)
            tt(o_re, t1, t2, ALU.subtract)
            tt(t3, a_re, b_im, ALU.mult, 'pool'); tt(t4, a_im, b_re, ALU.mult, 'pool')
            tt(o_im, t3, t4, ALU.add, 'pool')

        def csq2(o_re, o_im, a_re, a_im, t1):
            tt(o_re, a_re, a_re, ALU.mult); tt(t1, a_im, a_im, ALU.mult); tt(o_re, o_re, t1, ALU.subtract)
            tt(o_im, a_re, a_im, ALU.mult, 'pool'); ts(o_im, o_im, 2.0, 0.0, ALU.mult, ALU.add, eng='pool')

        def discretise(lam_re, lam_im, logdt, n, alloc):
            dtv = alloc(n); act(dtv, logdt, AF.Exp)
            prd = alloc(n); tt(prd, lam_re, dtv, ALU.mult)
            mag = alloc(n); act(mag, prd, AF.Exp)
            ang = alloc(n); tt(ang, lam_im, dtv, ALU.mult)
            c = alloc(n); s_ = alloc(n); c2 = alloc(n); s2 = alloc(n); t1 = alloc(n); hp = alloc(n)
            ts(hp, ang, -1.0 / 8.0, float(np.pi / 2), ALU.mult, ALU.add)
            act(c, hp, AF.Sin)
            act(s_, ang, AF.Sin, scale=1.0 / 8.0)
            for _ in range(3):
                csq2(c2, s2, c, s_, t1)
                c, c2 = c2, c; s_, s2 = s2, s_
            lb_re = alloc(n); lb_im = alloc(n)
            tt(lb_re, mag, c, ALU.mult); tt(lb_im, mag, s_, ALU.mult, 'pool')
            return dict(mag=mag, c1=c, s1=s_, lb_re=lb_re, lb_im=lb_im, dtv=dtv, prd=prd)

        evac_rr = [0]

        def evac(dst, src):
            evac_rr[0] ^= 1
            if evac_rr[0]:
                act(dst, src, AF.Copy)
            else:
                cp(dst, src)

        def s5_setup(i, first_setup=True):
            cur = [0]

            def alloc23(n, dtype=F32):
                b_ = R23v(cur[0], dtype, n); cur[0] += n * (4 if dtype == F32 else 2)
                assert cur[0] <= 65536, cur[0]
                return b_
            cs0 = alloc23(2048)
            hin = alloc23(2560)
            dma('sp' if first_setup else 'act', hin, s5h_d[i], 'par')
            lamre, lamim, ldt, cre, cim = [hin[k * 512:(k + 1) * 512] for k in range(5)]
            d = discretise(lamre, lamim, ldt, 512, alloc23)
            zb = [[alloc23(512), alloc23(512)], [alloc23(512), alloc23(512)]]
            t3 = alloc23(512); t4 = alloc23(512)
            assert cur[0] == 55296, cur[0]
            t1 = TMPv(16384, F32, 512); t2 = TMPv(18432, F32, 512)
            cste = [[TMPv(0, F32, 1024), TMPv(4096, F32, 1024)], [TMPv(8192, F32, 1024), TMPv(12288, F32, 1024)]]
            csst = R1v(0, BF16, 8 * T * 2 * 128)
            qs = {}

            def q_small():
                allocq = alloc23
                qin = allocq(1120)
                dma('sp' if first_setup else 'act', qin, s5q_d[i], 'par')
                qlre = qin[0:32]; qlim = qin[32:64]; qldt = qin[64:96]
                bqre = qin[96:608]; bqim = qin[608:1120]
                dq = discretise(qlre, qlim, qldt, 32, allocq)
                den = allocq(32); q1 = allocq(32); q2 = allocq(32); fre = allocq(32); fim = allocq(32)
                lm1 = allocq(32)
                tt(q1, qlre, qlre, ALU.mult); tt(q2, qlim, qlim, ALU.mult); tt(den, q1, q2, ALU.add)
                recip(den, den)
                ts(lm1, dq['lb_re'], -1.0, None, ALU.add)
                tt(q1, lm1, qlre, ALU.mult); tt(q2, dq['lb_im'], qlim, ALU.mult); tt(q1, q1, q2, ALU.add)
                tt(fre, q1, den, ALU.mult)
                tt(q1, dq['lb_im'], qlre, ALU.mult); tt(q2, lm1, qlim, ALU.mult); tt(q1, q1, q2, ALU.subtract)
                tt(fim, q1, den, ALU.mult)
                act(dec[i * 32:(i + 1) * 32], dq['prd'], AF.Exp, scale=float(T))
                uc = allocq(32); us = allocq(32); uc2 = allocq(32); us2 = allocq(32)
                cp(uc, dq['c1']); cp(us, dq['s1'])
                nsq = {8: 3, 16: 4}[T]
                for _ in range(nsq):
                    csq2(uc2, us2, uc, us, q1)
                    uc, uc2 = uc2, uc; us, us2 = us2, us
                cp(utp[i * 64:i * 64 + 32], uc); cp(utp[i * 64 + 32:i * 64 + 64], us)
                qs.update(dq=dq, fre=fre, fim=fim, bqre=bqre, bqim=bqim, allocq=allocq)
            if first_setup:
                q_small()
            z = [cre, cim]
            for k in range(-1, T):
                if k >= 0:
                    zn = zb[k % 2]
                    cmul2(zn[0], zn[1], z[0], z[1], d['lb_re'], d['lb_im'], t1, t2, t3, t4)
                    z = zn
                ce = cste[(k + 1) % 2]
                for ri in range(2):
                    for e_ in range(2):
                        outv = ce[ri].v(ce[ri].ap.rearrange("p (u e q) -> p u e q", u=8, e=2)[:, :, e_, :])
                        inv = z[ri].v(z[ri].ap.rearrange("p (u q) -> p u q", u=8))
                        mc = (C_MASKE if ri == 0 else C_MASKN) + e_
                        act(outv, inv, AF.Identity, scale=cst[mc:mc + 1])
                for ri in range(2):
                    for hf in range(2):
                        bank = ((k + 1) % 2) * 4 + ri * 2 + hf
                        for u4 in range(4):
                            ut = hf * 4 + u4
                            tr(PS(bank, u4 * 128, (u4 + 1) * 128), ce[ri][ut * 128:(ut + 1) * 128])
                        psv = PSv(PS(bank).ap.rearrange("p (a b) -> p a b", a=4), PS(bank).keys)
                        if k == -1:
                            dst = cs0.v(cs0.ap.rearrange("p (u f) -> p u f", u=8)[:, hf * 4:(hf + 1) * 4, ri * 128:(ri + 1) * 128])
                        else:
                            o_ = (k * 2 + ri) * 128
                            dst = csst.v(csst.ap.rearrange("p (u f) -> p u f", u=8)[:, hf * 4:(hf + 1) * 4, o_:o_ + 128])
                        evac(dst, psv)
            S.op('sp', lambda e: e.dma_start(out=cs_s[i].rearrange("u p f -> p u f"),
                                             in_=csst.ap.rearrange("p (u f) -> p u f", u=8)),
                 csst.keys, [('cs_s', i)], dma=uchan())

            if not first_setup:
                q_small()
            dq = qs['dq']; fre = qs['fre']; fim = qs['fim']; bqre = qs['bqre']; bqim = qs['bqim']; allocq = qs['allocq']
            cur[0] = 8192

            def bc16(b32):
                return b32.v(b32.ap.rearrange("p (g o) -> p g o", o=1).to_broadcast([128, 32, 16]))

            def v3(b512):
                return b512.v(b512.ap.rearrange("p (g h) -> p g h", h=16))
            bb = [[allocq(512), allocq(512)], [allocq(512), allocq(512)]]
            bste = [[allocq(1024), allocq(1024)], [allocq(1024), allocq(1024)]]
            kst = allocq(8 * T * 128, BF16)
            assert cur[0] <= 55296, cur[0]
            b1 = TMPv(0, F32, 512); b2 = TMPv(2048, F32, 512); b3 = TMPv(4096, F32, 512); b4 = TMPv(6144, F32, 512)
            ktmp = TMPv(8192, F32, 512)
            cmul2(v3(bb[0][0]), v3(bb[0][1]), bc16(fre), bc16(fim), v3(bqre), v3(bqim), v3(b1), v3(b2), v3(b3), v3(b4))
            xbst = R1v(0, BF16, 8 * T * 2 * 128)
            memset(bste[0][0], 0.0); memset(bste[0][1], 0.0, eng='pool')
            memset(bste[1][0], 0.0); memset(bste[1][1], 0.0, eng='pool')
            bd4 = bd32.v(bd32.ap.rearrange("p (o c) -> p o c", o=1).to_broadcast([128, 4, 128]))
            curb = bb[0]
            for m in range(T):
                if m >= 1:
                    nb = bb[m % 2]
                    cmul2(v3(nb[0]), v3(nb[1]), bc16(dq['lb_re']), bc16(dq['lb_im']), v3(curb[0]), v3(curb[1]),
                          v3(b1), v3(b2), v3(b3), v3(b4))
                    curb = nb
                be = bste[m % 2]
                for ri in range(2):
                    for e_ in range(2):
                        w0 = be[ri].off // 4
                        dstap = be[ri].tens[e_ * 64:(e_ + 1) * 64, w0:w0 + 1024] \
                            .rearrange("p (g e h) -> p g e h", e=2, h=16)[:, :, e_, :]
                        w1 = curb[ri].off // 4
                        srcap = curb[ri].tens[e_ * 64:(e_ + 1) * 64, w1:w1 + 512] \
                            .rearrange("p (g h) -> p g h", h=16)
                        S.op('act', (lambda da, sa: lambda e: e.activation(out=da, in_=sa, func=AF.Copy))(dstap, srcap),
                             curb[ri].keys, be[ri].keys)
                k = T - 1 - m
                for ri in range(2):
                    for hf in range(2):
                        bank = ri * 2 + hf
                        for u4 in range(4):
                            ut = hf * 4 + u4
                            tr(PS(bank, u4 * 128, (u4 + 1) * 128), be[ri][ut * 128:(ut + 1) * 128])
                        psv = PSv(PS(bank).ap.rearrange("p (a b) -> p a b", a=4), PS(bank).keys)
                        o_ = (k * 2 + ri) * 128
                        dst = xbst.v(xbst.ap.rearrange("p (u f) -> p u f", u=8)[:, hf * 4:(hf + 1) * 4, o_:o_ + 128])
                        evac(dst, psv)
                for hf in range(2):
                    kb = 4 + (m % 2) * 2 + hf
                    for u4 in range(4):
                        ut = hf * 4 + u4
                        for ri in range(2):
                            mm(PS(kb, u4 * 128, (u4 + 1) * 128), be[ri][ut * 128:(ut + 1) * 128],
                               cs0[(ut * 2 + ri) * 128:(ut * 2 + ri + 1) * 128], ri == 0, ri == 1)
                    psv = PSv(PS(kb).ap.rearrange("p (a b) -> p a b", a=4), PS(kb).keys)
                    if m == 0:
                        kt4 = ktmp.v(ktmp.ap.rearrange("p (a b) -> p a b", a=4))
                        tt(kt4, psv, bd4, ALU.mult)
                        for u4 in range(4):
                            ut = hf * 4 + u4
                            ko = (ut * T + m) * 128
                            stt(kst[ko:ko + 128], ident, pcol(PV_S5D + i * 8 + ut), ktmp[u4 * 128:(u4 + 1) * 128],
                                ALU.mult, ALU.add)
                    else:
                        dst = kst.v(kst.ap.rearrange("p (u f) -> p u f", u=8)[:, hf * 4:(hf + 1) * 4, m * 128:(m + 1) * 128])
                        tt(dst, psv, bd4, ALU.mult)
            S.op('sp', lambda e: e.dma_start(out=xb_s[i].rearrange("u p f -> p u f"),
                                             in_=xbst.ap.rearrange("p (u f) -> p u f", u=8)),
                 xbst.keys, [('xb_s', i)], dma=uchan())
            S.op('sp', lambda e: e.dma_start(out=kb_s[i].rearrange("u p f -> p u f"),
                                             in_=kst.ap.rearrange("p (u f) -> p u f", u=8)),
                 kst.keys, [('kb_s', i)], dma=uchan())

            tabc = R1v(0, F32, 32 * NCH)
            tabs = R23v(32768, F32, 32 * NCH)
            twd = [R23v(0, F32, 2048), R23v(8192, F32, 2048)]
            twp = [R23v(16384, F32, 2048), R23v(24576, F32, 2048)]
            pn = [sm[0:32], sm[32:64]]; pn2 = [sm[64:96], sm[96:128]]; sq1 = sm[128:160]

            def t3(b_, g0, g1, n0, n1):
                return b_.v(b_.ap.rearrange("p (g c) -> p g c", c=NCH)[:, g0:g1, n0:n1])

            def bcn(b32, g0, g1, n):
                return b32.v(b32.ap.rearrange("p (g o) -> p g o", o=1)[:, g0:g1, :].to_broadcast([128, g1 - g0, n]))
            Bc = TMPv(0, F32, 512); Bs = TMPv(2048, F32, 512); Ac = TMPv(4096, F32, 512); As = TMPv(6144, F32, 512)

            def tiny(k_):
                return TMPv(8192 + 128 * k_, F32, 32)
            powB = [[tiny(0), tiny(1)], [tiny(2), tiny(3)], [tiny(4), tiny(5)], [tiny(6), tiny(7)]]
            powA = [[tiny(8), tiny(9)], [tiny(10), tiny(11)], [tiny(12), tiny(13)], [tiny(14), tiny(15)]]
            sq1 = tiny(16)
            cp(powB[0][0], utp[i * 64:i * 64 + 32]); cp(powB[0][1], utp[i * 64 + 32:i * 64 + 64])
            for j_ in range(3):
                csq2(powB[j_ + 1][0], powB[j_ + 1][1], powB[j_][0], powB[j_][1], sq1)
            csq2(powA[0][0], powA[0][1], powB[3][0], powB[3][1], sq1)
            for j_ in range(3):
                csq2(powA[j_ + 1][0], powA[j_ + 1][1], powA[j_][0], powA[j_][1], sq1)

            def small_table(tc, ts_, pows):
                def s3(b_, n0, n1):
                    return b_.v(b_.ap.rearrange("p (g c) -> p g c", c=16)[:, :, n0:n1])
                memset(s3(tc, 0, 1), 1.0); memset(s3(ts_, 0, 1), 0.0)
                n_ = 1; lv = 0
                while n_ < 16:
                    pr_, pi_ = pows[lv]
                    w1 = twd[0][0:32 * n_]; w2 = twd[1][0:32 * n_]
                    w1v = w1.v(w1.ap.rearrange("p (g c) -> p g c", c=n_)); w2v = w2.v(w2.ap.rearrange("p (g c) -> p g c", c=n_))
                    tt(w1v, s3(tc, 0, n_), bcn(pr_, 0, 32, n_), ALU.mult); tt(w2v, s3(ts_, 0, n_), bcn(pi_, 0, 32, n_), ALU.mult)
                    tt(s3(tc, n_, 2 * n_), w1v, w2v, ALU.subtract)
                    tt(w1v, s3(tc, 0, n_), bcn(pi_, 0, 32, n_), ALU.mult); tt(w2v, s3(ts_, 0, n_), bcn(pr_, 0, 32, n_), ALU.mult)
                    tt(s3(ts_, n_, 2 * n_), w1v, w2v, ALU.add)
                    n_ *= 2; lv += 1
            small_table(Bc, Bs, powB)
            small_table(Ac, As, powA)
            act(PS(0), Ac, AF.Copy); act(PS(1), As, AF.Copy)
            for g0 in range(0, 32, 8):
                g1 = g0 + 8

                def abro(psb):
                    return PSv(psb.ap.rearrange("p (g a o) -> p g a o", a=16, o=1)[:, g0:g1, :, :].to_broadcast([128, 8, 16, 16]), psb.keys)

                def bbro(bt_):
                    return bt_.v(bt_.ap.rearrange("p (g o b) -> p g o b", o=1, b=16)[:, g0:g1, :, :].to_broadcast([128, 8, 16, 16]))

                def w4(b_):
                    return b_.v(b_.ap.rearrange("p (g a b) -> p g a b", a=16, b=16))

                def o4(tb_):
                    return tb_.v(tb_.ap.rearrange("p (g a b) -> p g a b", a=16, b=16)[:, g0:g1, :, :])
                tt(w4(twd[0]), abro(PS(0)), bbro(Bc), ALU.mult)
                tt(w4(twd[1]), abro(PS(1)), bbro(Bs), ALU.mult)
                tt(o4(tabc), w4(twd[0]), w4(twd[1]), ALU.subtract, 'pool')
                tt(w4(twp[0]), abro(PS(0)), bbro(Bs), ALU.mult)
                tt(w4(twp[1]), abro(PS(1)), bbro(Bc), ALU.mult)
                tt(o4(tabs), w4(twp[0]), w4(twp[1]), ALU.add, 'pool')
            for (src, offs) in ((tabc, (0, 512)), (tabs, (256,))):
                for o_ in offs:
                    for ut in range(8):
                        S.op('sp', (lambda src, o_, ut: lambda e: e.dma_start(
                            out=tab_s[i, ut].rearrange("p (r w) -> p r w", w=768)[:, :, o_:o_ + NCH],
                            in_=src.ap.rearrange("p (g c) -> p g c", c=NCH)[:, ut * 4:(ut + 1) * 4, :]))(src, o_, ut),
                            src.keys, [('tab_s', i, o_, ut)], dma=uchan())

        setup_layers = [l // 2 for l in layers if l % 2 == 0]
        for n_s, i in enumerate(setup_layers):
            s5_setup(i, first_setup=(n_s == 0))

        SQSUM = TMPv(0, F32, 2048); SQT = TMPv(8192, F32, 2048); SQB = TMPv(16384, BF16, 2048)

        def tail_stats(o):
            if o == 0:
                act(SQSUM, HT(0), AF.Square)
            else:
                act(SQT, HT(o), AF.Square)
                tt(SQB if o == 7 else SQSUM, SQSUM, SQT, ALU.add)

        def rmsnorm_to(gcol0, dst_fn, pre=False):
            if pre:
                for c in range(4):
                    mm(PS(c), onesb, SQB[c * 512:(c + 1) * 512], True, True)
            else:
                sq = [TMPv(0, BF16, 2048), TMPv(4096, BF16, 2048)]
                for ft in range(8):
                    sqb = sq[ft % 2]
                    act(sqb, HT(ft), AF.Square)
                    for c in range(4):
                        mm(PS(c), onesb, sqb[c * 512:(c + 1) * 512], ft == 0, ft == 7)
            rs = TMPv(8192, F32, 2048)
            act(rs, PSspan(0, 4), AF.Ln, bias=1e-6, scale=1.0 / 1024.0)
            act(PSspan(4, 4), rs, AF.Exp, scale=-0.5)
            for ft in range(8):
                stt(dst_fn(ft), HT(ft), pcol(gcol0 + ft), PSspan(4, 4), ALU.mult, ALU.mult)

        def proj(wt, rhs_fn, nk, g):
            for kt in range(nk):
                wbuf = wt[kt // 8]
                lhs = wbuf[(kt % 8) * 128:(kt % 8 + 1) * 128]
                for c in range(4):
                    mm(PS(4 * g + c), lhs, rhs_fn(kt, c), kt == 0, kt == nk - 1)

        grp = [0]

        def nextg():
            g = grp[0]; grp[0] ^= 1
            return g

        def hn(kt, c): return R1b(kt, c * 512, (c + 1) * 512)

        def residual_add(o, g):
            tt(HT(o), PSspan(4 * g, 4), HT(o), ALU.add)

        def odd_layer(l, pre):
            i = l // 2
            rmsnorm_to(PV_NORM + l * 8, lambda ft: R1b(ft), pre=pre)
            V = TMPv(0, F32, 2064)
            ACC = TMPv(8256, F32, 2048)
            SG = TMPv(8256 + 8192, BF16, 2048)
            memset(V[0:16], 0.0)
            Vd = V[16:2064]
            for j in range(16):
                cw = [pcol(PV_CONVW + i * 48 + k * 16 + j) for k in range(3)]
                cb = pcol(PV_CONVB + i * 16 + j)
                g = nextg(); proj([wget()], hn, 8, g)
                act(Vd, PSspan(4 * g, 4), AF.Copy)
                g = nextg(); proj([wget()], hn, 8, g)
                tt(Vd, PSspan(4 * g, 4), Vd, ALU.mult)
                act(ACC, Vd, AF.Identity, bias=cb, scale=cw[2])
                stt(ACC, V[15:2063], cw[1], ACC, ALU.mult, ALU.add)
                stt(ACC, V[14:2062], cw[0], ACC, ALU.mult, ALU.add)
                g = nextg(); proj([wget()], hn, 8, g)
                act(SG, PSspan(4 * g, 4), AF.Silu)
                g = nextg(); proj([wget()], hn, 8, g)
                tt(ACC, PSspan(4 * g, 4), ACC, ALU.mult)
                tt(R23b(j), ACC, SG, ALU.mult)
            for o in range(8):
                g = nextg()
                proj([wget(), wget()], lambda kt, c: R23b(kt, c * 512, (c + 1) * 512), 16, g)
                residual_add(o, g)
                tail_stats(o)

        def even_layer(l, pre):
            i = l // 2
            rmsnorm_to(PV_NORM + l * 8, lambda ft: R1b(ft), pre=pre)
            R3o = 32768
            UB = R23v(R3o + 0, F32, 2064)
            WA = R23v(R3o + 9216, F32, 2064)
            WBf = R23v(R3o + 18432, F32, 2064)
            DFB = [TMPv(0, BF16, 2048), TMPv(4096, BF16, 2048)]
            SGB = [TMPv(8192, BF16, 2048), TMPv(12288, BF16, 2048)]
            T16 = TMPv(16384, F32, 16)
            memset(UB[0:16], 0.0); memset(WA[0:16], 0.0); memset(WBf[0:16], 0.0)
            PW = [R23v(R3o + 27648, BF16, 1024), R23v(R3o + 27648 + 2048, BF16, 1024)]
            for hh in range(2):
                dma('pool', PW[hh], w_pool[i][hh], ('pw', hh))
            for j in range(8):
                gi = j // 2; w = POOL_W[gi]
                g = nextg(); proj([wget()], hn, 8, g)
                act(UB[16:2064], PSspan(4 * g, 4), AF.Copy)
                bufs = [UB, WA, WBf]
                src = UB; sh = 1; nadd = {2: 1, 4: 2, 8: 3, 16: 4}[w]
                dsts = [WA, WBf]
                for a in range(nadd):
                    dst = dsts[a % 2]
                    tt(dst[16:2064], src[16:2064], src[16 - sh:2064 - sh], ALU.add, 'dve' if a in (0, 3) else 'pool')
                    src = dst; sh *= 2
                stt(DFB[j % 2], src[16:2064], 1.0 / w, UB[16:2064], ALU.mult, ALU.subtract)
                tt(T16, src[16:32], cst[C_INVC + gi * 16: C_INVC + gi * 16 + 16], ALU.mult)
                tt(DFB[j % 2][0:16], T16, UB[16:32], ALU.subtract)
                g = nextg(); proj([wget()], hn, 8, g)
                act(SGB[j % 2], PSspan(4 * g, 4), AF.Silu)
                if j % 2 == 1:
                    for dt_ in range(2):
                        g = nextg()
                        pwb = PW[gi // 2]
                        base = ((gi % 2) * 2 + dt_) * 256
                        for kt in range(2):
                            for c in range(4):
                                mm(PS(4 * g + c), pwb[base + kt * 128: base + (kt + 1) * 128],
                                   DFB[kt][c * 512:(c + 1) * 512], kt == 0, kt == 1)
                        o_t = gi * 2 + dt_
                        stt(R23b(o_t), PSspan(4 * g, 4), pcol(PV_PSCALE + i * 8 + o_t), SGB[dt_], ALU.mult, ALU.mult)
            for o in range(8):
                g = nextg()
                proj([wget()], lambda kt, c: R23b(kt, c * 512, (c + 1) * 512), 8, g)
                residual_add(o, g)
            for j in range(8):
                g = nextg(); proj([wget()], hn, 8, g)
                ubj = R23b(j)
                psn = PSspan(4 * g, 4)
                act(ubj.v(ubj.ap.rearrange("p (k c) -> p c k", k=T)),
                    PSv(psn.ap.rearrange("p (c k) -> p c k", k=T), psn.keys), AF.Copy)
            for j in range(8):
                g = nextg(); proj([wget()], hn, 8, g)
                act(R23b(8 + j), PSspan(4 * g, 4), AF.Silu)
            tset = [dict(P1=TMPv(0, F32, 512), P2=TMPv(2048, F32, 512), BT=TMPv(4096, F32, 512),
                         Q1=TMPv(6144, F32, 512), Q2=TMPv(8192, F32, 512)),
                    dict(P1=R1v(22528, F32, 512), P2=R1v(24576, F32, 512), BT=R1v(26624, F32, 512),
                         Q1=R1v(28672, F32, 512), Q2=R1v(30720, F32, 512))]
            SBW = 258
            SBs = [TMPv(12288, BF16, 4 * 2 * SBW), TMPv(12288 + 4 * 2 * SBW * 2, BF16, 4 * 2 * SBW)]
            memset(SBs[0], 0.0); memset(SBs[1], 0.0)
            XBc = R1v(0, BF16, T * 2 * 128)
            CSc = R1v(4096, BF16, T * 2 * 128)
            KBc = R1v(8192, BF16, T * 128)
            TAB = R1v(10240, F32, 4 * 768)
            pcount = 0

            def c_load_x(ut):
                dma('sp', XBc, xb_s[i, ut], ('c5', 0), rkeys=[('xb_s', i)])
                for pr in range(4):
                    dma('sp', TAB[pr * 768:(pr + 1) * 768], tab_s[i, ut][:, pr * 768:(pr + 1) * 768], ('c5', 3, pr),
                        rkeys=[('tab_s', i, 0, ut), ('tab_s', i, 256, ut), ('tab_s', i, 512, ut)])

            def c_load_y(ut):
                dma('sp', CSc, cs_s[i, ut], ('c5', 1), rkeys=[('cs_s', i)])
                dma('sp', KBc, kb_s[i, ut], ('c5', 2), rkeys=[('kb_s', i)])

            def c_xmm(ut):
                ub = R23b(ut)
                for ri in range(2):
                    for k in range(T):
                        for pr in range(4):
                            bank = 4 + pr
                            tp = (96, 0) if pr == 3 else None
                            o_ = (k * 2 + ri) * 128
                            lhs = XBc.v(XBc.ap[32 * pr:32 * pr + 32, o_:o_ + 128])
                            rhs = ub.v(ub.ap[32 * pr:32 * pr + 32, k * NCH:(k + 1) * NCH])
                            mm(PS(bank, ri * NCH, (ri + 1) * NCH), lhs, rhs, k == 0, k == T - 1, tile_position=tp)

            c_load_x(0); c_load_y(0); c_xmm(0)
            for ut in range(8):
                SB = SBs[ut % 2]
                ub = R23b(ut)
                for pr in range(4):
                    gp = ut * 4 + pr
                    bank = 4 + pr
                    ts_ = tset[pcount % 2]; pcount += 1
                    P1 = ts_['P1']; P2 = ts_['P2']; BT = ts_['BT']; Q1 = ts_['Q1']; Q2 = ts_['Q2']
                    tb = TAB[pr * 768:(pr + 1) * 768]
                    tt(P1, PS(bank), tb[0:512], ALU.mult)
                    tt(P2, PS(bank), tb[256:768], ALU.mult)
                    tt(BT[0:256], P1[0:256], P1[256:512], ALU.add, 'pool')
                    tt(BT[256:512], P2[256:512], P2[0:256], ALU.subtract, 'pool')
                    dcol = dec[i * 32 + gp: i * 32 + gp + 1]
                    dbc = dcol.v(dcol.ap.to_broadcast([128, NCH]))
                    for ri in range(2):
                        S.op('dve', (lambda o, d0, d1: lambda e: e.tensor_tensor_scan(
                            out=o.ap, data0=d0.ap, data1=d1.ap, initial=0.0, op0=ALU.mult, op1=ALU.add))(
                            PS(bank, ri * NCH, (ri + 1) * NCH), dbc, BT[ri * 256:(ri + 1) * 256]),
                            _k(dbc, BT[ri * 256:(ri + 1) * 256]), PS(bank).keys)
                    tt(Q1, PS(bank), tb[0:512], ALU.mult)
                    tt(Q2, PS(bank), tb[256:768], ALU.mult)
                    sre = SB[(pr * 2 + 0) * SBW + 1:(pr * 2 + 0) * SBW + 1 + NCH]
                    sim = SB[(pr * 2 + 1) * SBW + 1:(pr * 2 + 1) * SBW + 1 + NCH]
                    tt(sre, Q1[0:256], Q1[256:512], ALU.subtract, 'pool')
                    tt(sim, Q2[0:256], Q2[256:512], ALU.add, 'pool')
                if ut + 1 < 8:
                    c_load_x(ut + 1)
                    c_xmm(ut + 1)
                for k in range(T):
                    bank = k // 2
                    c0 = (k % 2) * NCH
                    for kp in range(k + 1):
                        lhs = KBc[(k - kp) * 128:(k - kp + 1) * 128]
                        rhs = ub[kp * NCH:(kp + 1) * NCH]
                        mm(PS(bank, c0, c0 + NCH), lhs, rhs, kp == 0, False)
                    for pr in range(4):
                        for ri in range(2):
                            o_ = (k * 2 + ri) * 128 + 32 * pr
                            lhs = CSc[o_:o_ + 32]
                            rhs = SB[(pr * 2 + ri) * SBW:(pr * 2 + ri) * SBW + NCH]
                            tp = (0, 96) if pr == 3 else None
                            mm(PS(bank, c0, c0 + NCH, 32 * pr, 32 * pr + 32), lhs, rhs, False,
                               (ri == 1), tile_position=tp)
                psy = PSspan(0, 4)
                psy_perm = PSv(psy.ap.rearrange("p (k c) -> p c k", k=T), psy.keys)
                outv = ub.v(ub.ap.rearrange("p (c k) -> p c k", k=T))
                act(outv, psy_perm, AF.Gelu_apprx_tanh)
                if ut + 1 < 8:
                    c_load_y(ut + 1)
            SGf = TMPv(0, F32, 2048); TT = TMPv(8192, F32, 2048)
            for j in range(8):
                gg = nextg(); proj([wget()], lambda kt, c: R23b(kt, c * 512, (c + 1) * 512), 8, gg)
                act(SGf, PSspan(4 * gg, 4), AF.Sigmoid, bias=pcol(PV_BGLU + i * 16 + 8 + j))
                gv = nextg(); proj([wget()], lambda kt, c: R23b(kt, c * 512, (c + 1) * 512), 8, gv)
                stt(TT, PSspan(4 * gv, 4), pcol(PV_BGLU + i * 16 + j), SGf, ALU.add, ALU.mult)
                tt(R1b(j), TT, R23b(8 + j), ALU.mult)
            for o in range(8):
                g = nextg()
                proj([wget()], lambda kt, c: R1b(kt, c * 512, (c + 1) * 512), 8, g)
                residual_add(o, g)
                tail_stats(o)

        for n_, l in enumerate(layers):
            if l % 2 == 0:
                even_layer(l, pre=(n_ > 0))
            else:
                odd_layer(l, pre=(n_ > 0))

        if do_final:
            rmsnorm_to(PV_FINAL, lambda ft: HT(ft), pre=(len(layers) > 0))
        ost = [R1v(0, F32, 1024), R1v(4096, F32, 1024)]
        for tt_ in range(16):
            b0 = (tt_ % 4) * 2
            for ft in range(8):
                tr(PS(b0 + ft // 4, (ft % 4) * 128, (ft % 4 + 1) * 128), HT(ft, tt_ * 128, (tt_ + 1) * 128))
            ob = ost[tt_ % 2]
            if tt_ % 2 == 0:
                act(ob, PSspan(b0, 2), AF.Copy)
            else:
                cp(ob, PSspan(b0, 2))
            dma('sp', out_d[tt_ * 128:(tt_ + 1) * 128, :], ob, ('o', tt_ % 2), out_is_dram=True, wkeys=[('out', tt_)])
        S.op('sp', lambda e: None, [('out', t_) for t_ in range(16)], [])
        assert wstate['next'] == len(wq), (wstate['next'], len(wq))
        S.emit(nc, st)
    return nc


def _tile_w(w, ncol_tiles=None):
    K, N = w.shape
    kb = K // 1024
    a = w.reshape(kb, 8, 128, N // 128, 128)
    a = a.transpose(3, 0, 2, 1, 4)
    return np.ascontiguousarray(a).reshape(N // 128, kb, 128, 1024)


def prep_inputs(inp):
    f = lambda a: np.asarray(a, dtype=np.float32)
    sc_w_in = f(inp["sc_w_in"]); sc_w_out = f(inp["sc_w_out"])
    ev_w_in = f(inp["ev_w_in"]); ev_w_out = f(inp["ev_w_out"])
    glu = f(inp["s5_w_glu"]); pw = f(inp["pool_w"])
    w_sc_in = np.zeros((2, 16, 4, 128, 1024), np.float32)
    w_sc_out = np.zeros((2, 8, 2, 128, 1024), np.float32)
    w_ev_in = np.zeros((2, 32, 128, 1024), np.float32)
    w_glu = np.zeros((2, 16, 128, 1024), np.float32)
    w_ev_out = np.zeros((2, 2, 8, 128, 1024), np.float32)
    w_pool = np.zeros((2, 2, 128, 1024), np.float32)
    order = (0, 2, 3, 1)
    for i in range(2):
        t = _tile_w(sc_w_in[i])[:, 0]
        for j in range(16):
            for w_, blk in enumerate(order):
                w_sc_in[i][j, w_] = t[blk * 16 + j]
        w_sc_out[i] = _tile_w(sc_w_out[i])
        w_ev_in[i] = _tile_w(ev_w_in[i])[:, 0]
        w_glu[i] = _tile_w(glu[i])[:, 0]
        t = _tile_w(ev_w_out[i])
        w_ev_out[i] = t.transpose(1, 0, 2, 3)
        a = pw[i].reshape(4, 2, 128, 2, 128)
        a = a.transpose(0, 3, 2, 1, 4).reshape(8, 128, 256)
        a = a.reshape(2, 4, 128, 256).transpose(0, 2, 1, 3).reshape(2, 128, 1024)
        w_pool[i] = a
    pvec = np.zeros((128, PV_N), np.float32)
    colT = lambda v: np.asarray(v, np.float32).reshape(-1, 128).T
    for l in range(4):
        pvec[:, PV_NORM + l * 8: PV_NORM + l * 8 + 8] = colT(inp["norm_g"][l])
    pvec[:, PV_FINAL:PV_FINAL + 8] = colT(inp["final_g"])
    for i in range(2):
        pvec[:, PV_S5D + i * 8: PV_S5D + i * 8 + 8] = colT(inp["s5_d"][i])
        pvec[:, PV_BGLU + i * 16: PV_BGLU + i * 16 + 16] = colT(inp["s5_b_glu"][i])
        pvec[:, PV_PSCALE + i * 8: PV_PSCALE + i * 8 + 8] = colT(inp["pool_scale"][i])
        for k in range(3):
            pvec[:, PV_CONVW + i * 48 + k * 16: PV_CONVW + i * 48 + k * 16 + 16] = colT(inp["sc_conv_w"][i][k])
        pvec[:, PV_CONVB + i * 16: PV_CONVB + i * 16 + 16] = colT(inp["sc_conv_b"][i])
    cst = np.zeros((128, C_N), np.float32)
    cst[:, C_IDENT:C_IDENT + 128] = np.eye(128, dtype=np.float32)
    r = np.arange(128)
    cst[:, C_BD32:C_BD32 + 128] = (r[:, None] // 32 == r[None, :] // 32)
    cst[:, C_MASKE + 0] = ((r // 16) % 2 == 0)
    cst[:, C_MASKE + 1] = ((r // 16) % 2 == 1)
    cst[:, C_MASKN + 0] = -cst[:, C_MASKE + 0]
    cst[:, C_MASKN + 1] = -cst[:, C_MASKE + 1]
    for wi, w_ in enumerate(POOL_W):
        cnt = np.minimum(np.arange(16) + 1, w_)
        cst[:, C_INVC + wi * 16: C_INVC + wi * 16 + 16] = np.float32(1.0) / cnt.astype(np.float32)
    s5q = np.zeros((2, 128, 1120), np.float32)
    s5h = np.zeros((2, 128, 2560), np.float32)
    for i in range(2):
        are = f(inp["s5_a_re"][i]); aim = f(inp["s5_a_im"][i]); ldt = f(inp["s5_log_dt"][i])
        bre = f(inp["s5_b_re"][i]); bim = f(inp["s5_b_im"][i])
        cre = f(inp["s5_c_re"][i]); cim = f(inp["s5_c_im"][i])
        qa = lambda a: a.reshape(32, 2, 64).transpose(1, 2, 0).reshape(128, 32)
        s5q[i, :, 0:32] = qa(are); s5q[i, :, 32:64] = qa(aim)
        s5q[i, :, 64:96] = qa(np.broadcast_to(ldt[:, None], (64, 64)))
        qb = lambda b: b.reshape(32, 2, 64, 16).transpose(1, 2, 0, 3).reshape(128, 512)
        s5q[i, :, 96:608] = qb(bre); s5q[i, :, 608:1120] = qb(bim)
        ha = lambda a: np.broadcast_to(a.reshape(8, 8, 1, 64), (8, 8, 16, 64)).transpose(1, 2, 0, 3).reshape(128, 512)
        s5h[i, :, 0:512] = ha(are); s5h[i, :, 512:1024] = ha(aim)
        s5h[i, :, 1024:1536] = ha(np.broadcast_to(ldt[:, None], (64, 64)))
        hc = lambda c: c.reshape(8, 8, 16, 64).transpose(1, 2, 0, 3).reshape(128, 512)
        s5h[i, :, 1536:2048] = hc(cre); s5h[i, :, 2048:2560] = hc(cim)
    shared = dict(pvec=pvec, cst=cst, s5q=s5q, s5h=s5h)
    for i in range(2):
        shared["w_sc_in%d" % i] = w_sc_in[i]; shared["w_sc_out%d" % i] = w_sc_out[i]
        shared["w_ev_in%d" % i] = w_ev_in[i]; shared["w_glu%d" % i] = w_glu[i]
        shared["w_ev_out%d" % i] = w_ev_out[i]; shared["w_pool%d" % i] = w_pool[i]
    return shared


def needed_keys(layers):
    ks = ["pvec", "cst", "s5q", "s5h"]
    for l in layers:
        i = l // 2
        if l % 2 == 1:
            ks += ["w_sc_in%d" % i, "w_sc_out%d" % i]
        else:
            ks += ["w_ev_in%d" % i, "w_glu%d" % i, "w_ev_out%d" % i, "w_pool%d" % i]
    return ks


_PROG = {}


def kernel(**inputs):
    x = np.asarray(inputs["x"], dtype=np.float32)
    shared = prep_inputs(inputs)
    key = "full"
    if key not in _PROG:
        _PROG[key] = build_program()
    nc = _PROG[key]
    in_maps = []
    for b in range(N_CORES):
        d = dict(shared)
        d["x"] = np.ascontiguousarray(x[b])
        in_maps.append(d)
    res = run_bass_kernel_spmd(nc, in_maps, core_ids=list(range(N_CORES)))
    out = np.stack([np.asarray(r["out"], dtype=np.float32) for r in res.results], axis=0)
    return out
```

```python
import contextlib
import numpy as np
import concourse.bass as bass
import concourse.mybir as mybir
from concourse.bass_utils import run_bass_kernel_spmd

F32 = mybir.dt.float32
BF16 = mybir.dt.bfloat16
AF = mybir.ActivationFunctionType
ALU = mybir.AluOpType

L_SEQ = 2048
T = 8
NCH = L_SEQ // T
POOL_W = (2, 4, 8, 16)
N_CORES = 8
STRICT_SAME_ENGINE = True

PV_NORM = 0
PV_FINAL = 32
PV_S5D = 40
PV_BGLU = 56
PV_PSCALE = 88
PV_CONVW = 104
PV_CONVB = 200
PV_N = 232
C_IDENT = 0
C_BD32 = 128
C_MASKE = 256
C_MASKN = 258
C_INVC = 260
C_N = 260 + 64


class Op:
    __slots__ = ('eng', 'fn', 'deps', 'sig', 'seq', 'dsem', 'dval', 'idx')

    def __init__(self, eng, fn):
        self.eng = eng; self.fn = fn; self.deps = []; self.sig = False
        self.seq = None; self.dsem = None; self.dval = None; self.idx = None


class Sched:
    ENGS = ('pe', 'act', 'dve', 'pool', 'sp')

    def __init__(self):
        self.ops = {e: [] for e in self.ENGS}
        self.lastw = {}
        self.readers = {}
        self.dcount = {}

    def _add_reader(self, key, o):
        d = self.readers.setdefault(key, {})
        if o.dsem is not None:
            d.setdefault('dma', []).append(o)
        else:
            d[o.eng] = o

    def op(self, eng, fn, reads=(), writes=(), dma=None):
        o = Op(eng, fn)
        if dma is not None:
            c = self.dcount.get(dma, 0) + 1
            self.dcount[dma] = c
            o.dsem = dma; o.dval = 16 * c
        deps = []
        for r in reads:
            w = self.lastw.get(r)
            if w is not None:
                deps.append((w, True))
        for k in writes:
            w = self.lastw.get(k)
            if w is not None:
                deps.append((w, False))
            rd = self.readers.get(k)
            if rd:
                for kk, v in rd.items():
                    if kk == 'dma':
                        deps.extend((x, False) for x in v)
                    else:
                        deps.append((v, False))
        best = {}
        dd = []
        for d, raw in deps:
            if d is o:
                continue
            if d.dsem is not None:
                if d not in dd:
                    dd.append(d)
            else:
                if eng == 'pe' and d.eng == 'pe' and o.dsem is None:
                    continue
                if (not STRICT_SAME_ENGINE) and d.eng == eng and o.dsem is None and not raw:
                    continue
                b = best.get(d.eng)
                if b is None or d.idx > b.idx:
                    best[d.eng] = d
        for d in best.values():
            d.sig = True
        o.deps = list(best.values()) + dd
        o.idx = len(self.ops[eng])
        self.ops[eng].append(o)
        for r in reads:
            self._add_reader(r, o)
        for k in writes:
            self.lastw[k] = o
            self.readers[k] = {}
        return o

    def emit(self, nc, stack):
        esem = {}
        for e in self.ENGS:
            esem[e] = stack.enter_context(nc.semaphore("es_" + e))
        dsem = {}
        for n, k in enumerate(self.dcount):
            dsem[k] = stack.enter_context(nc.semaphore("ds_%d" % n))
        for e in self.ENGS:
            c = 0
            for o in self.ops[e]:
                if o.dsem is None and o.sig:
                    c += 1; o.seq = c

        def run(e, engobj):
            waited = {}
            for o in self.ops[e]:
                for d in o.deps:
                    if d.dsem is not None:
                        key = ('d', d.dsem); sem = dsem[d.dsem]; val = d.dval
                    else:
                        key = ('e', d.eng); sem = esem[d.eng]; val = d.seq
                    if waited.get(key, 0) < val:
                        engobj.wait_ge(sem, val)
                        waited[key] = val
                inst = o.fn(engobj)
                if inst is None:
                    continue
                if o.dsem is not None:
                    inst.then_inc(dsem[o.dsem], 16)
                elif o.sig:
                    inst.then_inc(esem[e], 1)

        block = stack.enter_context(nc.Block())

        @block.tensor
        def _(t): run('pe', t)

        @block.scalar
        def _(t): run('act', t)

        @block.vector
        def _(t): run('dve', t)

        @block.gpsimd
        def _(t): run('pool', t)

        @block.sync
        def _(t): run('sp', t)


class Buf:
    def __init__(self, name, tens, off, dtype, n, ap=None):
        self.name = name; self.tens = tens; self.off = off; self.dtype = dtype; self.n = n
        self.esz = 4 if dtype == F32 else 2
        size = n * self.esz
        assert off % 4 == 0 and size % 4 == 0, (off, size)
        self.keys = [(name, b) for b in range(off // 1024, (off + size - 1) // 1024 + 1)]
        if ap is None:
            ap = tens[:, off // 4:(off + size) // 4]
            if dtype != F32:
                ap = ap.bitcast(dtype)
        self.ap = ap

    def __getitem__(self, s):
        a = 0 if s.start is None else s.start
        b = self.n if s.stop is None else s.stop
        assert s.step is None and 0 <= a < b <= self.n, (a, b, self.n)
        if (a * self.esz) % 4 == 0 and ((b - a) * self.esz) % 4 == 0:
            return Buf(self.name, self.tens, self.off + a * self.esz, self.dtype, b - a)
        o = Buf.__new__(Buf)
        o.name = self.name; o.tens = self.tens; o.off = self.off; o.dtype = self.dtype
        o.n = b - a; o.esz = self.esz
        lo = self.off + a * self.esz; hi = self.off + b * self.esz
        o.keys = [(self.name, k) for k in range(lo // 1024, (hi - 1) // 1024 + 1)]
        o.ap = self.ap[:, a:b]
        return o

    def v(self, ap):
        o = Buf.__new__(Buf)
        o.name = self.name; o.tens = self.tens; o.off = self.off; o.dtype = self.dtype
        o.n = self.n; o.esz = self.esz; o.keys = self.keys; o.ap = ap
        return o


class PSv:
    def __init__(self, ap, keys):
        self.ap = ap; self.keys = keys


def _k(*objs):
    out = []
    for o in objs:
        if o is None or isinstance(o, (int, float)):
            continue
        out.extend(o.keys)
    return out


def build_program(layers=(0, 1, 2, 3), do_final=True, n_setup=None):
    nc = bass.Bass("TRN2", target_bir_lowering=False)
    S = Sched()
    st = contextlib.ExitStack()

    def din(name, shape, dt=F32):
        return nc.dram_tensor(name, list(shape), dt, kind="ExternalInput").ap()

    x_d = din("x", [2048, 1024])
    need_i = sorted(set(l // 2 for l in layers))
    w_sc_in = {}; w_sc_out = {}; w_ev_in = {}; w_glu = {}; w_ev_out = {}; w_pool = {}
    for l in layers:
        i = l // 2
        if l % 2 == 1:
            w_sc_in[i] = din("w_sc_in%d" % i, [16, 4, 128, 1024])
            w_sc_out[i] = din("w_sc_out%d" % i, [8, 2, 128, 1024])
        else:
            w_ev_in[i] = din("w_ev_in%d" % i, [32, 128, 1024])
            w_glu[i] = din("w_glu%d" % i, [16, 128, 1024])
            w_ev_out[i] = din("w_ev_out%d" % i, [2, 8, 128, 1024])
            w_pool[i] = din("w_pool%d" % i, [2, 128, 1024])
    pvec_d = din("pvec", [128, PV_N])
    cst_d = din("cst", [128, C_N])
    s5q_d = din("s5q", [2, 128, 1120])
    s5h_d = din("s5h", [2, 128, 2560])
    out_d = nc.dram_tensor("out", [2048, 1024], F32, kind="ExternalOutput").ap()
    xb_s = nc.dram_tensor("xb_s", [2, 8, 128, T * 2 * 128], BF16, kind="Internal").ap()
    cs_s = nc.dram_tensor("cs_s", [2, 8, 128, T * 2 * 128], BF16, kind="Internal").ap()
    kb_s = nc.dram_tensor("kb_s", [2, 8, 128, T * 128], BF16, kind="Internal").ap()
    tab_s = nc.dram_tensor("tab_s", [2, 8, 128, 4 * 512], F32, kind="Internal").ap()

    with st:
        def sb(name, nwords):
            return st.enter_context(nc.sbuf_tensor(name, [128, nwords], F32))
        HT_t = sb("HT", 16384)
        R1_t = sb("R1", 8192)
        R23_t = sb("R23", 16384)
        WB_t = sb("WB", 4096)
        TMP_t = sb("TMP", 5184)
        MISC_t = sb("MISC", 1024)
        assert PV_N + C_N + 256 + 160 <= 1024
        ps_t = st.enter_context(nc.psum_tensor("ps", [128, 4096], F32))

        def HT(ft, c0=0, c1=2048):
            return Buf("HT", HT_t, (ft * 2048 + c0) * 4, F32, c1 - c0)

        def R1b(tile, c0=0, c1=2048):
            return Buf("R1", R1_t, (tile * 2048 + c0) * 2, BF16, c1 - c0)

        def R23b(tile, c0=0, c1=2048):
            return Buf("R23", R23_t, (tile * 2048 + c0) * 2, BF16, c1 - c0)

        def R1v(off, dtype, n): return Buf("R1", R1_t, off, dtype, n)
        def R23v(off, dtype, n): return Buf("R23", R23_t, off, dtype, n)
        def TMPv(off, dtype, n): return Buf("TMP", TMP_t, off, dtype, n)
        def MISCv(off, dtype, n): return Buf("MISC", MISC_t, off, dtype, n)

        def PS(bank, c0=0, c1=512, p0=0, p1=128):
            return PSv(ps_t[p0:p1, bank * 512 + c0: bank * 512 + c1], [('ps', bank)])

        def PSspan(b0, nb):
            return PSv(ps_t[:, b0 * 512:(b0 + nb) * 512], [('ps', b) for b in range(b0, b0 + nb)])

        pvec = MISCv(0, F32, PV_N)
        cst = MISCv(PV_N * 4, F32, C_N)
        dec = MISCv((PV_N + C_N) * 4, F32, 64)
        utp = MISCv((PV_N + C_N + 64) * 4, F32, 128)
        onesb = MISCv((PV_N + C_N + 192) * 4, BF16, 128)
        sm = MISCv((PV_N + C_N + 256) * 4, F32, 160)

        def pcol(c): return pvec[c:c + 1]
        ident = cst[C_IDENT:C_IDENT + 128]
        bd32 = cst[C_BD32:C_BD32 + 128]

        uniq = [0]

        def uchan():
            uniq[0] += 1
            return ('u', uniq[0])

        def dma(eng, out, in_ap_or_buf, chan, out_is_dram=False, rkeys=(), wkeys=()):
            if chan == 'par':
                chan = uchan()
            if out_is_dram:
                src = in_ap_or_buf
                S.op(eng, lambda e: e.dma_start(out=out, in_=src.ap), list(src.keys) + list(rkeys), list(wkeys), dma=chan)
            else:
                src = in_ap_or_buf
                S.op(eng, lambda e: e.dma_start(out=out.ap, in_=src), list(rkeys), list(out.keys) + list(wkeys), dma=chan)

        def act(out, in_, func, bias=None, scale=None):
            kw = {}
            rk = _k(in_, bias, scale)
            if bias is not None:
                kw['bias'] = bias if isinstance(bias, (int, float)) else bias.ap
            if scale is not None:
                kw['scale'] = scale if isinstance(scale, (int, float)) else scale.ap
            S.op('act', lambda e: e.activation(out=out.ap, in_=in_.ap, func=func, **kw), rk, out.keys)

        def tt(out, a, b, op, eng='dve'):
            S.op(eng, lambda e: e.tensor_tensor(out=out.ap, in0=a.ap, in1=b.ap, op=op), _k(a, b), out.keys)

        def stt(out, in0, scalar, in1, op0, op1, eng='dve'):
            sc = scalar if isinstance(scalar, (int, float)) else scalar.ap
            S.op(eng, lambda e: e.scalar_tensor_tensor(out=out.ap, in0=in0.ap, scalar=sc, in1=in1.ap, op0=op0, op1=op1),
                 _k(in0, scalar, in1), out.keys)

        def ts(out, in0, s1, s2, op0, op1=None, eng='dve'):
            a1 = s1 if isinstance(s1, (int, float)) else s1.ap
            a2 = None if s2 is None else (s2 if isinstance(s2, (int, float)) else s2.ap)
            kw = {} if op1 is None else {'op1': op1}
            S.op(eng, lambda e: e.tensor_scalar(out=out.ap, in0=in0.ap, scalar1=a1, scalar2=a2, op0=op0, **kw),
                 _k(in0, s1, s2), out.keys)

        def cp(out, in_, eng='dve'):
            S.op(eng, lambda e: e.tensor_copy(out=out.ap, in_=in_.ap), _k(in_), out.keys)

        def memset(out, val, eng='dve'):
            S.op(eng, lambda e: e.memset(out.ap, val), [], out.keys)

        def recip(out, in_):
            S.op('dve', lambda e: e.reciprocal(out=out.ap, in_=in_.ap), _k(in_), out.keys)

        def mm(out, lhsT, rhs, start, stop, tile_position=None):
            kw = {}
            if tile_position is not None:
                kw['tile_position'] = tile_position
            S.op('pe', lambda e: e.matmul(out.ap, lhsT=lhsT.ap, rhs=rhs.ap, start=start, stop=stop, **kw),
                 _k(lhsT, rhs), out.keys)

        def tr(out, in_):
            S.op('pe', lambda e: e.transpose(out.ap, in_.ap, ident.ap), _k(in_, ident), out.keys)

        NSLOT = 8
        wq = []
        wstate = {'issued': 0, 'next': 0}
        AHEAD = 6

        def wslot(n):
            return Buf("WB", WB_t, (n % NSLOT) * 2048, BF16, 1024)

        def w_issue_upto(n):
            while wstate['issued'] <= n and wstate['issued'] < len(wq):
                i = wstate['issued']
                dma('pool', wslot(i), wq[i], ('w', i % NSLOT))
                wstate['issued'] += 1

        def wget():
            n = wstate['next']
            wstate['next'] += 1
            w_issue_upto(n + AHEAD)
            return wslot(n)

        for l in layers:
            i = l // 2
            if l % 2 == 0:
                for j in range(8):
                    wq.append(w_ev_in[i][16 + j]); wq.append(w_ev_in[i][24 + j])
                for o in range(8):
                    wq.append(w_ev_out[i][1, o])
                for j in range(8):
                    wq.append(w_ev_in[i][j])
                for j in range(8):
                    wq.append(w_ev_in[i][8 + j])
                for j in range(8):
                    wq.append(w_glu[i][8 + j]); wq.append(w_glu[i][j])
                for o in range(8):
                    wq.append(w_ev_out[i][0, o])
            else:
                for j in range(16):
                    for w in range(4):
                        wq.append(w_sc_in[i][j, w])
                for o in range(8):
                    wq.append(w_sc_out[i][o, 0]); wq.append(w_sc_out[i][o, 1])

        dma('sp', pvec, pvec_d, 'par')
        dma('sp', cst, cst_d, 'par')
        memset(onesb, 1.0)
        w_issue_upto(AHEAD - 1)

        xs = [R1v(s_ * 16384, F32, 4096) for s_ in range(2)]
        for c in range(4):
            xsb = xs[c % 2]
            S.op('sp', (lambda xsb, c: lambda e: e.dma_start(
                out=xsb.ap.rearrange("p (t f) -> p t f", t=4),
                in_=x_d[c * 512:(c + 1) * 512, :].rearrange("(t p) f -> p t f", p=128)))(xsb, c),
                [], xsb.keys, dma=('x', c % 2))
            for ft in range(8):
                for tl in range(4):
                    tr(PS(ft, tl * 128, (tl + 1) * 128), xsb[tl * 1024 + ft * 128: tl * 1024 + (ft + 1) * 128])
                if ft % 2 == 0:
                    act(HT(ft, c * 512, (c + 1) * 512), PS(ft), AF.Copy)
                else:
                    cp(HT(ft, c * 512, (c + 1) * 512), PS(ft))

        def cmul2(o_re, o_im, a_re, a_im, b_re, b_im, t1, t2, t3, t4):
            tt(t1, a_re, b_re, ALU.mult); tt(t2, a_im, b_im, ALU.mult)
            tt(o_re, t1, t2, ALU.subtract)
            tt(t3, a_re, b_im, ALU.mult, 'pool'); tt(t4, a_im, b_re, ALU.mult, 'pool')
            tt(o_im, t3, t4, ALU.add, 'pool')

        def csq2(o_re, o_im, a_re, a_im, t1):
            tt(o_re, a_re, a_re, ALU.mult); tt(t1, a_im, a_im, ALU.mult); tt(o_re, o_re, t1, ALU.subtract)
            tt(o_im, a_re, a_im, ALU.mult, 'pool'); ts(o_im, o_im, 2.0, 0.0, ALU.mult, ALU.add, eng='pool')

        def discretise(lam_re, lam_im, logdt, n, alloc):
            dtv = alloc(n); act(dtv, logdt, AF.Exp)
            prd = alloc(n); tt(prd, lam_re, dtv, ALU.mult)
            mag = alloc(n); act(mag, prd, AF.Exp)
            ang = alloc(n); tt(ang, lam_im, dtv, ALU.mult)
            c = alloc(n); s_ = alloc(n); c2 = alloc(n); s2 = alloc(n); t1 = alloc(n); hp = alloc(n)
            ts(hp, ang, -1.0 / 8.0, float(np.pi / 2), ALU.mult, ALU.add)
            act(c, hp, AF.Sin)
            act(s_, ang, AF.Sin, scale=1.0 / 8.0)
            for _ in range(3):
                csq2(c2, s2, c, s_, t1)
                c, c2 = c2, c; s_, s2 = s2, s_
            lb_re = alloc(n); lb_im = alloc(n)
            tt(lb_re, mag, c, ALU.mult); tt(lb_im, mag, s_, ALU.mult, 'pool')
            return dict(mag=mag, c1=c, s1=s_, lb_re=lb_re, lb_im=lb_im, dtv=dtv, prd=prd)

        evac_rr = [0]

        def evac(dst, src):
            evac_rr[0] ^= 1
            if evac_rr[0]:
                act(dst, src, AF.Copy)
            else:
                cp(dst, src)

        def s5_setup(i, first_setup=True):
            cur = [0]

            def alloc23(n, dtype=F32):
                b_ = R23v(cur[0], dtype, n); cur[0] += n * (4 if dtype == F32 else 2)
                assert cur[0] <= 65536, cur[0]
                return b_
            cs0 = alloc23(2048)
            hin = alloc23(2560)
            dma('sp' if first_setup else 'act', hin, s5h_d[i], 'par')
            lamre, lamim, ldt, cre, cim = [hin[k * 512:(k + 1) * 512] for k in range(5)]
            d = discretise(lamre, lamim, ldt, 512, alloc23)
            zb = [[alloc23(512), alloc23(512)], [alloc23(512), alloc23(512)]]
            t3 = alloc23(512); t4 = alloc23(512)
            assert cur[0] == 55296, cur[0]
            t1 = TMPv(16384, F32, 512); t2 = TMPv(18432, F32, 512)
            cste = [[TMPv(0, F32, 1024), TMPv(4096, F32, 1024)], [TMPv(8192, F32, 1024), TMPv(12288, F32, 1024)]]
            csst = R1v(0, BF16, 8 * T * 2 * 128)
            qs = {}

            def q_small():
                allocq = alloc23
                qin = allocq(1120)
                dma('sp' if first_setup else 'act', qin, s5q_d[i], 'par')
                qlre = qin[0:32]; qlim = qin[32:64]; qldt = qin[64:96]
                bqre = qin[96:608]; bqim = qin[608:1120]
                dq = discretise(qlre, qlim, qldt, 32, allocq)
                den = allocq(32); q1 = allocq(32); q2 = allocq(32); fre = allocq(32); fim = allocq(32)
                lm1 = allocq(32)
                tt(q1, qlre, qlre, ALU.mult); tt(q2, qlim, qlim, ALU.mult); tt(den, q1, q2, ALU.add)
                recip(den, den)
                ts(lm1, dq['lb_re'], -1.0, None, ALU.add)
                tt(q1, lm1, qlre, ALU.mult); tt(q2, dq['lb_im'], qlim, ALU.mult); tt(q1, q1, q2, ALU.add)
                tt(fre, q1, den, ALU.mult)
                tt(q1, dq['lb_im'], qlre, ALU.mult); tt(q2, lm1, qlim, ALU.mult); tt(q1, q1, q2, ALU.subtract)
                tt(fim, q1, den, ALU.mult)
                act(dec[i * 32:(i + 1) * 32], dq['prd'], AF.Exp, scale=float(T))
                uc = allocq(32); us = allocq(32); uc2 = allocq(32); us2 = allocq(32)
                cp(uc, dq['c1']); cp(us, dq['s1'])
                nsq = {8: 3, 16: 4}[T]
                for _ in range(nsq):
                    csq2(uc2, us2, uc, us, q1)
                    uc, uc2 = uc2, uc; us, us2 = us2, us
                cp(utp[i * 64:i * 64 + 32], uc); cp(utp[i * 64 + 32:i * 64 + 64], us)
                qs.update(dq=dq, fre=fre, fim=fim, bqre=bqre, bqim=bqim, allocq=allocq)
            if first_setup:
                q_small()
            z = [cre, cim]
            for k in range(-1, T):
                if k >= 0:
                    zn = zb[k % 2]
                    cmul2(zn[0], zn[1], z[0], z[1], d['lb_re'], d['lb_im'], t1, t2, t3, t4)
                    z = zn
                ce = cste[(k + 1) % 2]
                for ri in range(2):
                    for e_ in range(2):
                        outv = ce[ri].v(ce[ri].ap.rearrange("p (u e q) -> p u e q", u=8, e=2)[:, :, e_, :])
                        inv = z[ri].v(z[ri].ap.rearrange("p (u q) -> p u q", u=8))
                        mc = (C_MASKE if ri == 0 else C_MASKN) + e_
                        act(outv, inv, AF.Identity, scale=cst[mc:mc + 1])
                for ri in range(2):
                    for hf in range(2):
                        bank = ((k + 1) % 2) * 4 + ri * 2 + hf
                        for u4 in range(4):
                            ut = hf * 4 + u4
                            tr(PS(bank, u4 * 128, (u4 + 1) * 128), ce[ri][ut * 128:(ut + 1) * 128])
                        psv = PSv(PS(bank).ap.rearrange("p (a b) -> p a b", a=4), PS(bank).keys)
                        if k == -1:
                            dst = cs0.v(cs0.ap.rearrange("p (u f) -> p u f", u=8)[:, hf * 4:(hf + 1) * 4, ri * 128:(ri + 1) * 128])
                        else:
                            o_ = (k * 2 + ri) * 128
                            dst = csst.v(csst.ap.rearrange("p (u f) -> p u f", u=8)[:, hf * 4:(hf + 1) * 4, o_:o_ + 128])
                        evac(dst, psv)
            S.op('sp', lambda e: e.dma_start(out=cs_s[i].rearrange("u p f -> p u f"),
                                             in_=csst.ap.rearrange("p (u f) -> p u f", u=8)),
                 csst.keys, [('cs_s', i)], dma=uchan())

            if not first_setup:
                q_small()
            dq = qs['dq']; fre = qs['fre']; fim = qs['fim']; bqre = qs['bqre']; bqim = qs['bqim']; allocq = qs['allocq']
            cur[0] = 8192

            def bc16(b32):
                return b32.v(b32.ap.rearrange("p (g o) -> p g o", o=1).to_broadcast([128, 32, 16]))

            def v3(b512):
                return b512.v(b512.ap.rearrange("p (g h) -> p g h", h=16))
            bb = [[allocq(512), allocq(512)], [allocq(512), allocq(512)]]
            bste = [[allocq(1024), allocq(1024)], [allocq(1024), allocq(1024)]]
            kst = allocq(8 * T * 128, BF16)
            assert cur[0] <= 55296, cur[0]
            b1 = TMPv(0, F32, 512); b2 = TMPv(2048, F32, 512); b3 = TMPv(4096, F32, 512); b4 = TMPv(6144, F32, 512)
            ktmp = TMPv(8192, F32, 512)
            cmul2(v3(bb[0][0]), v3(bb[0][1]), bc16(fre), bc16(fim), v3(bqre), v3(bqim), v3(b1), v3(b2), v3(b3), v3(b4))
            xbst = R1v(0, BF16, 8 * T * 2 * 128)
            memset(bste[0][0], 0.0); memset(bste[0][1], 0.0, eng='pool')
            memset(bste[1][0], 0.0); memset(bste[1][1], 0.0, eng='pool')
            bd4 = bd32.v(bd32.ap.rearrange("p (o c) -> p o c", o=1).to_broadcast([128, 4, 128]))
            curb = bb[0]
            for m in range(T):
                if m >= 1:
                    nb = bb[m % 2]
                    cmul2(v3(nb[0]), v3(nb[1]), bc16(dq['lb_re']), bc16(dq['lb_im']), v3(curb[0]), v3(curb[1]),
                          v3(b1), v3(b2), v3(b3), v3(b4))
                    curb = nb
                be = bste[m % 2]
                for ri in range(2):
                    for e_ in range(2):
                        w0 = be[ri].off // 4
                        dstap = be[ri].tens[e_ * 64:(e_ + 1) * 64, w0:w0 + 1024] \
                            .rearrange("p (g e h) -> p g e h", e=2, h=16)[:, :, e_, :]
                        w1 = curb[ri].off // 4
                        srcap = curb[ri].tens[e_ * 64:(e_ + 1) * 64, w1:w1 + 512] \
                            .rearrange("p (g h) -> p g h", h=16)
                        S.op('act', (lambda da, sa: lambda e: e.activation(out=da, in_=sa, func=AF.Copy))(dstap, srcap),
                             curb[ri].keys, be[ri].keys)
                k = T - 1 - m
                for ri in range(2):
                    for hf in range(2):
                        bank = ri * 2 + hf
                        for u4 in range(4):
                            ut = hf * 4 + u4
                            tr(PS(bank, u4 * 128, (u4 + 1) * 128), be[ri][ut * 128:(ut + 1) * 128])
                        psv = PSv(PS(bank).ap.rearrange("p (a b) -> p a b", a=4), PS(bank).keys)
                        o_ = (k * 2 + ri) * 128
                        dst = xbst.v(xbst.ap.rearrange("p (u f) -> p u f", u=8)[:, hf * 4:(hf + 1) * 4, o_:o_ + 128])
                        evac(dst, psv)
                for hf in range(2):
                    kb = 4 + (m % 2) * 2 + hf
                    for u4 in range(4):
                        ut = hf * 4 + u4
                        for ri in range(2):
                            mm(PS(kb, u4 * 128, (u4 + 1) * 128), be[ri][ut * 128:(ut + 1) * 128],
                               cs0[(ut * 2 + ri) * 128:(ut * 2 + ri + 1) * 128], ri == 0, ri == 1)
                    psv = PSv(PS(kb).ap.rearrange("p (a b) -> p a b", a=4), PS(kb).keys)
                    if m == 0:
                        kt4 = ktmp.v(ktmp.ap.rearrange("p (a b) -> p a b", a=4))
                        tt(kt4, psv, bd4, ALU.mult)
                        for u4 in range(4):
                            ut = hf * 4 + u4
                            ko = (ut * T + m) * 128
                            stt(kst[ko:ko + 128], ident, pcol(PV_S5D + i * 8 + ut), ktmp[u4 * 128:(u4 + 1) * 128],
                                ALU.mult, ALU.add)
                    else:
                        dst = kst.v(kst.ap.rearrange("p (u f) -> p u f", u=8)[:, hf * 4:(hf + 1) * 4, m * 128:(m + 1) * 128])
                        tt(dst, psv, bd4, ALU.mult)
            S.op('sp', lambda e: e.dma_start(out=xb_s[i].rearrange("u p f -> p u f"),
                                             in_=xbst.ap.rearrange("p (u f) -> p u f", u=8)),
                 xbst.keys, [('xb_s', i)], dma=uchan())
            S.op('sp', lambda e: e.dma_start(out=kb_s[i].rearrange("u p f -> p u f"),
                                             in_=kst.ap.rearrange("p (u f) -> p u f", u=8)),
                 kst.keys, [('kb_s', i)], dma=uchan())

            tabc = R1v(0, F32, 32 * NCH)
            tabs = R23v(32768, F32, 32 * NCH)
            twd = [R23v(0, F32, 2048), R23v(8192, F32, 2048)]
            twp = [R23v(16384, F32, 2048), R23v(24576, F32, 2048)]
            pn = [sm[0:32], sm[32:64]]; pn2 = [sm[64:96], sm[96:128]]; sq1 = sm[128:160]

            def t3(b_, g0, g1, n0, n1):
                return b_.v(b_.ap.rearrange("p (g c) -> p g c", c=NCH)[:, g0:g1, n0:n1])

            def bcn(b32, g0, g1, n):
                return b32.v(b32.ap.rearrange("p (g o) -> p g o", o=1)[:, g0:g1, :].to_broadcast([128, g1 - g0, n]))
            Bc = TMPv(0, F32, 512); Bs = TMPv(2048, F32, 512); Ac = TMPv(4096, F32, 512); As = TMPv(6144, F32, 512)

            def tiny(k_):
                return TMPv(8192 + 128 * k_, F32, 32)
            powB = [[tiny(0), tiny(1)], [tiny(2), tiny(3)], [tiny(4), tiny(5)], [tiny(6), tiny(7)]]
            powA = [[tiny(8), tiny(9)], [tiny(10), tiny(11)], [tiny(12), tiny(13)], [tiny(14), tiny(15)]]
            sq1 = tiny(16)
            cp(powB[0][0], utp[i * 64:i * 64 + 32]); cp(powB[0][1], utp[i * 64 + 32:i * 64 + 64])
            for j_ in range(3):
                csq2(powB[j_ + 1][0], powB[j_ + 1][1], powB[j_][0], powB[j_][1], sq1)
            csq2(powA[0][0], powA[0][1], powB[3][0], powB[3][1], sq1)
            for j_ in range(3):
                csq2(powA[j_ + 1][0], powA[j_ + 1][1], powA[j_][0], powA[j_][1], sq1)

            def small_table(tc, ts_, pows):
                def s3(b_, n0, n1):
                    return b_.v(b_.ap.rearrange("p (g c) -> p g c", c=16)[:, :, n0:n1])
                memset(s3(tc, 0, 1), 1.0); memset(s3(ts_, 0, 1), 0.0)
                n_ = 1; lv = 0
                while n_ < 16:
                    pr_, pi_ = pows[lv]
                    w1 = twd[0][0:32 * n_]; w2 = twd[1][0:32 * n_]
                    w1v = w1.v(w1.ap.rearrange("p (g c) -> p g c", c=n_)); w2v = w2.v(w2.ap.rearrange("p (g c) -> p g c", c=n_))
                    tt(w1v, s3(tc, 0, n_), bcn(pr_, 0, 32, n_), ALU.mult); tt(w2v, s3(ts_, 0, n_), bcn(pi_, 0, 32, n_), ALU.mult)
                    tt(s3(tc, n_, 2 * n_), w1v, w2v, ALU.subtract)
                    tt(w1v, s3(tc, 0, n_), bcn(pi_, 0, 32, n_), ALU.mult); tt(w2v, s3(ts_, 0, n_), bcn(pr_, 0, 32, n_), ALU.mult)
                    tt(s3(ts_, n_, 2 * n_), w1v, w2v, ALU.add)
                    n_ *= 2; lv += 1
            small_table(Bc, Bs, powB)
            small_table(Ac, As, powA)
            act(PS(0), Ac, AF.Copy); act(PS(1), As, AF.Copy)
            for g0 in range(0, 32, 8):
                g1 = g0 + 8

                def abro(psb):
                    return PSv(psb.ap.rearrange("p (g a o) -> p g a o", a=16, o=1)[:, g0:g1, :, :].to_broadcast([128, 8, 16, 16]), psb.keys)

                def bbro(bt_):
                    return bt_.v(bt_.ap.rearrange("p (g o b) -> p g o b", o=1, b=16)[:, g0:g1, :, :].to_broadcast([128, 8, 16, 16]))

                def w4(b_):
                    return b_.v(b_.ap.rearrange("p (g a b) -> p g a b", a=16, b=16))

                def o4(tb_):
                    return tb_.v(tb_.ap.rearrange("p (g a b) -> p g a b", a=16, b=16)[:, g0:g1, :, :])
                tt(w4(twd[0]), abro(PS(0)), bbro(Bc), ALU.mult)
                tt(w4(twd[1]), abro(PS(1)), bbro(Bs), ALU.mult)
                tt(o4(tabc), w4(twd[0]), w4(twd[1]), ALU.subtract, 'pool')
                tt(w4(twp[0]), abro(PS(0)), bbro(Bs), ALU.mult)
                tt(w4(twp[1]), abro(PS(1)), bbro(Bc), ALU.mult)
                tt(o4(tabs), w4(twp[0]), w4(twp[1]), ALU.add, 'pool')
            for (src, offs) in ((tabc, (0,)), (tabs, (256,))):
                for o_ in offs:
                    for ut in range(8):
                        S.op('sp', (lambda src, o_, ut: lambda e: e.dma_start(
                            out=tab_s[i, ut].rearrange("p (r w) -> p r w", w=512)[:, :, o_:o_ + NCH],
                            in_=src.ap.rearrange("p (g c) -> p g c", c=NCH)[:, ut * 4:(ut + 1) * 4, :]))(src, o_, ut),
                            src.keys, [('tab_s', i, o_, ut)], dma=uchan())

        setup_layers = [l // 2 for l in layers if l % 2 == 0]
        for n_s, i in enumerate(setup_layers):
            s5_setup(i, first_setup=(n_s == 0))

        SQSUM = TMPv(0, F32, 2048); SQT = TMPv(8192, F32, 2048); SQB = TMPv(16384, BF16, 2048)

        def tail_stats(o):
            if o == 0:
                act(SQSUM, HT(0), AF.Square)
            else:
                act(SQT, HT(o), AF.Square)
                tt(SQB if o == 7 else SQSUM, SQSUM, SQT, ALU.add)

        def rmsnorm_to(gcol0, dst_fn, pre=False):
            if pre:
                for c in range(4):
                    mm(PS(c), onesb, SQB[c * 512:(c + 1) * 512], True, True)
            else:
                sq = [TMPv(0, BF16, 2048), TMPv(4096, BF16, 2048)]
                for ft in range(8):
                    sqb = sq[ft % 2]
                    act(sqb, HT(ft), AF.Square)
                    for c in range(4):
                        mm(PS(c), onesb, sqb[c * 512:(c + 1) * 512], ft == 0, ft == 7)
            rs = TMPv(8192, F32, 2048)
            act(rs, PSspan(0, 4), AF.Ln, bias=1e-6, scale=1.0 / 1024.0)
            act(PSspan(4, 4), rs, AF.Exp, scale=-0.5)
            for ft in range(8):
                stt(dst_fn(ft), HT(ft), pcol(gcol0 + ft), PSspan(4, 4), ALU.mult, ALU.mult)

        def proj(wt, rhs_fn, nk, g):
            for kt in range(nk):
                wbuf = wt[kt // 8]
                lhs = wbuf[(kt % 8) * 128:(kt % 8 + 1) * 128]
                for c in range(4):
                    mm(PS(4 * g + c), lhs, rhs_fn(kt, c), kt == 0, kt == nk - 1)

        grp = [0]

        def nextg():
            g = grp[0]; grp[0] ^= 1
            return g

        def hn(kt, c): return R1b(kt, c * 512, (c + 1) * 512)

        def residual_add(o, g):
            tt(HT(o), PSspan(4 * g, 4), HT(o), ALU.add)

        def odd_layer(l, pre):
            i = l // 2
            rmsnorm_to(PV_NORM + l * 8, lambda ft: R1b(ft), pre=pre)
            V = TMPv(0, F32, 2064)
            ACC = TMPv(8256, F32, 2048)
            SG = TMPv(8256 + 8192, BF16, 2048)
            memset(V[0:16], 0.0)
            Vd = V[16:2064]
            for j in range(16):
                cw = [pcol(PV_CONVW + i * 48 + k * 16 + j) for k in range(3)]
                cb = pcol(PV_CONVB + i * 16 + j)
                g = nextg(); proj([wget()], hn, 8, g)
                act(Vd, PSspan(4 * g, 4), AF.Copy)
                g = nextg(); proj([wget()], hn, 8, g)
                tt(Vd, PSspan(4 * g, 4), Vd, ALU.mult)
                act(ACC, Vd, AF.Identity, bias=cb, scale=cw[2])
                stt(ACC, V[15:2063], cw[1], ACC, ALU.mult, ALU.add)
                stt(ACC, V[14:2062], cw[0], ACC, ALU.mult, ALU.add)
                g = nextg(); proj([wget()], hn, 8, g)
                act(SG, PSspan(4 * g, 4), AF.Silu)
                g = nextg(); proj([wget()], hn, 8, g)
                tt(ACC, PSspan(4 * g, 4), ACC, ALU.mult)
                tt(R23b(j), ACC, SG, ALU.mult)
            for o in range(8):
                g = nextg()
                proj([wget(), wget()], lambda kt, c: R23b(kt, c * 512, (c + 1) * 512), 16, g)
                residual_add(o, g)
                tail_stats(o)

        def even_layer(l, pre):
            i = l // 2
            rmsnorm_to(PV_NORM + l * 8, lambda ft: R1b(ft), pre=pre)
            R3o = 32768
            UB = R23v(R3o + 0, F32, 2064)
            WA = R23v(R3o + 9216, F32, 2064)
            WBf = R23v(R3o + 18432, F32, 2064)
            DFB = [TMPv(0, BF16, 2048), TMPv(4096, BF16, 2048)]
            SGB = [TMPv(8192, BF16, 2048), TMPv(12288, BF16, 2048)]
            T16 = TMPv(16384, F32, 16)
            memset(UB[0:16], 0.0); memset(WA[0:16], 0.0); memset(WBf[0:16], 0.0)
            PW = [R23v(R3o + 27648, BF16, 1024), R23v(R3o + 27648 + 2048, BF16, 1024)]
            for hh in range(2):
                dma('pool', PW[hh], w_pool[i][hh], ('pw', hh))
            for j in range(8):
                gi = j // 2; w = POOL_W[gi]
                g = nextg(); proj([wget()], hn, 8, g)
                act(UB[16:2064], PSspan(4 * g, 4), AF.Copy)
                bufs = [UB, WA, WBf]
                src = UB; sh = 1; nadd = {2: 1, 4: 2, 8: 3, 16: 4}[w]
                dsts = [WA, WBf]
                for a in range(nadd):
                    dst = dsts[a % 2]
                    tt(dst[16:2064], src[16:2064], src[16 - sh:2064 - sh], ALU.add, 'dve' if a in (0, 3) else 'pool')
                    src = dst; sh *= 2
                stt(DFB[j % 2], src[16:2064], 1.0 / w, UB[16:2064], ALU.mult, ALU.subtract)
                tt(T16, src[16:32], cst[C_INVC + gi * 16: C_INVC + gi * 16 + 16], ALU.mult)
                tt(DFB[j % 2][0:16], T16, UB[16:32], ALU.subtract)
                g = nextg(); proj([wget()], hn, 8, g)
                act(SGB[j % 2], PSspan(4 * g, 4), AF.Silu)
                if j % 2 == 1:
                    for dt_ in range(2):
                        g = nextg()
                        pwb = PW[gi // 2]
                        base = ((gi % 2) * 2 + dt_) * 256
                        for kt in range(2):
                            for c in range(4):
                                mm(PS(4 * g + c), pwb[base + kt * 128: base + (kt + 1) * 128],
                                   DFB[kt][c * 512:(c + 1) * 512], kt == 0, kt == 1)
                        o_t = gi * 2 + dt_
                        stt(R23b(o_t), PSspan(4 * g, 4), pcol(PV_PSCALE + i * 8 + o_t), SGB[dt_], ALU.mult, ALU.mult)
            for o in range(8):
                g = nextg()
                proj([wget()], lambda kt, c: R23b(kt, c * 512, (c + 1) * 512), 8, g)
                residual_add(o, g)
            for j in range(8):
                g = nextg(); proj([wget()], hn, 8, g)
                ubj = R23b(j)
                psn = PSspan(4 * g, 4)
                act(ubj.v(ubj.ap.rearrange("p (k c) -> p c k", k=T)),
                    PSv(psn.ap.rearrange("p (c k) -> p c k", k=T), psn.keys), AF.Copy)
            for j in range(8):
                g = nextg(); proj([wget()], hn, 8, g)
                act(R23b(8 + j), PSspan(4 * g, 4), AF.Silu)
            tset = [dict(P1=TMPv(0, F32, 512), P2=TMPv(2048, F32, 512), BT=TMPv(4096, F32, 512),
                         Q1=TMPv(6144, F32, 512), Q2=TMPv(8192, F32, 512)),
                    dict(P1=R1v(22528, F32, 512), P2=R1v(24576, F32, 512), BT=R1v(26624, F32, 512),
                         Q1=R1v(28672, F32, 512), Q2=R1v(30720, F32, 512))]
            SBW = 258
            SBs = [TMPv(12288, BF16, 4 * 2 * SBW), TMPv(12288 + 4 * 2 * SBW * 2, BF16, 4 * 2 * SBW)]
            memset(SBs[0], 0.0); memset(SBs[1], 0.0)
            XBc = R1v(0, BF16, T * 2 * 128)
            CSc = R1v(4096, BF16, T * 2 * 128)
            KBc = R1v(8192, BF16, T * 128)
            TAB = R1v(10240, F32, 4 * 768)
            pcount = 0

            def c_load_x(ut):
                dma('sp', XBc, xb_s[i, ut], ('c5', 0), rkeys=[('xb_s', i)])
                for pr in range(4):
                    dma('sp', TAB[pr * 768:pr * 768 + 512], tab_s[i, ut][:, pr * 512:(pr + 1) * 512], ('c5', 3, pr),
                        rkeys=[('tab_s', i, 0, ut), ('tab_s', i, 256, ut)])
                    dma('sp', TAB[pr * 768 + 512:(pr + 1) * 768], tab_s[i, ut][:, pr * 512:pr * 512 + 256], ('c5', 4, pr),
                        rkeys=[('tab_s', i, 0, ut)])

            def c_load_y(ut):
                dma('sp', CSc, cs_s[i, ut], ('c5', 1), rkeys=[('cs_s', i)])
                dma('sp', KBc, kb_s[i, ut], ('c5', 2), rkeys=[('kb_s', i)])

            def c_xmm(ut):
                ub = R23b(ut)
                for ri in range(2):
                    for k in range(T):
                        for pr in range(4):
                            bank = 4 + pr
                            tp = (96, 0) if pr == 3 else None
                            o_ = (k * 2 + ri) * 128
                            lhs = XBc.v(XBc.ap[32 * pr:32 * pr + 32, o_:o_ + 128])
                            rhs = ub.v(ub.ap[32 * pr:32 * pr + 32, k * NCH:(k + 1) * NCH])
                            mm(PS(bank, ri * NCH, (ri + 1) * NCH), lhs, rhs, k == 0, k == T - 1, tile_position=tp)

            c_load_x(0); c_load_y(0); c_xmm(0)
            for ut in range(8):
                SB = SBs[ut % 2]
                ub = R23b(ut)
                for pr in range(4):
                    gp = ut * 4 + pr
                    bank = 4 + pr
                    ts_ = tset[pcount % 2]; pcount += 1
                    P1 = ts_['P1']; P2 = ts_['P2']; BT = ts_['BT']; Q1 = ts_['Q1']; Q2 = ts_['Q2']
                    tb = TAB[pr * 768:(pr + 1) * 768]
                    tt(P1, PS(bank), tb[0:512], ALU.mult)
                    tt(P2, PS(bank), tb[256:768], ALU.mult)
                    tt(BT[0:256], P1[0:256], P1[256:512], ALU.add, 'pool')
                    tt(BT[256:512], P2[256:512], P2[0:256], ALU.subtract, 'pool')
                    dcol = dec[i * 32 + gp: i * 32 + gp + 1]
                    dbc = dcol.v(dcol.ap.to_broadcast([128, NCH]))
                    for ri in range(2):
                        S.op('dve', (lambda o, d0, d1: lambda e: e.tensor_tensor_scan(
                            out=o.ap, data0=d0.ap, data1=d1.ap, initial=0.0, op0=ALU.mult, op1=ALU.add))(
                            PS(bank, ri * NCH, (ri + 1) * NCH), dbc, BT[ri * 256:(ri + 1) * 256]),
                            _k(dbc, BT[ri * 256:(ri + 1) * 256]), PS(bank).keys)
                    tt(Q1, PS(bank), tb[0:512], ALU.mult)
                    tt(Q2, PS(bank), tb[256:768], ALU.mult)
                    sre = SB[(pr * 2 + 0) * SBW + 1:(pr * 2 + 0) * SBW + 1 + NCH]
                    sim = SB[(pr * 2 + 1) * SBW + 1:(pr * 2 + 1) * SBW + 1 + NCH]
                    tt(sre, Q1[0:256], Q1[256:512], ALU.subtract, 'pool')
                    tt(sim, Q2[0:256], Q2[256:512], ALU.add, 'pool')
                if ut + 1 < 8:
                    c_load_x(ut + 1)
                    c_xmm(ut + 1)
                for k in range(T):
                    bank = k // 2
                    c0 = (k % 2) * NCH
                    for kp in range(k + 1):
                        lhs = KBc[(k - kp) * 128:(k - kp + 1) * 128]
                        rhs = ub[kp * NCH:(kp + 1) * NCH]
                        mm(PS(bank, c0, c0 + NCH), lhs, rhs, kp == 0, False)
                    for pr in range(4):
                        for ri in range(2):
                            o_ = (k * 2 + ri) * 128 + 32 * pr
                            lhs = CSc[o_:o_ + 32]
                            rhs = SB[(pr * 2 + ri) * SBW:(pr * 2 + ri) * SBW + NCH]
                            tp = (0, 96) if pr == 3 else None
                            mm(PS(bank, c0, c0 + NCH, 32 * pr, 32 * pr + 32), lhs, rhs, False,
                               (ri == 1), tile_position=tp)
                psy = PSspan(0, 4)
                psy_perm = PSv(psy.ap.rearrange("p (k c) -> p c k", k=T), psy.keys)
                outv = ub.v(ub.ap.rearrange("p (c k) -> p c k", k=T))
                act(outv, psy_perm, AF.Gelu_apprx_tanh)
                if ut + 1 < 8:
                    c_load_y(ut + 1)
            SGf = TMPv(0, F32, 2048); TT = TMPv(8192, F32, 2048)
            for j in range(8):
                gg = nextg(); proj([wget()], lambda kt, c: R23b(kt, c * 512, (c + 1) * 512), 8, gg)
                act(SGf, PSspan(4 * gg, 4), AF.Sigmoid, bias=pcol(PV_BGLU + i * 16 + 8 + j))
                gv = nextg(); proj([wget()], lambda kt, c: R23b(kt, c * 512, (c + 1) * 512), 8, gv)
                stt(TT, PSspan(4 * gv, 4), pcol(PV_BGLU + i * 16 + j), SGf, ALU.add, ALU.mult)
                tt(R1b(j), TT, R23b(8 + j), ALU.mult)
            for o in range(8):
                g = nextg()
                proj([wget()], lambda kt, c: R1b(kt, c * 512, (c + 1) * 512), 8, g)
                residual_add(o, g)
                tail_stats(o)

        for n_, l in enumerate(layers):
            if l % 2 == 0:
                even_layer(l, pre=(n_ > 0))
            else:
                odd_layer(l, pre=(n_ > 0))

        if do_final:
            rmsnorm_to(PV_FINAL, lambda ft: HT(ft), pre=(len(layers) > 0))
        ost = [R1v(0, F32, 1024), R1v(4096, F32, 1024)]
        for tt_ in range(16):
            b0 = (tt_ % 4) * 2
            for ft in range(8):
                tr(PS(b0 + ft // 4, (ft % 4) * 128, (ft % 4 + 1) * 128), HT(ft, tt_ * 128, (tt_ + 1) * 128))
            ob = ost[tt_ % 2]
            if tt_ % 2 == 0:
                act(ob, PSspan(b0, 2), AF.Copy)
            else:
                cp(ob, PSspan(b0, 2))
            dma('sp', out_d[tt_ * 128:(tt_ + 1) * 128, :], ob, ('o', tt_ % 2), out_is_dram=True, wkeys=[('out', tt_)])
        S.op('sp', lambda e: None, [('out', t_) for t_ in range(16)], [])
        assert wstate['next'] == len(wq), (wstate['next'], len(wq))
        S.emit(nc, st)
    return nc


def _tile_w(w, ncol_tiles=None):
    K, N = w.shape
    kb = K // 1024
    a = w.reshape(kb, 8, 128, N // 128, 128)
    a = a.transpose(3, 0, 2, 1, 4)
    return np.ascontiguousarray(a).reshape(N // 128, kb, 128, 1024)


def prep_inputs(inp):
    f = lambda a: np.asarray(a, dtype=np.float32)
    sc_w_in = f(inp["sc_w_in"]); sc_w_out = f(inp["sc_w_out"])
    ev_w_in = f(inp["ev_w_in"]); ev_w_out = f(inp["ev_w_out"])
    glu = f(inp["s5_w_glu"]); pw = f(inp["pool_w"])
    w_sc_in = np.zeros((2, 16, 4, 128, 1024), np.float32)
    w_sc_out = np.zeros((2, 8, 2, 128, 1024), np.float32)
    w_ev_in = np.zeros((2, 32, 128, 1024), np.float32)
    w_glu = np.zeros((2, 16, 128, 1024), np.float32)
    w_ev_out = np.zeros((2, 2, 8, 128, 1024), np.float32)
    w_pool = np.zeros((2, 2, 128, 1024), np.float32)
    order = (0, 2, 3, 1)
    for i in range(2):
        t = _tile_w(sc_w_in[i])[:, 0]
        for j in range(16):
            for w_, blk in enumerate(order):
                w_sc_in[i][j, w_] = t[blk * 16 + j]
        w_sc_out[i] = _tile_w(sc_w_out[i])
        w_ev_in[i] = _tile_w(ev_w_in[i])[:, 0]
        w_glu[i] = _tile_w(glu[i])[:, 0]
        t = _tile_w(ev_w_out[i])
        w_ev_out[i] = t.transpose(1, 0, 2, 3)
        a = pw[i].reshape(4, 2, 128, 2, 128)
        a = a.transpose(0, 3, 2, 1, 4).reshape(8, 128, 256)
        a = a.reshape(2, 4, 128, 256).transpose(0, 2, 1, 3).reshape(2, 128, 1024)
        w_pool[i] = a
    pvec = np.zeros((128, PV_N), np.float32)
    colT = lambda v: np.asarray(v, np.float32).reshape(-1, 128).T
    for l in range(4):
        pvec[:, PV_NORM + l * 8: PV_NORM + l * 8 + 8] = colT(inp["norm_g"][l])
    pvec[:, PV_FINAL:PV_FINAL + 8] = colT(inp["final_g"])
    for i in range(2):
        pvec[:, PV_S5D + i * 8: PV_S5D + i * 8 + 8] = colT(inp["s5_d"][i])
        pvec[:, PV_BGLU + i * 16: PV_BGLU + i * 16 + 16] = colT(inp["s5_b_glu"][i])
        pvec[:, PV_PSCALE + i * 8: PV_PSCALE + i * 8 + 8] = colT(inp["pool_scale"][i])
        for k in range(3):
            pvec[:, PV_CONVW + i * 48 + k * 16: PV_CONVW + i * 48 + k * 16 + 16] = colT(inp["sc_conv_w"][i][k])
        pvec[:, PV_CONVB + i * 16: PV_CONVB + i * 16 + 16] = colT(inp["sc_conv_b"][i])
    cst = np.zeros((128, C_N), np.float32)
    cst[:, C_IDENT:C_IDENT + 128] = np.eye(128, dtype=np.float32)
    r = np.arange(128)
    cst[:, C_BD32:C_BD32 + 128] = (r[:, None] // 32 == r[None, :] // 32)
    cst[:, C_MASKE + 0] = ((r // 16) % 2 == 0)
    cst[:, C_MASKE + 1] = ((r // 16) % 2 == 1)
    cst[:, C_MASKN + 0] = -cst[:, C_MASKE + 0]
    cst[:, C_MASKN + 1] = -cst[:, C_MASKE + 1]
    for wi, w_ in enumerate(POOL_W):
        cnt = np.minimum(np.arange(16) + 1, w_)
        cst[:, C_INVC + wi * 16: C_INVC + wi * 16 + 16] = np.float32(1.0) / cnt.astype(np.float32)
    s5q = np.zeros((2, 128, 1120), np.float32)
    s5h = np.zeros((2, 128, 2560), np.float32)
    for i in range(2):
        are = f(inp["s5_a_re"][i]); aim = f(inp["s5_a_im"][i]); ldt = f(inp["s5_log_dt"][i])
        bre = f(inp["s5_b_re"][i]); bim = f(inp["s5_b_im"][i])
        cre = f(inp["s5_c_re"][i]); cim = f(inp["s5_c_im"][i])
        qa = lambda a: a.reshape(32, 2, 64).transpose(1, 2, 0).reshape(128, 32)
        s5q[i, :, 0:32] = qa(are); s5q[i, :, 32:64] = qa(aim)
        s5q[i, :, 64:96] = qa(np.broadcast_to(ldt[:, None], (64, 64)))
        qb = lambda b: b.reshape(32, 2, 64, 16).transpose(1, 2, 0, 3).reshape(128, 512)
        s5q[i, :, 96:608] = qb(bre); s5q[i, :, 608:1120] = qb(bim)
        ha = lambda a: np.broadcast_to(a.reshape(8, 8, 1, 64), (8, 8, 16, 64)).transpose(1, 2, 0, 3).reshape(128, 512)
        s5h[i, :, 0:512] = ha(are); s5h[i, :, 512:1024] = ha(aim)
        s5h[i, :, 1024:1536] = ha(np.broadcast_to(ldt[:, None], (64, 64)))
        hc = lambda c: c.reshape(8, 8, 16, 64).transpose(1, 2, 0, 3).reshape(128, 512)
        s5h[i, :, 1536:2048] = hc(cre); s5h[i, :, 2048:2560] = hc(cim)
    shared = dict(pvec=pvec, cst=cst, s5q=s5q, s5h=s5h)
    for i in range(2):
        shared["w_sc_in%d" % i] = w_sc_in[i]; shared["w_sc_out%d" % i] = w_sc_out[i]
        shared["w_ev_in%d" % i] = w_ev_in[i]; shared["w_glu%d" % i] = w_glu[i]
        shared["w_ev_out%d" % i] = w_ev_out[i]; shared["w_pool%d" % i] = w_pool[i]
    return shared


def needed_keys(layers):
    ks = ["pvec", "cst", "s5q", "s5h"]
    for l in layers:
        i = l // 2
        if l % 2 == 1:
            ks += ["w_sc_in%d" % i, "w_sc_out%d" % i]
        else:
            ks += ["w_ev_in%d" % i, "w_glu%d" % i, "w_ev_out%d" % i, "w_pool%d" % i]
    return ks


_PROG = {}


def kernel(**inputs):
    x = np.asarray(inputs["x"], dtype=np.float32)
    shared = prep_inputs(inputs)
    key = "full"
    if key not in _PROG:
        _PROG[key] = build_program()
    nc = _PROG[key]
    in_maps = []
    for b in range(N_CORES):
        d = dict(shared)
        d["x"] = np.ascontiguousarray(x[b])
        in_maps.append(d)
    res = run_bass_kernel_spmd(nc, in_maps, core_ids=list(range(N_CORES)))
    out = np.stack([np.asarray(r["out"], dtype=np.float32) for r in res.results], axis=0)
    return out
```
